# Optimizing a Trainium2 kernel written in Bass

```python
import jax, jax.numpy as jnp
from jax import lax
import numpy as np

D_MODEL = 1024
BATCH = 4
SEQ = 8192
DEPTH = 2

CHUNK = 64
N_MEM = 256
A_GROUPS = 4
A_GROUP_DIM = 128
A_WIDTH = A_GROUPS * A_GROUP_DIM
A_BLOCK = 128
FOX_HEADS = 8
FOX_HEAD_DIM = 64
FOX_WIDTH = FOX_HEADS * FOX_HEAD_DIM
Q_BLOCK = 128
POOL_WINDOWS = (2, 4, 8, 16)
POOL_GROUP_DIM = 128
POOL_WIDTH = len(POOL_WINDOWS) * POOL_GROUP_DIM
CONV_WIDTH = 512
CONV_K = 3
MIX_WIDTH = 1024
EVEN_IN = 2 * A_WIDTH + 3 * FOX_WIDTH + FOX_HEADS
ODD_IN = POOL_WIDTH + 3 * CONV_WIDTH
XA_HEADS = 4
XA_HEAD_DIM = 128
XA_WIDTH = XA_HEADS * XA_HEAD_DIM
D_FF = -(-(8 * D_MODEL) // (3 * 256)) * 256
EPS = 1e-6
N_EVEN = (DEPTH + 1) // 2
N_ODD = DEPTH // 2

kernel_name = "hybrid_streaming_gmlp_fox_pool_conv"


def rms_norm(x, g):
    xf = x.astype(jnp.float32)
    y = xf * lax.rsqrt(jnp.mean(xf * xf, axis=-1, keepdims=True) + EPS)
    return (y * g.astype(jnp.float32)).astype(x.dtype)


def chunk_gmlp(u, v, g_v, w_s, b_s):
    B, S, _ = v.shape
    v = rms_norm(v.reshape(B, S, A_GROUPS, A_GROUP_DIM), g_v.reshape(A_GROUPS, A_GROUP_DIM))
    vb = v.reshape(B, S // A_BLOCK, A_BLOCK, A_GROUPS, A_GROUP_DIM)
    cpos = np.arange(A_BLOCK) // CHUNK
    mask = jnp.asarray(cpos[:, None] >= cpos[None, :]).astype(w_s.dtype)
    w = w_s * mask[None]
    s = jnp.einsum('gts,bnsgc->bntgc', w, vb) + b_s.T[None, None, :, :, None]
    return u * s.reshape(B, S, A_WIDTH)


def forgetting_attention(q, k, v, log_f):
    B, S, H, hd = q.shape
    nb = S // Q_BLOCK
    scale = 1.0 / np.sqrt(hd)
    c = jnp.cumsum(log_f.astype(jnp.float32), axis=1).transpose(0, 2, 1)
    qb = q.reshape(B, nb, Q_BLOCK, H, hd).transpose(1, 0, 2, 3, 4)
    cb = c.reshape(B, H, nb, Q_BLOCK).transpose(2, 0, 1, 3)
    kpos = jnp.arange(S)

    def block(args):
        qi, ci, i = args
        logits = jnp.einsum('bqhd,bkhd->bhqk', qi, k).astype(jnp.float32) * scale
        logits = logits + ci[..., :, None] - c[:, :, None, :]
        qpos = i * Q_BLOCK + jnp.arange(Q_BLOCK)
        allowed = kpos[None, :] <= qpos[:, None]
        logits = jnp.where(allowed[None, None], logits, -jnp.inf)
        p = jax.nn.softmax(logits, axis=-1)
        return jnp.einsum('bhqk,bkhd->bqhd', p.astype(v.dtype), v)

    out = lax.map(block, (qb, cb, jnp.arange(nb)))
    return out.transpose(1, 0, 2, 3, 4).reshape(B, S, H * hd)


def multiscale_pool(z, w_pool, s_pool):
    B, S, _ = z.shape
    zf = z.astype(jnp.float32)
    cs = jnp.cumsum(zf, axis=1)
    pos = jnp.arange(S)
    outs = []
    for g, w in enumerate(POOL_WINDOWS):
        sl = slice(g * POOL_GROUP_DIM, (g + 1) * POOL_GROUP_DIM)
        csg = cs[..., sl]
        lag = jnp.pad(csg, ((0, 0), (w, 0), (0, 0)))[:, :S]
        cnt = jnp.minimum(pos + 1, w).astype(jnp.float32)[None, :, None]
        outs.append((csg - lag) / cnt - zf[..., sl])
    p = jnp.concatenate(outs, axis=-1).astype(z.dtype)
    p = p.reshape(B, S, len(POOL_WINDOWS), POOL_GROUP_DIM)
    y = jnp.einsum('bsgc,gcd->bsgd', p, w_pool).reshape(B, S, POOL_WIDTH)
    return y * s_pool


def short_gated_conv(h, gate_b, gate_c, conv_w):
    S = h.shape[1]
    xg = gate_c * h
    xp = jnp.pad(xg, ((0, 0), (CONV_K - 1, 0), (0, 0)))
    conv = sum(conv_w[j] * xp[:, j:j + S] for j in range(CONV_K))
    return gate_b * conv


def even_mixer(h, w_in, b_f, g_v, w_s, b_s, g_qn, g_kn, w_out):
    B, S, _ = h.shape
    z = h @ w_in
    uv = jax.nn.gelu(z[..., :2 * A_WIDTH])
    u, v = uv[..., :A_WIDTH], uv[..., A_WIDTH:]
    o = 2 * A_WIDTH
    q = z[..., o:o + FOX_WIDTH].reshape(B, S, FOX_HEADS, FOX_HEAD_DIM)
    k = z[..., o + FOX_WIDTH:o + 2 * FOX_WIDTH].reshape(B, S, FOX_HEADS, FOX_HEAD_DIM)
    vv = z[..., o + 2 * FOX_WIDTH:o + 3 * FOX_WIDTH].reshape(B, S, FOX_HEADS, FOX_HEAD_DIM)
    f_logit = z[..., o + 3 * FOX_WIDTH:].astype(jnp.float32) + b_f.astype(jnp.float32)
    log_f = jax.nn.log_sigmoid(f_logit)
    q = rms_norm(q, g_qn)
    k = rms_norm(k, g_kn)
    y_a = chunk_gmlp(u, v, g_v, w_s, b_s)
    y_b = forgetting_attention(q, k, vv, log_f).astype(h.dtype)
    return jnp.concatenate([y_a, y_b], axis=-1) @ w_out


def odd_mixer(h, w_in, w_pool, s_pool, conv_w, w_out):
    z = h @ w_in
    zc = z[..., :POOL_WIDTH]
    o = POOL_WIDTH
    hd = z[..., o:o + CONV_WIDTH]
    gb = z[..., o + CONV_WIDTH:o + 2 * CONV_WIDTH]
    gc = z[..., o + 2 * CONV_WIDTH:]
    y_c = multiscale_pool(zc, w_pool, s_pool)
    y_d = short_gated_conv(hd, gb, gc, conv_w)
    return jnp.concatenate([y_c, y_d], axis=-1) @ w_out


def memory_cross_attention(h, m, w_q, w_kv, w_o, g_qn, g_kn):
    B, S, _ = h.shape
    M = m.shape[1]
    q = rms_norm((h @ w_q).reshape(B, S, XA_HEADS, XA_HEAD_DIM), g_qn)
    kv = m @ w_kv
    k = rms_norm(kv[..., :XA_WIDTH].reshape(B, M, XA_HEADS, XA_HEAD_DIM), g_kn)
    v = kv[..., XA_WIDTH:].reshape(B, M, XA_HEADS, XA_HEAD_DIM)
    logits = jnp.einsum('bshd,bmhd->bhsm', q, k).astype(jnp.float32) / np.sqrt(XA_HEAD_DIM)
    p = jax.nn.softmax(logits, axis=-1).astype(v.dtype)
    o = jnp.einsum('bhsm,bmhd->bshd', p, v).reshape(B, S, XA_WIDTH)
    return o @ w_o


def swiglu(h, w_gate, w_up, w_down):
    return (jax.nn.silu(h @ w_gate) * (h @ w_up)) @ w_down


def setup_inputs(seed: int = 0) -> dict:
    key = jax.random.key(seed)
    ks = jax.random.split(key, 32)
    f32 = jnp.float32

    def nrm(k, shape, scale):
        return jax.random.normal(k, shape, f32) * scale

    def gain(k, shape):
        return 1.0 + 0.05 * jax.random.normal(k, shape, f32)

    L, NE, NO, D = DEPTH, N_EVEN, N_ODD, D_MODEL
    return {
        "x": nrm(ks[0], (BATCH, SEQ, D), 1.0),
        "mem": nrm(ks[1], (BATCH, N_MEM, D), 1.0),
        "g_mix": gain(ks[2], (L, D)),
        "g_xa": gain(ks[3], (L, D)),
        "g_mem": gain(ks[4], (L, D)),
        "xa_wq": nrm(ks[5], (L, D, XA_WIDTH), D ** -0.5),
        "xa_wkv": nrm(ks[6], (L, D, 2 * XA_WIDTH), D ** -0.5),
        "xa_wo": nrm(ks[7], (L, XA_WIDTH, D), XA_WIDTH ** -0.5),
        "xa_gq": gain(ks[8], (L, XA_HEAD_DIM)),
        "xa_gk": gain(ks[9], (L, XA_HEAD_DIM)),
        "g_ffn": gain(ks[10], (L, D)),
        "w_gate": nrm(ks[11], (L, D, D_FF), D ** -0.5),
        "w_up": nrm(ks[12], (L, D, D_FF), D ** -0.5),
        "w_down": nrm(ks[13], (L, D_FF, D), D_FF ** -0.5),
        "e_w_in": nrm(ks[14], (NE, D, EVEN_IN), D ** -0.5),
        "e_b_f": jnp.linspace(1.0, 6.0, FOX_HEADS, dtype=f32)[None] + 0.1 * jax.random.normal(ks[15], (NE, FOX_HEADS), f32),
        "e_g_v": gain(ks[16], (NE, A_WIDTH)),
        "e_w_s": nrm(ks[17], (NE, A_GROUPS, A_BLOCK, A_BLOCK), A_BLOCK ** -0.5),
        "e_b_s": gain(ks[18], (NE, A_GROUPS, A_BLOCK)),
        "e_g_qn": gain(ks[19], (NE, FOX_HEAD_DIM)),
        "e_g_kn": gain(ks[20], (NE, FOX_HEAD_DIM)),
        "e_w_out": nrm(ks[21], (NE, MIX_WIDTH, D), MIX_WIDTH ** -0.5),
        "o_w_in": nrm(ks[22], (NO, D, ODD_IN), D ** -0.5),
        "o_w_pool": nrm(ks[23], (NO, len(POOL_WINDOWS), POOL_GROUP_DIM, POOL_GROUP_DIM), POOL_GROUP_DIM ** -0.5),
        "o_s_pool": gain(ks[24], (NO, POOL_WIDTH)),
        "o_conv_w": nrm(ks[25], (NO, CONV_K, CONV_WIDTH), CONV_K ** -0.5),
        "o_w_out": nrm(ks[26], (NO, MIX_WIDTH, D), MIX_WIDTH ** -0.5),
    }


def reference(x, mem, g_mix, g_xa, g_mem, xa_wq, xa_wkv, xa_wo, xa_gq, xa_gk,
              g_ffn, w_gate, w_up, w_down,
              e_w_in, e_b_f, e_g_v, e_w_s, e_b_s, e_g_qn, e_g_kn, e_w_out,
              o_w_in, o_w_pool, o_s_pool, o_conv_w, o_w_out):
    for layer in range(DEPTH):
        i = layer // 2
        h = rms_norm(x, g_mix[layer])
        if layer % 2 == 0:
            y = even_mixer(h, e_w_in[i], e_b_f[i], e_g_v[i], e_w_s[i], e_b_s[i],
                           e_g_qn[i], e_g_kn[i], e_w_out[i])
        else:
            y = odd_mixer(h, o_w_in[i], o_w_pool[i], o_s_pool[i], o_conv_w[i], o_w_out[i])
        x = x + y
        m = rms_norm(mem, g_mem[layer])
        x = x + memory_cross_attention(rms_norm(x, g_xa[layer]), m, xa_wq[layer], xa_wkv[layer],
                                       xa_wo[layer], xa_gq[layer], xa_gk[layer])
        x = x + swiglu(rms_norm(x, g_ffn[layer]), w_gate[layer], w_up[layer], w_down[layer])
    return x
```

```python
import numpy as np
from contextlib import ExitStack
import concourse.bass as bass
import concourse.mybir as mybir
from concourse.bass_utils import run_bass_kernel_spmd

F32 = mybir.dt.float32
BF16 = mybir.dt.bfloat16
AF = mybir.ActivationFunctionType
ALU = mybir.AluOpType
AX = mybir.AxisListType

D = 1024
DFF = 2816
NCH = 8
EPS = 1e-6
SEQ = 8192
HALF = 4096
HALO = 128
OWN0 = HALF - HALO
NOWN = HALF + HALO
NEG = -30000.0


class Tk:
    __slots__ = ("w", "r")

    def __init__(self):
        self.w = {}
        self.r = {}


def _merge(d, s):
    for k, v in s.items():
        if d.get(k, 0) < v:
            d[k] = v


class Prog:
    ENG = ("pe", "act", "dve", "pool", "sp")
    NDS = 12

    def __init__(self, nc, es):
        self.nc = nc
        self.sem = {}
        self.cnt = {}
        self.known = {e: {} for e in self.ENG}
        self.streams = {e: [] for e in self.ENG}
        for e in self.ENG:
            self.sem[e] = es.enter_context(nc.semaphore("s_" + e))
            self.cnt[e] = 0
        self.drr = {}
        for q in ("sp", "pool", "act"):
            self.drr[q] = 0
            for k in range(self.NDS):
                key = (q, k)
                self.sem[key] = es.enter_context(nc.semaphore("d_%s%d" % (q, k)))
                self.cnt[key] = 0

    def _waits(self, eng, deps):
        st = self.streams[eng]
        kn = self.known[eng]
        for key, val in deps.items():
            if val <= 0:
                continue
            if eng == "pe" and key == "pe":
                continue
            if kn.get(key, 0) >= val:
                continue
            kn[key] = val
            st.append(("w", key, val))

    def op(self, eng, fn, rd=(), wr=()):
        deps = {}
        for t in rd:
            _merge(deps, t.w)
        for t in wr:
            _merge(deps, t.w)
            _merge(deps, t.r)
        self._waits(eng, deps)
        self.cnt[eng] += 1
        v = self.cnt[eng]
        self.streams[eng].append(("o", fn, eng))
        for t in rd:
            if t.r.get(eng, 0) < v:
                t.r[eng] = v
        for t in wr:
            t.w = {eng: v}
            t.r = {}

    def dma(self, q, out, in_, rd=(), wr=()):
        k = self.drr[q] % self.NDS
        self.drr[q] += 1
        key = (q, k)
        deps = {key: self.cnt[key]}
        for t in rd:
            _merge(deps, t.w)
        for t in wr:
            _merge(deps, t.w)
            _merge(deps, t.r)
        self._waits(q, deps)
        self.cnt[key] += 16
        v = self.cnt[key]
        self.streams[q].append(("d", out, in_, key))
        for t in rd:
            if t.r.get(key, 0) < v:
                t.r[key] = v
        for t in wr:
            t.w = {key: v}
            t.r = {}

    def barrier(self):
        allc = {k: v for k, v in self.cnt.items() if v > 0}
        for e in self.ENG:
            self._waits(e, dict(allc))

    def flush(self):
        nc = self.nc
        streams = self.streams
        self.streams = {e: [] for e in self.ENG}
        sem = self.sem

        def run(e, items):
            for it in items:
                if it[0] == "w":
                    e.wait_ge(sem[it[1]], it[2])
                elif it[0] == "o":
                    it[1](e).then_inc(sem[it[2]], 1)
                else:
                    e.dma_start(out=it[1], in_=it[2]).then_inc(sem[it[3]], 16)

        with nc.allow_non_contiguous_dma(reason="tiny strided parameter/stat DMAs"), nc.Block() as blk:
            @blk.tensor
            def _(e):
                run(e, streams["pe"])

            @blk.scalar
            def _(e):
                run(e, streams["act"])

            @blk.vector
            def _(e):
                run(e, streams["dve"])

            @blk.gpsimd
            def _(e):
                run(e, streams["pool"])

            @blk.sync
            def _(e):
                run(e, streams["sp"])


class Ctx:
    _uid = [0]

    def __init__(self, nc, P):
        self.nc = nc
        self.P = P
        self.es = ExitStack()
        Ctx._uid[0] += 1
        self.n = Ctx._uid[0] * 1000

    def sb(self, shape, dt, name=None):
        self.n += 1
        return self.es.enter_context(self.nc.sbuf_tensor("%s_%d" % (name or "t", self.n), list(shape), dt))

    def psum_banks(self):
        banks = []
        for i in range(8):
            self.n += 1
            t = self.es.enter_context(self.nc.psum_tensor("ps_%d" % self.n, [128, 512], F32))
            banks.append((t, Tk()))
        return banks

    def close(self):
        self.es.close()


def load_weight_bf16(P, dst, dst_tk, w_ap, kchunks, ncols, col0=0, split=2):
    per = (kchunks + split - 1) // split
    for s in range(0, kchunks, per):
        e = min(kchunks, s + per)
        src = w_ap[s * 128:e * 128, col0:col0 + ncols].rearrange("(k p) n -> p k n", p=128)
        P.dma("pool", dst[:, s:e, :], src, wr=[dst_tk])


def emit_norm_T(P, C, xt, xt_tk, nsub, gb, hn, hn_tk, ss, ss_tk, rstd, rstd_tk, junk, junk_tk,
                hnT, hnT_tks, tp_banks, ident, evac_engs=("act", "dve")):
    for s in range(nsub):
        P.op("act", lambda e, s=s: e.activation(out=junk[:], in_=xt[:, s, :], func=AF.Square,
                                                accum_out=ss[:, s:s + 1]),
             rd=[xt_tk], wr=[junk_tk, ss_tk])
    P.op("dve", lambda e: e.tensor_scalar(out=rstd[:, 0:nsub], in0=ss[:, 0:nsub], scalar1=1.0 / D, scalar2=EPS,
                                          op0=ALU.mult, op1=ALU.add), rd=[ss_tk], wr=[rstd_tk])
    P.op("act", lambda e: e.activation(out=rstd[:, 0:nsub], in_=rstd[:, 0:nsub], func=AF.Sqrt),
         rd=[rstd_tk], wr=[rstd_tk])
    P.op("dve", lambda e: e.reciprocal(out=rstd[:, 0:nsub], in_=rstd[:, 0:nsub]),
         rd=[rstd_tk], wr=[rstd_tk])
    for s in range(nsub):
        P.op("dve", lambda e, s=s: e.scalar_tensor_tensor(out=hn[:, s % 2, :], in0=xt[:, s, :], scalar=rstd[:, s:s + 1],
                                                          in1=gb[:], op0=ALU.mult, op1=ALU.mult),
             rd=[xt_tk, rstd_tk], wr=[hn_tk[s % 2]])
        bank, btk = tp_banks[s % len(tp_banks)]
        pb = bank[:].bitcast(BF16)
        for c in range(NCH):
            P.op("pe", lambda e, s=s, c=c, pb=pb: e.transpose(out=pb[:, c * 128:(c + 1) * 128],
                                                             in_=hn[:, s % 2, c * 128:(c + 1) * 128], identity=ident[:]),
                 rd=[hn_tk[s % 2]], wr=[btk])
        eng = evac_engs[s % len(evac_engs)]
        dst = hnT[:, :, s * 128:(s + 1) * 128]
        srcv = pb.rearrange("p (c t) -> p c t", c=NCH)
        if eng == "act":
            P.op("act", lambda e, dst=dst, srcv=srcv: e.activation(out=dst, in_=srcv, func=AF.Copy),
                 rd=[btk], wr=[hnT_tks[s]])
        else:
            P.op(eng, lambda e, dst=dst, srcv=srcv: e.tensor_copy(out=dst, in_=srcv), rd=[btk], wr=[hnT_tks[s]])


def make_ident(P, C, dt=BF16):
    raise NotImplementedError


def phase_ffn(nc, P, xin, xout, tiles_in, tiles_out, g_ap, wg_ap, wu_ap, wd_ap, ident_ap):
    C = Ctx(nc, P)
    NM = DFF // 128
    wg = C.sb([128, NCH, DFF], BF16, "wg"); wg_tk = Tk()
    wu = C.sb([128, NCH, DFF], BF16, "wu"); wu_tk = Tk()
    wd = C.sb([128, NM, D], BF16, "wd"); wd_tk = Tk()
    gb = C.sb([128, D], F32, "gb"); gb_tk = Tk()
    ident = C.sb([128, 128], BF16, "ident"); ident_tk = Tk()
    xts = [(C.sb([128, 4, D], F32, "xt"), Tk()) for _ in range(2)]
    hn = C.sb([128, 2, D], BF16, "hn"); hn_tk = [Tk() for _ in range(2)]
    hnT = C.sb([128, NCH, 512], BF16, "hnT"); hnT_tks = [Tk() for _ in range(4)]
    hT = C.sb([128, NM, 512], BF16, "hT"); hT_tks = [Tk() for _ in range(NM)]
    junk = C.sb([128, D], BF16, "junk"); junk_tk = Tk()
    sgs = [(C.sb([128, 512], BF16, "sg"), Tk()) for _ in range(2)]
    ss = C.sb([128, 8], F32, "ss"); ss_tk = Tk()
    rstd = C.sb([128, 8], F32, "rstd"); rstd_tk = Tk()
    banks = C.psum_banks()

    P.dma("pool", ident[:], ident_ap, wr=[ident_tk])
    P.dma("sp", gb[:], g_ap.partition_broadcast(128), wr=[gb_tk])
    load_weight_bf16(P, wg, wg_tk, wg_ap, NCH, DFF, split=4)
    load_weight_bf16(P, wu, wu_tk, wu_ap, NCH, DFF, split=4)
    load_weight_bf16(P, wd, wd_tk, wd_ap, NM, D, split=4)

    nt = len(tiles_in)

    def load_x(i):
        r0, ntok = tiles_in[i]
        xt, xt_tk = xts[i % 2]
        ns = ntok // 128
        P.dma("sp", xt[:, 0:ns, :], xin[r0:r0 + ntok, :].rearrange("(s p) d -> p s d", p=128), wr=[xt_tk])

    def norm(i):
        r0, ntok = tiles_in[i]
        xt, xt_tk = xts[i % 2]
        ns = ntok // 128
        emit_norm_T(P, C, xt, xt_tk, ns, gb, hn, hn_tk, ss, ss_tk, rstd, rstd_tk, junk, junk_tk,
                    hnT, hnT_tks, banks[4:6], ident)

    def gate_up(i):
        r0, ntok = tiles_in[i]
        ns = ntok // 128
        for mo in range(NM):
            bg, bg_tk = banks[(2 * mo) % 4]
            bu, bu_tk = banks[(2 * mo + 1) % 4]
            for kc in range(NCH):
                P.op("pe", lambda e, kc=kc, mo=mo, bg=bg: e.matmul(bg[:, 0:ntok], lhsT=wg[:, kc, mo * 128:(mo + 1) * 128],
                                                                  rhs=hnT[:, kc, 0:ntok], start=(kc == 0), stop=(kc == NCH - 1)),
                     rd=[wg_tk] + hnT_tks[0:ns], wr=[bg_tk])
            for kc in range(NCH):
                P.op("pe", lambda e, kc=kc, mo=mo, bu=bu: e.matmul(bu[:, 0:ntok], lhsT=wu[:, kc, mo * 128:(mo + 1) * 128],
                                                                  rhs=hnT[:, kc, 0:ntok], start=(kc == 0), stop=(kc == NCH - 1)),
                     rd=[wu_tk] + hnT_tks[0:ns], wr=[bu_tk])
            sg, sg_tk = sgs[mo % 2]
            P.op("act", lambda e, sg=sg, bg=bg: e.activation(out=sg[:, 0:ntok], in_=bg[:, 0:ntok], func=AF.Silu),
                 rd=[bg_tk], wr=[sg_tk])
            P.op("dve", lambda e, sg=sg, bu=bu, mo=mo: e.tensor_tensor(out=hT[:, mo, 0:ntok], in0=sg[:, 0:ntok],
                                                                       in1=bu[:, 0:ntok], op=ALU.mult),
                 rd=[sg_tk, bu_tk], wr=[hT_tks[mo]])

    def down(i):
        r0, ntok = tiles_in[i]
        ro = tiles_out[i]
        xt, xt_tk = xts[i % 2]
        ns = ntok // 128
        j = 0
        for s in range(ns):
            for nh in range(2):
                bo, bo_tk = banks[6 + (j % 2)]
                j += 1
                for mo in range(NM):
                    P.op("pe", lambda e, s=s, nh=nh, mo=mo, bo=bo: e.matmul(bo[:, :], lhsT=hT[:, mo, s * 128:(s + 1) * 128],
                                                                           rhs=wd[:, mo, nh * 512:(nh + 1) * 512],
                                                                           start=(mo == 0), stop=(mo == NM - 1)),
                         rd=[wd_tk, hT_tks[mo]], wr=[bo_tk])
                P.op("dve", lambda e, s=s, nh=nh, bo=bo, xt=xt: e.tensor_tensor(out=xt[:, s, nh * 512:(nh + 1) * 512],
                                                                               in0=xt[:, s, nh * 512:(nh + 1) * 512],
                                                                               in1=bo[:, :], op=ALU.add),
                     rd=[bo_tk], wr=[xt_tk])
        if ro is not None:
            P.dma("sp", xout[ro:ro + ntok, :].rearrange("(s p) d -> p s d", p=128), xt[:, 0:ns, :], rd=[xt_tk])

    load_x(0)
    if nt > 1:
        load_x(1)
    norm_dep(P, [gb_tk, ident_tk])
    norm(0)
    for i in range(nt):
        gate_up(i)
        if i + 1 < nt:
            norm(i + 1)
        down(i)
        if i + 2 < nt:
            load_x(i + 2)
    P.barrier()
    P.flush()
    C.close()


def norm_dep(P, tks):
    deps = {}
    for t in tks:
        _merge(deps, t.w)
    for e in ("pe", "dve", "act", "pool"):
        P._waits(e, dict(deps))


class Kit:
    def __init__(self, nc, P, C, g_ap, ident_ap, nx=2, nsubmax=4):
        self.P = P
        self.C = C
        self.gb = C.sb([128, D], F32, "gb"); self.gb_tk = Tk()
        self.ident = C.sb([128, 128], BF16, "ident"); self.ident_tk = Tk()
        self.xts = [(C.sb([128, nsubmax, D], F32, "xt"), Tk()) for _ in range(nx)]
        self.hn = C.sb([128, 2, D], BF16, "hn"); self.hn_tk = [Tk(), Tk()]
        self.hnT = C.sb([128, NCH, 128 * nsubmax], BF16, "hnT"); self.hnT_tks = [Tk() for _ in range(nsubmax)]
        self.junk = C.sb([128, D], BF16, "junk"); self.junk_tk = Tk()
        self.ss = C.sb([128, 8], F32, "ss"); self.ss_tk = Tk()
        self.rstd = C.sb([128, 8], F32, "rstd"); self.rstd_tk = Tk()
        self.banks = C.psum_banks()
        P.dma("pool", self.ident[:], ident_ap, wr=[self.ident_tk])
        if g_ap is not None:
            P.dma("sp", self.gb[:], g_ap.partition_broadcast(128), wr=[self.gb_tk])

    def consts_ready(self, extra=()):
        norm_dep(self.P, [self.gb_tk, self.ident_tk] + list(extra))

    def load_x(self, slot, src_rows_ap, ns):
        xt, xt_tk = self.xts[slot]
        self.P.dma("sp", xt[:, 0:ns, :], src_rows_ap.rearrange("(s p) d -> p s d", p=128), wr=[xt_tk])

    def norm(self, slot, ns, tp=(0, 1)):
        xt, xt_tk = self.xts[slot]
        emit_norm_T(self.P, self.C, xt, xt_tk, ns, self.gb, self.hn, self.hn_tk, self.ss, self.ss_tk,
                    self.rstd, self.rstd_tk, self.junk, self.junk_tk, self.hnT, self.hnT_tks,
                    [self.banks[i] for i in tp], self.ident)


def fm_chunk(P, bank, btk, W, wtk, col0, M, hnT, hnT_tks, ns, nk=NCH):
    ntok = ns * 128
    for kc in range(nk):
        P.op("pe", lambda e, kc=kc: e.matmul(bank[0:M, 0:ntok], lhsT=W[:, kc, col0:col0 + M], rhs=hnT[:, kc, 0:ntok],
                                             start=(kc == 0), stop=(kc == nk - 1)),
             rd=[wtk] + list(hnT_tks[0:ns]), wr=[btk])


def tm_block(P, bank, btk, hnT, hnT_tk_s, s, W, wtk, col0, ncols, nk=NCH):
    for kc in range(nk):
        P.op("pe", lambda e, kc=kc: e.matmul(bank[:, 0:ncols], lhsT=hnT[:, kc, s * 128:(s + 1) * 128],
                                             rhs=W[:, kc, col0:col0 + ncols], start=(kc == 0), stop=(kc == nk - 1)),
             rd=[wtk, hnT_tk_s], wr=[btk])


def rsqrt_small(P, t, tk, n, scale):
    P.op("dve", lambda e: e.tensor_scalar(out=t[:, 0:n], in0=t[:, 0:n], scalar1=scale, scalar2=EPS,
                                          op0=ALU.mult, op1=ALU.add), rd=[tk], wr=[tk])
    P.op("act", lambda e: e.activation(out=t[:, 0:n], in_=t[:, 0:n], func=AF.Sqrt), rd=[tk], wr=[tk])
    P.op("dve", lambda e: e.reciprocal(out=t[:, 0:n], in_=t[:, 0:n]), rd=[tk], wr=[tk])


def headnorm_fm(P, K, src_bank, src_tk, ntok, BD, gcol, sq, sq_tk, ssb, ssb_tk, rs, rs_tk, out_ap, out_tk, inv_n):
    P.op("act", lambda e: e.activation(out=sq[:, 0:ntok], in_=src_bank[:, 0:ntok], func=AF.Square),
         rd=[src_tk], wr=[sq_tk])
    P.op("pe", lambda e: e.matmul(ssb[:, 0:ntok], lhsT=BD[:], rhs=sq[:, 0:ntok], start=True, stop=True),
         rd=[sq_tk], wr=[ssb_tk])
    P.op("dve", lambda e: e.tensor_scalar(out=rs[:, 0:ntok], in0=ssb[:, 0:ntok], scalar1=inv_n, scalar2=EPS,
                                          op0=ALU.mult, op1=ALU.add), rd=[ssb_tk], wr=[rs_tk])
    P.op("act", lambda e: e.activation(out=rs[:, 0:ntok], in_=rs[:, 0:ntok], func=AF.Sqrt), rd=[rs_tk], wr=[rs_tk])
    P.op("dve", lambda e: e.reciprocal(out=rs[:, 0:ntok], in_=rs[:, 0:ntok]), rd=[rs_tk], wr=[rs_tk])
    P.op("dve", lambda e: e.scalar_tensor_tensor(out=out_ap, in0=src_bank[:, 0:ntok], scalar=gcol, in1=rs[:, 0:ntok],
                                                 op0=ALU.mult, op1=ALU.mult), rd=[src_tk, rs_tk], wr=[out_tk])


def phase_a(nc, P, S, prm):
    xc = S["xc"]
    C = Ctx(nc, P)
    K = Kit(nc, P, C, prm["g_mix"][0], S["ident"])
    NW = 2568
    W = C.sb([128, NCH, NW], BF16, "win"); W_tk = Tk()
    load_weight_bf16(P, W, W_tk, prm["e_w_in"][0], NCH, NW, split=4)
    BD = C.sb([128, 128], BF16, "bd"); c_tk = Tk()
    P.dma("pool", BD[:], S["bd64"], wr=[c_tk])
    id8 = C.sb([8, 8], F32, "id8")
    P.dma("sp", id8[:], S["identf"][0:8, 0:8], wr=[c_tk])
    gq = C.sb([128, 1], F32, "gq"); gk = C.sb([128, 1], F32, "gk")
    for hh in range(2):
        P.dma("sp", gq[hh * 64:(hh + 1) * 64, :], prm["e_g_qn"][0].rearrange("(p o) -> p o", o=1), wr=[c_tk])
        P.dma("sp", gk[hh * 64:(hh + 1) * 64, :], prm["e_g_kn"][0].rearrange("(p o) -> p o", o=1), wr=[c_tk])
    nbf = C.sb([8, 1], F32, "nbf")
    P.dma("sp", nbf[:], prm["e_b_f"][0].rearrange("(p o) -> p o", o=1), wr=[c_tk])
    gvb = C.sb([128, 512], F32, "gvb")
    P.dma("sp", gvb[:], prm["e_g_v"][0].partition_broadcast(128), wr=[c_tk])
    bsb = C.sb([128, 512], F32, "bsb")
    P.dma("sp", bsb[:], prm["e_b_s"][0].rearrange("g t -> (g t)").partition_broadcast(128), wr=[c_tk])
    wsf = C.sb([128, 4, 128], F32, "wsf")
    P.dma("sp", wsf[:], prm["e_w_s"][0].rearrange("g t s -> t g s"), wr=[c_tk])
    wsb = C.sb([128, 4, 128], BF16, "wsb"); wsT = C.sb([128, 4, 128], BF16, "wsT")
    ones8 = C.sb([8, 512], F32, "ones8")
    carry = C.sb([8, 1], F32, "carry"); carry_tk = Tk()
    K.consts_ready([c_tk])
    P.op("dve", lambda e: e.memset(ones8[:], 1.0), wr=[c_tk])
    P.op("dve", lambda e: e.memset(carry[:], 0.0), wr=[carry_tk])
    P.op("dve", lambda e: e.tensor_scalar(out=gq[:], in0=gq[:], scalar1=0.125, scalar2=None, op0=ALU.mult), wr=[c_tk])
    P.op("dve", lambda e: e.tensor_scalar(out=nbf[:], in0=nbf[:], scalar1=-1.0, scalar2=None, op0=ALU.mult), wr=[c_tk])
    P.op("dve", lambda e: e.memset(wsf[0:64, :, 64:128], 0.0), wr=[c_tk])
    P.op("dve", lambda e: e.tensor_copy(out=wsb[:], in_=wsf[:]), wr=[c_tk])
    b0, b0tk = K.banks[0]
    pb = b0[:].bitcast(BF16)
    for g in range(4):
        P.op("pe", lambda e, g=g: e.transpose(out=pb[:, g * 128:(g + 1) * 128], in_=wsb[:, g, :], identity=K.ident[:]),
             rd=[c_tk], wr=[b0tk])
    P.op("dve", lambda e: e.tensor_copy(out=wsT[:].rearrange("p g t -> p (g t)"), in_=pb[:, 0:512]), rd=[b0tk], wr=[c_tk])
    norm_dep(P, [c_tk])

    sq = C.sb([128, 512], BF16, "sq"); sq_tk = Tk()
    rs = C.sb([128, 512], F32, "rs"); rs_tk = Tk()
    kn = [(C.sb([128, 512], BF16, "kn"), Tk()) for _ in range(2)]
    vtm = [(C.sb([128, 512], BF16, "vtm"), Tk()) for _ in range(2)]
    fe = C.sb([8, 512], F32, "fe"); fe_tk = Tk()
    negc = C.sb([8, 512], F32, "negc"); negc_tk = Tk()
    nk = C.sb([128, 4, 8], F32, "nk"); nk_tk = Tk()
    uT = C.sb([128, 4, 512], BF16, "uT"); uT_tk = [Tk() for _ in range(4)]
    vg = C.sb([128, 512], F32, "vg"); vg_tk = Tk()
    vn = C.sb([128, 512], BF16, "vn"); vn_tk = Tk()
    ssg = C.sb([128, 4], F32, "ssg"); ssg_tk = Tk()
    t1 = C.sb([128, 512], F32, "t1"); t1_tk = Tk()
    yaT = C.sb([128, 4, 512], BF16, "yaT"); ya_tk = Tk()
    banks = K.banks
    rr = [0]

    def fmbank():
        rr[0] += 1
        return banks[2 + rr[0] % 2]

    tiles = S["tiles_a"]
    kt_of = lambda ctx0: ctx0 // 128
    def tile_body(i, c0, ntok, full):
        ns = ntok // 128
        slot = i % 2
        K.load_x(slot, xc[c0:c0 + ntok, :], ns)
        K.norm(slot, ns)
        hnT, hts = K.hnT, K.hnT_tks
        for c in range(4):
            b, btk = fmbank()
            fm_chunk(P, b, btk, W, W_tk, 1536 + c * 128, 128, hnT, hts, ns)
            o, otk = kn[c % 2]
            headnorm_fm(P, K, b, btk, ntok, BD, gk[:, 0:1], sq, sq_tk, banks[4][0], banks[4][1], rs, rs_tk,
                        o[:, 0:ntok], otk, 1.0 / 64)
            P.dma("sp", S["KT"][c * 128:(c + 1) * 128, c0:c0 + ntok], o[:, 0:ntok], rd=[otk])
        if full:
            o0 = c0 - OWN0
            for c in range(4):
                b, btk = fmbank()
                fm_chunk(P, b, btk, W, W_tk, 1024 + c * 128, 128, hnT, hts, ns)
                o, otk = kn[c % 2]
                headnorm_fm(P, K, b, btk, ntok, BD, gq[:, 0:1], sq, sq_tk, banks[4][0], banks[4][1], rs, rs_tk,
                            o[:, 0:ntok], otk, 1.0 / 64)
                P.dma("sp", S["QT"][c * 128:(c + 1) * 128, o0:o0 + ntok], o[:, 0:ntok], rd=[otk])
        for s in range(ns):
            b, btk = banks[5]
            tm_block(P, b, btk, hnT, hts[s], s, W, W_tk, 2048, 512)
            o, otk = vtm[s % 2]
            P.op("act", lambda e, o=o, b=b: e.activation(out=o[:], in_=b[:], func=AF.Copy), rd=[btk], wr=[otk])
            P.dma("sp", S["VV"][c0 + s * 128:c0 + (s + 1) * 128, :], o[:], rd=[otk])
        b, btk = banks[6]
        fm_chunk(P, b, btk, W, W_tk, 2560, 8, hnT, hts, ns)
        P.op("act", lambda e, b=b: e.activation(out=fe[:, 0:ntok], in_=b[0:8, 0:ntok], func=AF.Exp, scale=-1.0,
                                                bias=nbf[:, 0:1]), rd=[btk], wr=[fe_tk])
        P.op("act", lambda e: e.activation(out=fe[:, 0:ntok], in_=fe[:, 0:ntok], func=AF.Ln, bias=1.0),
             rd=[fe_tk], wr=[fe_tk])
        P.op("dve", lambda e: e.tensor_tensor_scan(out=negc[:, 0:ntok], data0=ones8[:, 0:ntok], data1=fe[:, 0:ntok],
                                                   initial=carry[:, 0:1], op0=ALU.mult, op1=ALU.add),
             rd=[fe_tk, carry_tk], wr=[negc_tk])
        P.op("dve", lambda e: e.tensor_copy(out=carry[:], in_=negc[:, ntok - 1:ntok]), rd=[negc_tk], wr=[carry_tk])
        P.dma("sp", S["NR"][:, c0 // 128:c0 // 128 + ns], negc[:, 64:ntok:128], rd=[negc_tk])
        b, btk = banks[7]
        for s in range(ns):
            P.op("pe", lambda e, s=s, b=b: e.matmul(b[:, s * 8:(s + 1) * 8], lhsT=negc[:, s * 128:(s + 1) * 128],
                                                    rhs=id8[:], start=True, stop=True), rd=[negc_tk], wr=[btk])
        P.op("dve", lambda e, b=b: e.tensor_copy(out=nk[:, 0:ns, :].rearrange("p s h -> p (s h)"), in_=b[:, 0:ns * 8]),
             rd=[btk], wr=[nk_tk])
        P.dma("sp", S["NK"][:, kt_of(c0):kt_of(c0) + ns, :], nk[:, 0:ns, :], rd=[nk_tk])
        if not full:
            return
        for c in range(4):
            b, btk = fmbank()
            fm_chunk(P, b, btk, W, W_tk, c * 128, 128, hnT, hts, ns)
            P.op("act", lambda e, c=c, b=b: e.activation(out=uT[:, c, 0:ntok], in_=b[:, 0:ntok], func=AF.Gelu),
                 rd=[btk], wr=[uT_tk[c]])
        for s in range(ns):
            b, btk = banks[5]
            tm_block(P, b, btk, hnT, hts[s], s, W, W_tk, 512, 512)
            P.op("act", lambda e, b=b: e.activation(out=vg[:], in_=b[:], func=AF.Gelu), rd=[btk], wr=[vg_tk])
            for g in range(4):
                P.op("act", lambda e, g=g: e.activation(out=K.junk[:, 0:128], in_=vg[:, g * 128:(g + 1) * 128],
                                                        func=AF.Square, accum_out=ssg[:, g:g + 1]),
                     rd=[vg_tk], wr=[K.junk_tk, ssg_tk])
            rsqrt_small(P, ssg, ssg_tk, 4, 1.0 / 128)
            for g in range(4):
                P.op("dve", lambda e, g=g: e.scalar_tensor_tensor(out=vn[:, g * 128:(g + 1) * 128],
                                                                  in0=vg[:, g * 128:(g + 1) * 128], scalar=ssg[:, g:g + 1],
                                                                  in1=gvb[:, g * 128:(g + 1) * 128], op0=ALU.mult, op1=ALU.mult),
                     rd=[vg_tk, ssg_tk], wr=[vn_tk])
            b2, b2tk = banks[6]
            for g in range(4):
                P.op("pe", lambda e, g=g, b2=b2: e.matmul(b2[:, g * 128:(g + 1) * 128], lhsT=vn[:, g * 128:(g + 1) * 128],
                                                          rhs=wsT[:, g, :], start=True, stop=True), rd=[vn_tk], wr=[b2tk])
            P.op("dve", lambda e, b2=b2: e.tensor_tensor(out=t1[:], in0=b2[:], in1=bsb[:], op=ALU.add),
                 rd=[b2tk], wr=[t1_tk])
            P.op("dve", lambda e, s=s: e.tensor_tensor(out=yaT[:, :, s * 128:(s + 1) * 128],
                                                       in0=t1[:].rearrange("p (g t) -> p g t", g=4),
                                                       in1=uT[:, :, s * 128:(s + 1) * 128], op=ALU.mult),
                 rd=[t1_tk] + uT_tk, wr=[ya_tk])
        o0 = c0 - OWN0
        P.dma("sp", S["YT"][0:512, o0:o0 + ntok].rearrange("(g p) t -> p g t", p=128), yaT[:, :, 0:ntok], rd=[ya_tk])

    for i, (c0, ntok, full) in enumerate(tiles):
        tile_body(i, c0, ntok, full)
    P.barrier()
    P.flush()
    C.close()


def phase_b(nc, P, S):
    C = Ctx(nc, P)
    banks = C.psum_banks()
    ident = C.sb([128, 128], BF16, "ident"); c_tk = Tk()
    P.dma("pool", ident[:], S["ident"], wr=[c_tk])
    tri = C.sb([128, 128], BF16, "tri")
    P.dma("pool", tri[:], S["tri"], wr=[c_tk])
    NKm = C.sb([128, 64, 8], F32, "nkm")
    P.dma("sp", NKm[:], S["NK"], wr=[c_tk])
    cm = C.sb([128, 64], F32, "cm")
    P.dma("sp", cm[:], S["cmask"], wr=[c_tk])
    Rb = C.sb([128, 8, 64], F32, "rb")
    P.dma("sp", Rb[:].rearrange("p h k -> p (h k)"), S["NR"].rearrange("h k -> (h k)").partition_broadcast(128), wr=[c_tk])
    norm_dep(P, [c_tk])
    for h in range(8):
        P.op("dve", lambda e, h=h: e.tensor_tensor(out=NKm[:, :, h], in0=NKm[:, :, h], in1=cm[:], op=ALU.add), wr=[c_tk])
    norm_dep(P, [c_tk])
    KTp = C.sb([128, SEQ], BF16, "ktp"); kt_tk = Tk()
    QTp = C.sb([128, NOWN], BF16, "qtp"); qt_tk = Tk()
    Va = C.sb([128, 64, 2, 65], BF16, "va"); va_tk = Tk()
    bias = [(C.sb([128, 64], F32, "bias"), Tk()) for _ in range(2)]
    pbuf = [(C.sb([128, 128], BF16, "pb"), Tk()) for _ in range(4)]
    ybt = C.sb([128, 33, 128], BF16, "ybt"); ybt_tk = Tk()
    rden = C.sb([128, 2], F32, "rden"); rden_tk = Tk()
    ybT = [(C.sb([128, 128], BF16, "ybT"), Tk()) for _ in range(2)]
    NQ = NOWN // 128
    for pr in range(4):
        P.dma("sp", KTp[:, 0:4096], S["KT"][pr * 128:(pr + 1) * 128, 0:4096], wr=[kt_tk])
        P.dma("sp", KTp[:, 4096:8192], S["KT"][pr * 128:(pr + 1) * 128, 4096:8192], wr=[kt_tk])
        P.dma("sp", QTp[:], S["QT"][pr * 128:(pr + 1) * 128, :], wr=[qt_tk])
        for hb in range(2):
            hcol = pr * 128 + hb * 64
            for q4 in range(4):
                P.dma("sp", Va[:, q4 * 16:(q4 + 1) * 16, hb, 0:64],
                      S["VV"][q4 * 2048:(q4 + 1) * 2048, hcol:hcol + 64].rearrange("(k p) d -> p k d", p=128), wr=[va_tk])
        P.op("pool", lambda e: e.memset(Va[:, :, :, 64:65], 1.0), wr=[va_tk])
        def head_block(pr, j, hb, it):
                qkt = (OWN0 // 128) + j
                nkt = qkt + 1
                h = pr * 2 + hb
                pl, ph = hb * 64, (hb + 1) * 64
                bs, bs_tk = bias[it % 2]
                P.op("dve", lambda e, bs=bs, h=h, qkt=qkt, nkt=nkt: e.tensor_scalar(
                    out=bs[:, 0:nkt], in0=NKm[:, 0:nkt, h], scalar1=Rb[:, h, qkt:qkt + 1], scalar2=None, op0=ALU.subtract),
                    wr=[bs_tk])
                acc, acc_tk = banks[4 + it % 2]

                def S_(kt):
                    sb_, sb_tk = banks[kt % 4]
                    P.op("pe", lambda e, kt=kt, sb_=sb_: e.matmul(sb_[:, 0:128], lhsT=KTp[pl:ph, kt * 128:(kt + 1) * 128],
                                                                  rhs=QTp[pl:ph, j * 128:(j + 1) * 128], start=True, stop=True),
                         rd=[kt_tk, qt_tk], wr=[sb_tk])

                def E_(kt):
                    sb_, sb_tk = banks[kt % 4]
                    pb_, pb_tk = pbuf[kt % 4]
                    P.op("act", lambda e, kt=kt, sb_=sb_, pb_=pb_: e.activation(out=pb_[:], in_=sb_[:, 0:128], func=AF.Exp,
                                                                                bias=bs[:, kt:kt + 1]),
                         rd=[sb_tk, bs_tk], wr=[pb_tk])
                    if kt == qkt:
                        P.op("pool", lambda e, pb_=pb_: e.tensor_tensor(out=pb_[:], in0=pb_[:], in1=tri[:], op=ALU.mult),
                             rd=[pb_tk], wr=[pb_tk])

                def V_(kt):
                    pb_, pb_tk = pbuf[kt % 4]
                    P.op("pe", lambda e, kt=kt, pb_=pb_: e.matmul(acc[:, 0:65], lhsT=pb_[:], rhs=Va[:, kt, hb, :],
                                                                  start=(kt == 0), stop=(kt == nkt - 1)),
                         rd=[pb_tk, va_tk], wr=[acc_tk])

                S_(0); E_(0)
                if nkt > 1:
                    S_(1); E_(1)
                for kt in range(nkt):
                    if kt + 2 < nkt:
                        S_(kt + 2); E_(kt + 2)
                    V_(kt)
                P.op("dve", lambda e, hb=hb: e.tensor_scalar(out=rden[:, hb:hb + 1], in0=acc[:, 64:65], scalar1=1e-30, scalar2=None,
                                                             op0=ALU.add), rd=[acc_tk], wr=[rden_tk])
                P.op("dve", lambda e, hb=hb: e.reciprocal(out=rden[:, hb:hb + 1], in_=rden[:, hb:hb + 1]), rd=[rden_tk], wr=[rden_tk])
                P.op("dve", lambda e, hb=hb, j=j: e.tensor_scalar(out=ybt[:, j, hb * 64:(hb + 1) * 64], in0=acc[:, 0:64],
                                                                  scalar1=rden[:, hb:hb + 1], scalar2=None, op0=ALU.mult),
                     rd=[acc_tk, rden_tk], wr=[ybt_tk])
        def qtile_tail(pr, j):
            tb, tb_tk = banks[6 + j % 2]
            tpb = tb[:].bitcast(BF16)
            P.op("pe", lambda e, j=j, tpb=tpb: e.transpose(out=tpb[:, 0:128], in_=ybt[:, j, :], identity=ident[:]),
                 rd=[ybt_tk], wr=[tb_tk])
            yo, yo_tk = ybT[j % 2]
            P.op("dve", lambda e, yo=yo, tpb=tpb: e.tensor_copy(out=yo[:], in_=tpb[:, 0:128]), rd=[tb_tk], wr=[yo_tk])
            P.dma("sp", S["YT"][512 + pr * 128:512 + (pr + 1) * 128, j * 128:(j + 1) * 128], yo[:], rd=[yo_tk])

        it = 0
        for j in range(NQ):
            for hb in range(2):
                head_block(pr, j, hb, it)
                it += 1
            qtile_tail(pr, j)
    P.barrier()
    P.flush()
    C.close()


def phase_cx(nc, P, S, prm, layer, w_out_ap, xsrc, tiles):
    C = Ctx(nc, P)
    K = Kit(nc, P, C, prm["g_xa"][layer], S["ident"])
    banks = K.banks
    c_tk = Tk()
    gbm = C.sb([128, D], F32, "gbm")
    P.dma("sp", gbm[:], prm["g_mem"][layer].partition_broadcast(128), wr=[c_tk])
    Wo = C.sb([128, NCH, D], BF16, "wo"); Wq = C.sb([128, NCH, 512], BF16, "wq")
    Wkv = C.sb([128, NCH, D], BF16, "wkv"); Wxo = C.sb([128, 4, D], BF16, "wxo")
    load_weight_bf16(P, Wo, c_tk, w_out_ap, NCH, D)
    load_weight_bf16(P, Wq, c_tk, prm["xa_wq"][layer], NCH, 512)
    load_weight_bf16(P, Wkv, c_tk, prm["xa_wkv"][layer], NCH, D)
    load_weight_bf16(P, Wxo, c_tk, prm["xa_wo"][layer], 4, D)
    ones = C.sb([128, 128], BF16, "ones")
    gqc = C.sb([128, 1], F32, "gqc")
    P.dma("sp", gqc[:], prm["xa_gq"][layer].rearrange("(p o) -> p o", o=1), wr=[c_tk])
    gkb = C.sb([128, 128], F32, "gkb")
    P.dma("sp", gkb[:], prm["xa_gk"][layer].partition_broadcast(128), wr=[c_tk])
    K.consts_ready([c_tk])
    P.op("dve", lambda e: e.memset(ones[:], 1.0), wr=[c_tk])
    P.op("dve", lambda e: e.tensor_scalar(out=gqc[:], in0=gqc[:], scalar1=float(128 ** -0.5), scalar2=None, op0=ALU.mult),
         wr=[c_tk])
    norm_dep(P, [c_tk])
    kT = C.sb([128, 4, 256], BF16, "kT"); vm = C.sb([128, 2, 512], BF16, "vm"); m_tk = Tk()
    ssk = C.sb([128, 4], F32, "ssk"); ssk_tk = Tk()
    kn = C.sb([128, 512], BF16, "kn"); kn_tk = Tk()
    K.load_x(0, S["mem"], 2)
    xt, xt_tk = K.xts[0]
    emit_norm_T(P, C, xt, xt_tk, 2, gbm, K.hn, K.hn_tk, K.ss, K.ss_tk, K.rstd, K.rstd_tk, K.junk, K.junk_tk,
                K.hnT, K.hnT_tks, [banks[0], banks[1]], K.ident)

    def mem_sub(s):
        b, btk = banks[2]
        tm_block(P, b, btk, K.hnT, K.hnT_tks[s], s, Wkv, c_tk, 0, 512)
        for h in range(4):
            P.op("act", lambda e, h=h: e.activation(out=K.junk[:, 0:128], in_=b[:, h * 128:(h + 1) * 128], func=AF.Square,
                                                    accum_out=ssk[:, h:h + 1]), rd=[btk], wr=[K.junk_tk, ssk_tk])
        rsqrt_small(P, ssk, ssk_tk, 4, 1.0 / 128)
        for h in range(4):
            P.op("dve", lambda e, h=h: e.scalar_tensor_tensor(out=kn[:, h * 128:(h + 1) * 128], in0=b[:, h * 128:(h + 1) * 128],
                                                              scalar=ssk[:, h:h + 1], in1=gkb[:], op0=ALU.mult, op1=ALU.mult),
                 rd=[btk, ssk_tk], wr=[kn_tk])
        b3, b3tk = banks[3]
        pb = b3[:].bitcast(BF16)
        for h in range(4):
            P.op("pe", lambda e, h=h: e.transpose(out=pb[:, h * 128:(h + 1) * 128], in_=kn[:, h * 128:(h + 1) * 128],
                                                  identity=K.ident[:]), rd=[kn_tk], wr=[b3tk])
        P.op("dve", lambda e: e.tensor_copy(out=kT[:, :, s * 128:(s + 1) * 128],
                                            in_=pb[:, 0:512].rearrange("p (h m) -> p h m", h=4)), rd=[b3tk], wr=[m_tk])
        b4, b4tk = banks[4]
        tm_block(P, b4, b4tk, K.hnT, K.hnT_tks[s], s, Wkv, c_tk, 512, 512)
        P.op("act", lambda e: e.activation(out=vm[:, s, :], in_=b4[:], func=AF.Copy), rd=[b4tk], wr=[m_tk])

    mem_sub(0)
    mem_sub(1)
    norm_dep(P, [m_tk])
    yT = C.sb([128, NCH, 512], BF16, "yT"); yT_tk = Tk()
    sq = C.sb([128, 512], BF16, "sq"); sq_tk = Tk()
    rs = C.sb([128, 512], F32, "rs"); rs_tk = Tk()
    qn = C.sb([128, 512], BF16, "qn"); qn_tk = Tk()
    pm = [(C.sb([128, 512], BF16, "pm"), Tk()) for _ in range(2)]
    rdn = C.sb([128, 512], F32, "rdn"); rdn_tk = Tk()
    oTn = C.sb([128, 4, 512], BF16, "oTn"); oTn_tk = Tk()

    def tile_body(i, r0, o0, ntok):
        ns = ntok // 128
        slot = 1 - (i % 2) if False else (i % 2)
        xt, xt_tk = K.xts[slot]
        K.load_x(slot, xsrc[r0:r0 + ntok, :], ns)
        P.dma("sp", yT[:, :, 0:ntok], S["YT"][:, o0:o0 + ntok].rearrange("(c p) t -> p c t", p=128), wr=[yT_tk])
        jj = 0
        for s in range(ns):
            for nh in range(2):
                b, btk = banks[6 + jj % 2]; jj += 1
                for kc in range(NCH):
                    P.op("pe", lambda e, kc=kc, s=s, nh=nh, b=b: e.matmul(b[:, :], lhsT=yT[:, kc, s * 128:(s + 1) * 128],
                                                                         rhs=Wo[:, kc, nh * 512:(nh + 1) * 512],
                                                                         start=(kc == 0), stop=(kc == NCH - 1)),
                         rd=[yT_tk], wr=[btk])
                P.op("dve", lambda e, s=s, nh=nh, b=b: e.tensor_tensor(out=xt[:, s, nh * 512:(nh + 1) * 512],
                                                                      in0=xt[:, s, nh * 512:(nh + 1) * 512], in1=b[:, :], op=ALU.add),
                     rd=[btk], wr=[xt_tk])
        K.norm(slot, ns)
        for h in range(4):
            b, btk = banks[2 + h % 2]
            fm_chunk(P, b, btk, Wq, c_tk, h * 128, 128, K.hnT, K.hnT_tks, ns)
            headnorm_fm(P, K, b, btk, ntok, ones, gqc[:, 0:1], sq, sq_tk, banks[4][0], banks[4][1], rs, rs_tk,
                        qn[:, 0:ntok], qn_tk, 1.0 / 128)
            for mt in range(2):
                sb_, sb_tk = banks[2 + mt]
                P.op("pe", lambda e, h=h, mt=mt, sb_=sb_: e.matmul(sb_[:, 0:ntok], lhsT=kT[:, h, mt * 128:(mt + 1) * 128],
                                                                  rhs=qn[:, 0:ntok], start=True, stop=True),
                     rd=[qn_tk], wr=[sb_tk])
                p_, p_tk = pm[mt]
                P.op("act", lambda e, sb_=sb_, p_=p_: e.activation(out=p_[:, 0:ntok], in_=sb_[:, 0:ntok], func=AF.Exp),
                     rd=[sb_tk], wr=[p_tk])
            bo, bo_tk = banks[4]
            bd, bd_tk = banks[5]
            for mt in range(2):
                p_, p_tk = pm[mt]
                P.op("pe", lambda e, h=h, mt=mt, p_=p_: e.matmul(bo[:, 0:ntok], lhsT=vm[:, mt, h * 128:(h + 1) * 128],
                                                                rhs=p_[:, 0:ntok], start=(mt == 0), stop=(mt == 1)),
                     rd=[p_tk], wr=[bo_tk])
            for mt in range(2):
                p_, p_tk = pm[mt]
                P.op("pe", lambda e, mt=mt, p_=p_: e.matmul(bd[:, 0:ntok], lhsT=ones[:], rhs=p_[:, 0:ntok],
                                                           start=(mt == 0), stop=(mt == 1)), rd=[p_tk], wr=[bd_tk])
            P.op("dve", lambda e: e.reciprocal(out=rdn[:, 0:ntok], in_=bd[:, 0:ntok]), rd=[bd_tk], wr=[rdn_tk])
            P.op("dve", lambda e, h=h: e.tensor_tensor(out=oTn[:, h, 0:ntok], in0=bo[:, 0:ntok], in1=rdn[:, 0:ntok], op=ALU.mult),
                 rd=[bo_tk, rdn_tk], wr=[oTn_tk])
        for s in range(ns):
            for nh in range(2):
                b, btk = banks[6 + jj % 2]; jj += 1
                for h in range(4):
                    P.op("pe", lambda e, h=h, s=s, nh=nh, b=b: e.matmul(b[:, :], lhsT=oTn[:, h, s * 128:(s + 1) * 128],
                                                                       rhs=Wxo[:, h, nh * 512:(nh + 1) * 512],
                                                                       start=(h == 0), stop=(h == 3)),
                         rd=[oTn_tk], wr=[btk])
                P.op("dve", lambda e, s=s, nh=nh, b=b: e.tensor_tensor(out=xt[:, s, nh * 512:(nh + 1) * 512],
                                                                      in0=xt[:, s, nh * 512:(nh + 1) * 512], in1=b[:, :], op=ALU.add),
                     rd=[btk], wr=[xt_tk])
        P.dma("sp", S["X1"][o0:o0 + ntok, :].rearrange("(s p) d -> p s d", p=128), xt[:, 0:ns, :], rd=[xt_tk])

    for i, (r0, o0, ntok) in enumerate(tiles):
        tile_body(i, r0, o0, ntok)
    P.barrier()
    P.flush()
    C.close()


def phase_d(nc, P, S, prm, tiles):
    C = Ctx(nc, P)
    K = Kit(nc, P, C, prm["g_mix"][1], S["ident"], nx=2)
    banks = K.banks
    c_tk = Tk()
    W = C.sb([128, NCH, 2048], BF16, "win1")
    load_weight_bf16(P, W, c_tk, prm["o_w_in"][0], NCH, 2048, split=4)
    Wp = C.sb([128, 4, 128], BF16, "wp")
    P.dma("pool", Wp[:], prm["o_w_pool"][0].rearrange("g c d -> c g d"), wr=[c_tk])
    spc = C.sb([128, 4], F32, "spc")
    cw = C.sb([128, 4, 3], F32, "cw")
    for g in range(4):
        P.dma("sp", spc[:, g:g + 1], prm["o_s_pool"][0][g * 128:(g + 1) * 128].rearrange("(p o) -> p o", o=1), wr=[c_tk])
        for k in range(3):
            P.dma("sp", cw[:, g, k:k + 1], prm["o_conv_w"][0][k, g * 128:(g + 1) * 128].rearrange("(p o) -> p o", o=1), wr=[c_tk])
    icf = C.sb([128, 4, 512], F32, "icf")
    P.dma("sp", icf[:].rearrange("p g t -> p (g t)"), S["icnt"].rearrange("g t -> (g t)").partition_broadcast(128), wr=[c_tk])
    hfl = C.sb([128, 1], F32, "hfl")
    P.dma("sp", hfl[:], S["hflag"].partition_broadcast(128), wr=[c_tk])
    K.consts_ready([c_tk])
    L = 16 + 512
    zext = C.sb([128, 4, L], F32, "zext"); z_tk = [Tk() for _ in range(4)]
    bA = C.sb([128, L], F32, "bA"); bB = C.sb([128, L], F32, "bB"); bC = C.sb([128, L], F32, "bC"); s_tk = Tk()
    pT = C.sb([128, 512], BF16, "pT"); pT_tk = Tk()
    tmp = C.sb([128, 512], F32, "tmp"); tmp_tk = Tk()
    xg = C.sb([128, 4, 2 + 512], F32, "xg"); xg_tk = [Tk() for _ in range(4)]
    gcs = C.sb([128, 512], F32, "gcs"); gcs_tk = Tk()
    acc = C.sb([128, 512], F32, "acc"); acc_tk = Tk()
    yD = C.sb([128, 8, 512], BF16, "yD"); yD_tk = Tk()
    for g in range(4):
        P.op("dve", lambda e, g=g: e.memset(zext[:, g, :], 0.0), wr=[z_tk[g]])
        P.op("dve", lambda e, g=g: e.memset(xg[:, g, :], 0.0), wr=[xg_tk[g]])
    WIN = (2, 4, 8, 16)

    def tile_body(i, o0, ntok):
        ns = ntok // 128
        Lt = 16 + ntok
        slot = i % 2
        K.load_x(slot, S["X1"][o0:o0 + ntok, :], ns)
        K.norm(slot, ns)
        hnT, hts = K.hnT, K.hnT_tks
        for g in range(4):
            b, btk = banks[2 + g % 2]
            fm_chunk(P, b, btk, W, c_tk, g * 128, 128, hnT, hts, ns)
            P.op("act", lambda e, g=g, b=b: e.activation(out=zext[:, g, 16:Lt], in_=b[:, 0:ntok], func=AF.Copy),
                 rd=[btk], wr=[z_tk[g]])
            if i > 0:
                z = zext[:, g, :]
                P.op("dve", lambda e, z=z: e.tensor_tensor(out=bA[:, 1:Lt], in0=z[:, 1:Lt], in1=z[:, 0:Lt - 1], op=ALU.add),
                     rd=[z_tk[g]], wr=[s_tk])
                sw = bA
                if g >= 1:
                    P.op("dve", lambda e: e.tensor_tensor(out=bB[:, 3:Lt], in0=bA[:, 3:Lt], in1=bA[:, 1:Lt - 2], op=ALU.add),
                         rd=[s_tk], wr=[s_tk])
                    sw = bB
                if g >= 2:
                    P.op("dve", lambda e: e.tensor_tensor(out=bC[:, 7:Lt], in0=bB[:, 7:Lt], in1=bB[:, 3:Lt - 4], op=ALU.add),
                         rd=[s_tk], wr=[s_tk])
                    sw = bC
                if g >= 3:
                    P.op("dve", lambda e: e.tensor_tensor(out=bA[:, 15:Lt], in0=bC[:, 15:Lt], in1=bC[:, 7:Lt - 8], op=ALU.add),
                         rd=[s_tk], wr=[s_tk])
                    sw = bA
                if i == 1:
                    P.op("dve", lambda e, g=g, sw=sw: e.tensor_tensor(out=tmp[:, 0:ntok], in0=sw[:, 16:Lt], in1=icf[:, g, 0:ntok],
                                                                      op=ALU.mult), rd=[s_tk], wr=[tmp_tk])
                    P.op("dve", lambda e, z=z: e.tensor_tensor(out=pT[:, 0:ntok], in0=tmp[:, 0:ntok], in1=z[:, 16:Lt],
                                                               op=ALU.subtract), rd=[tmp_tk, z_tk[g]], wr=[pT_tk])
                else:
                    P.op("dve", lambda e, g=g, sw=sw, z=z: e.scalar_tensor_tensor(out=pT[:, 0:ntok], in0=sw[:, 16:Lt],
                                                                                  scalar=1.0 / WIN[g], in1=z[:, 16:Lt],
                                                                                  op0=ALU.mult, op1=ALU.subtract),
                         rd=[s_tk, z_tk[g]], wr=[pT_tk])
                b2, b2tk = banks[4 + g % 2]
                P.op("pe", lambda e, g=g, b2=b2: e.matmul(b2[:, 0:ntok], lhsT=Wp[:, g, :], rhs=pT[:, 0:ntok], start=True, stop=True),
                     rd=[pT_tk], wr=[b2tk])
                P.op("dve", lambda e, g=g, b2=b2: e.tensor_scalar(out=yD[:, g, 0:ntok], in0=b2[:, 0:ntok], scalar1=spc[:, g:g + 1],
                                                                  scalar2=None, op0=ALU.mult), rd=[b2tk], wr=[yD_tk])
            if i == 0:
                P.op("dve", lambda e, g=g: e.tensor_scalar(out=zext[:, g, 0:16], in0=zext[:, g, Lt - 16:Lt], scalar1=hfl[:, 0:1],
                                                           scalar2=None, op0=ALU.mult), rd=[z_tk[g]], wr=[z_tk[g]])
            else:
                P.op("dve", lambda e, g=g: e.tensor_copy(out=zext[:, g, 0:16], in_=zext[:, g, Lt - 16:Lt]),
                     rd=[z_tk[g]], wr=[z_tk[g]])
        for c in range(4):
            b, btk = banks[2]
            fm_chunk(P, b, btk, W, c_tk, 1536 + c * 128, 128, hnT, hts, ns)
            P.op("act", lambda e, b=b: e.activation(out=gcs[:, 0:ntok], in_=b[:, 0:ntok], func=AF.Copy), rd=[btk], wr=[gcs_tk])
            b1, b1tk = banks[3]
            fm_chunk(P, b1, b1tk, W, c_tk, 512 + c * 128, 128, hnT, hts, ns)
            P.op("dve", lambda e, c=c, b1=b1: e.tensor_tensor(out=xg[:, c, 2:2 + ntok], in0=gcs[:, 0:ntok], in1=b1[:, 0:ntok],
                                                              op=ALU.mult), rd=[gcs_tk, b1tk], wr=[xg_tk[c]])
            if i > 0:
                P.op("dve", lambda e, c=c: e.tensor_scalar(out=acc[:, 0:ntok], in0=xg[:, c, 2:2 + ntok], scalar1=cw[:, c, 2:3],
                                                           scalar2=None, op0=ALU.mult), rd=[xg_tk[c]], wr=[acc_tk])
                for k in (1, 0):
                    P.op("dve", lambda e, c=c, k=k: e.scalar_tensor_tensor(out=acc[:, 0:ntok], in0=xg[:, c, k:k + ntok],
                                                                           scalar=cw[:, c, k:k + 1], in1=acc[:, 0:ntok],
                                                                           op0=ALU.mult, op1=ALU.add),
                         rd=[xg_tk[c], acc_tk], wr=[acc_tk])
                b2, b2tk = banks[6 + c % 2]
                fm_chunk(P, b2, b2tk, W, c_tk, 1024 + c * 128, 128, hnT, hts, ns)
                P.op("dve", lambda e, c=c, b2=b2: e.tensor_tensor(out=yD[:, 4 + c, 0:ntok], in0=acc[:, 0:ntok], in1=b2[:, 0:ntok],
                                                                  op=ALU.mult), rd=[acc_tk, b2tk], wr=[yD_tk])
            if i == 0:
                P.op("dve", lambda e, c=c: e.tensor_scalar(out=xg[:, c, 0:2], in0=xg[:, c, ntok:ntok + 2], scalar1=hfl[:, 0:1],
                                                           scalar2=None, op0=ALU.mult), rd=[xg_tk[c]], wr=[xg_tk[c]])
            else:
                P.op("dve", lambda e, c=c: e.tensor_copy(out=xg[:, c, 0:2], in_=xg[:, c, ntok:ntok + 2]),
                     rd=[xg_tk[c]], wr=[xg_tk[c]])
        if i > 0:
            P.dma("sp", S["YT"][:, o0:o0 + ntok].rearrange("(c p) t -> p c t", p=128), yD[:, :, 0:ntok], rd=[yD_tk])

    for i, (o0, ntok) in enumerate(tiles):
        tile_body(i, o0, ntok)
    P.barrier()
    P.flush()
    C.close()


PARAMS = ["g_mix", "g_xa", "g_mem", "xa_wq", "xa_wkv", "xa_wo", "xa_gq", "xa_gk", "g_ffn", "w_gate", "w_up", "w_down",
          "e_w_in", "e_b_f", "e_g_v", "e_w_s", "e_b_s", "e_g_qn", "e_g_kn", "e_w_out",
          "o_w_in", "o_w_pool", "o_s_pool", "o_conv_w", "o_w_out"]


def build_program(shapes, debug=False, nphases=7):
    nc = bass.Bass("TRN2", target_bir_lowering=False)
    S = {}
    prm = {}
    ein = lambda name, shape: nc.dram_tensor(name, list(shape), F32, kind="ExternalInput").ap()
    S["xc"] = ein("xc", [SEQ, D])
    S["mem"] = ein("mem", [256, D])
    for k in PARAMS:
        prm[k] = ein(k, shapes[k])
    S["ident"] = ein("ident", [128, 128])
    S["identf"] = S["ident"]
    S["bd64"] = ein("bd64", [128, 128])
    S["tri"] = ein("tri", [128, 128])
    S["cmask"] = ein("cmask", [128, 64])
    S["icnt"] = ein("icnt", [4, 512])
    S["hflag"] = ein("hflag", [1])
    kind = "ExternalOutput" if debug else "Internal"
    scr = lambda name, shape, dt: nc.dram_tensor(name, list(shape), dt, kind=kind).ap()
    S["KT"] = scr("KT", [512, SEQ], BF16)
    S["QT"] = scr("QT", [512, NOWN], BF16)
    S["VV"] = scr("VV", [SEQ, 512], BF16)
    S["NK"] = scr("NK", [128, 64, 8], F32)
    S["NR"] = scr("NR", [8, 64], F32)
    S["YT"] = scr("YT", [D, NOWN], BF16)
    S["X1"] = scr("X1", [NOWN, D], F32)
    out = nc.dram_tensor("out", [HALF, D], F32, kind="ExternalOutput").ap()
    tiles_a = [(512 * i, 512, False) for i in range(7)] + [(3584, 384, False), (OWN0, 128, True)] + \
              [(HALF + 512 * i, 512, True) for i in range(8)]
    S["tiles_a"] = tiles_a
    own = [(0, 128)] + [(128 + 512 * i, 512) for i in range(8)]
    with ExitStack() as es:
        P = Prog(nc, es)
        phase_a(nc, P, S, prm)
        if nphases > 1:
            phase_b(nc, P, S)
        if nphases > 2:
            phase_cx(nc, P, S, prm, 0, prm["e_w_out"][0], S["xc"], [(OWN0 + o, o, n) for (o, n) in own])
        if nphases > 3:
            phase_ffn(nc, P, S["X1"], S["X1"], own, [o for (o, n) in own], prm["g_ffn"][0], prm["w_gate"][0], prm["w_up"][0],
                      prm["w_down"][0], S["ident"])
        if nphases > 4:
            phase_d(nc, P, S, prm, own)
        if nphases > 5:
            phase_cx(nc, P, S, prm, 1, prm["o_w_out"][0], S["X1"], [(o, o, n) for (o, n) in own[1:]])
        if nphases > 6:
            phase_ffn(nc, P, S["X1"], out, own[1:], [o - 128 for (o, n) in own[1:]], prm["g_ffn"][1], prm["w_gate"][1],
                      prm["w_up"][1], prm["w_down"][1], S["ident"])
    return nc


def make_in_maps(inputs):
    x = np.ascontiguousarray(inputs["x"], dtype=np.float32)
    mem = np.ascontiguousarray(inputs["mem"], dtype=np.float32)
    ident = np.eye(128, dtype=np.float32)
    bd = np.zeros((128, 128), np.float32)
    bd[:64, :64] = 1
    bd[64:, 64:] = 1
    tri = np.triu(np.ones((128, 128), np.float32))
    maps = []
    for c in range(8):
        b, h = c // 2, c % 2
        m = {k: np.ascontiguousarray(inputs[k], dtype=np.float32) for k in PARAMS}
        if h == 1:
            m["xc"] = x[b]
            cm = np.zeros((128, 64), np.float32)
            ic = np.tile((1.0 / np.array([2, 4, 8, 16], np.float32))[:, None], (1, 512))
            hf = np.ones((1,), np.float32)
        else:
            m["xc"] = np.concatenate([np.zeros((HALF, D), np.float32), x[b, :HALF]], axis=0)
            cm = np.zeros((128, 64), np.float32)
            cm[:, :32] = NEG
            pos = np.arange(512, dtype=np.float32)
            ic = np.stack([1.0 / np.minimum(pos + 1, w) for w in (2, 4, 8, 16)]).astype(np.float32)
            hf = np.zeros((1,), np.float32)
        m.update(mem=mem[b], ident=ident, bd64=bd, tri=tri, cmask=cm, icnt=ic, hflag=hf)
        maps.append(m)
    return maps


def kernel(**inputs):
    shapes = {k: tuple(np.shape(inputs[k])) for k in PARAMS}
    nc = build_program(shapes)
    maps = make_in_maps(inputs)
    res = run_bass_kernel_spmd(nc, maps, core_ids=list(range(8)))
    out = np.empty((4, SEQ, D), np.float32)
    for c in range(8):
        b, h = c // 2, c % 2
        out[b, h * HALF:(h + 1) * HALF] = res.results[c]["out"]
    return out
```

```python
import numpy as np
from contextlib import ExitStack
import concourse.bass as bass
import concourse.mybir as mybir
from concourse.bass_utils import run_bass_kernel_spmd

F32 = mybir.dt.float32
BF16 = mybir.dt.bfloat16
AF = mybir.ActivationFunctionType
ALU = mybir.AluOpType
AX = mybir.AxisListType

D = 1024
DFF = 2816
NCH = 8
EPS = 1e-6
SEQ = 8192
HALF = 4096
HALO = 128
OWN0 = HALF - HALO
NOWN = HALF + HALO
NEG = -30000.0


class Tk:
    __slots__ = ("w", "r")

    def __init__(self):
        self.w = {}
        self.r = {}


def _merge(d, s):
    for k, v in s.items():
        if d.get(k, 0) < v:
            d[k] = v


class Prog:
    ENG = ("pe", "act", "dve", "pool", "sp")
    NDS = 12

    def __init__(self, nc, es):
        self.nc = nc
        self.sem = {}
        self.cnt = {}
        self.known = {e: {} for e in self.ENG}
        self.streams = {e: [] for e in self.ENG}
        for e in self.ENG:
            self.sem[e] = es.enter_context(nc.semaphore("s_" + e))
            self.cnt[e] = 0
        self.drr = {}
        for q in ("sp", "pool", "act"):
            self.drr[q] = 0
            for k in range(self.NDS):
                key = (q, k)
                self.sem[key] = es.enter_context(nc.semaphore("d_%s%d" % (q, k)))
                self.cnt[key] = 0

    def _waits(self, eng, deps):
        st = self.streams[eng]
        kn = self.known[eng]
        for key, val in deps.items():
            if val <= 0:
                continue
            if eng == "pe" and key == "pe":
                continue
            if kn.get(key, 0) >= val:
                continue
            kn[key] = val
            st.append(("w", key, val))

    def op(self, eng, fn, rd=(), wr=()):
        deps = {}
        for t in rd:
            _merge(deps, t.w)
        for t in wr:
            _merge(deps, t.w)
            _merge(deps, t.r)
        self._waits(eng, deps)
        self.cnt[eng] += 1
        v = self.cnt[eng]
        self.streams[eng].append(("o", fn, eng))
        for t in rd:
            if t.r.get(eng, 0) < v:
                t.r[eng] = v
        for t in wr:
            t.w = {eng: v}
            t.r = {}

    def dma(self, q, out, in_, rd=(), wr=()):
        k = self.drr[q] % self.NDS
        self.drr[q] += 1
        key = (q, k)
        deps = {key: self.cnt[key]}
        for t in rd:
            _merge(deps, t.w)
        for t in wr:
            _merge(deps, t.w)
            _merge(deps, t.r)
        self._waits(q, deps)
        self.cnt[key] += 16
        v = self.cnt[key]
        self.streams[q].append(("d", out, in_, key))
        for t in rd:
            if t.r.get(key, 0) < v:
                t.r[key] = v
        for t in wr:
            t.w = {key: v}
            t.r = {}

    def barrier(self):
        allc = {k: v for k, v in self.cnt.items() if v > 0}
        for e in self.ENG:
            self._waits(e, dict(allc))

    def flush(self):
        nc = self.nc
        streams = self.streams
        self.streams = {e: [] for e in self.ENG}
        sem = self.sem

        def run(e, items):
            for it in items:
                if it[0] == "w":
                    e.wait_ge(sem[it[1]], it[2])
                elif it[0] == "o":
                    it[1](e).then_inc(sem[it[2]], 1)
                else:
                    e.dma_start(out=it[1], in_=it[2]).then_inc(sem[it[3]], 16)

        with nc.allow_non_contiguous_dma(reason="tiny strided parameter/stat DMAs"), nc.Block() as blk:
            @blk.tensor
            def _(e):
                run(e, streams["pe"])

            @blk.scalar
            def _(e):
                run(e, streams["act"])

            @blk.vector
            def _(e):
                run(e, streams["dve"])

            @blk.gpsimd
            def _(e):
                run(e, streams["pool"])

            @blk.sync
            def _(e):
                run(e, streams["sp"])


class Ctx:
    _uid = [0]

    def __init__(self, nc, P):
        self.nc = nc
        self.P = P
        self.es = ExitStack()
        Ctx._uid[0] += 1
        self.n = Ctx._uid[0] * 1000

    def sb(self, shape, dt, name=None):
        self.n += 1
        return self.es.enter_context(self.nc.sbuf_tensor("%s_%d" % (name or "t", self.n), list(shape), dt))

    def psum_banks(self):
        banks = []
        for i in range(8):
            self.n += 1
            t = self.es.enter_context(self.nc.psum_tensor("ps_%d" % self.n, [128, 512], F32))
            banks.append((t, Tk()))
        return banks

    def close(self):
        self.es.close()


def load_weight_bf16(P, dst, dst_tk, w_ap, kchunks, ncols, col0=0, split=2):
    per = (kchunks + split - 1) // split
    for s in range(0, kchunks, per):
        e = min(kchunks, s + per)
        src = w_ap[s * 128:e * 128, col0:col0 + ncols].rearrange("(k p) n -> p k n", p=128)
        P.dma("pool", dst[:, s:e, :], src, wr=[dst_tk])


def emit_norm_T(P, C, xt, xt_tk, nsub, gb, hn, hn_tk, ss, ss_tk, rstd, rstd_tk, junk, junk_tk,
                hnT, hnT_tks, tp_banks, ident, evac_engs=("act", "dve")):
    for s in range(nsub):
        P.op("act", lambda e, s=s: e.activation(out=junk[:], in_=xt[:, s, :], func=AF.Square,
                                                accum_out=ss[:, s:s + 1]),
             rd=[xt_tk], wr=[junk_tk, ss_tk])
    P.op("dve", lambda e: e.tensor_scalar(out=rstd[:, 0:nsub], in0=ss[:, 0:nsub], scalar1=1.0 / D, scalar2=EPS,
                                          op0=ALU.mult, op1=ALU.add), rd=[ss_tk], wr=[rstd_tk])
    P.op("act", lambda e: e.activation(out=rstd[:, 0:nsub], in_=rstd[:, 0:nsub], func=AF.Sqrt),
         rd=[rstd_tk], wr=[rstd_tk])
    P.op("dve", lambda e: e.reciprocal(out=rstd[:, 0:nsub], in_=rstd[:, 0:nsub]),
         rd=[rstd_tk], wr=[rstd_tk])
    for s in range(nsub):
        P.op("dve", lambda e, s=s: e.scalar_tensor_tensor(out=hn[:, s % 2, :], in0=xt[:, s, :], scalar=rstd[:, s:s + 1],
                                                          in1=gb[:], op0=ALU.mult, op1=ALU.mult),
             rd=[xt_tk, rstd_tk], wr=[hn_tk[s % 2]])
        bank, btk = tp_banks[s % len(tp_banks)]
        pb = bank[:].bitcast(BF16)
        for c in range(NCH):
            P.op("pe", lambda e, s=s, c=c, pb=pb: e.transpose(out=pb[:, c * 128:(c + 1) * 128],
                                                             in_=hn[:, s % 2, c * 128:(c + 1) * 128], identity=ident[:]),
                 rd=[hn_tk[s % 2]], wr=[btk])
        eng = evac_engs[s % len(evac_engs)]
        dst = hnT[:, :, s * 128:(s + 1) * 128]
        srcv = pb.rearrange("p (c t) -> p c t", c=NCH)
        if eng == "act":
            P.op("act", lambda e, dst=dst, srcv=srcv: e.activation(out=dst, in_=srcv, func=AF.Copy),
                 rd=[btk], wr=[hnT_tks[s]])
        else:
            P.op(eng, lambda e, dst=dst, srcv=srcv: e.tensor_copy(out=dst, in_=srcv), rd=[btk], wr=[hnT_tks[s]])


def make_ident(P, C, dt=BF16):
    raise NotImplementedError


def phase_ffn(nc, P, xin, xout, tiles_in, tiles_out, g_ap, wg_ap, wu_ap, wd_ap, ident_ap):
    C = Ctx(nc, P)
    NM = DFF // 128
    wg = C.sb([128, NCH, DFF], BF16, "wg"); wg_tk = Tk()
    wu = C.sb([128, NCH, DFF], BF16, "wu"); wu_tk = Tk()
    wd = C.sb([128, NM, D], BF16, "wd"); wd_tk = Tk()
    gb = C.sb([128, D], F32, "gb"); gb_tk = Tk()
    ident = C.sb([128, 128], BF16, "ident"); ident_tk = Tk()
    xts = [(C.sb([128, 4, D], F32, "xt"), Tk()) for _ in range(2)]
    hn = C.sb([128, 2, D], BF16, "hn"); hn_tk = [Tk() for _ in range(2)]
    hnT = C.sb([128, NCH, 512], BF16, "hnT"); hnT_tks = [Tk() for _ in range(4)]
    hT = C.sb([128, NM, 512], BF16, "hT"); hT_tks = [Tk() for _ in range(NM)]
    junk = C.sb([128, D], BF16, "junk"); junk_tk = Tk()
    sgs = [(C.sb([128, 512], BF16, "sg"), Tk()) for _ in range(2)]
    ss = C.sb([128, 8], F32, "ss"); ss_tk = Tk()
    rstd = C.sb([128, 8], F32, "rstd"); rstd_tk = Tk()
    banks = C.psum_banks()

    P.dma("pool", ident[:], ident_ap, wr=[ident_tk])
    P.dma("sp", gb[:], g_ap.partition_broadcast(128), wr=[gb_tk])
    load_weight_bf16(P, wg, wg_tk, wg_ap, NCH, DFF, split=4)
    load_weight_bf16(P, wu, wu_tk, wu_ap, NCH, DFF, split=4)
    load_weight_bf16(P, wd, wd_tk, wd_ap, NM, D, split=4)

    nt = len(tiles_in)

    def load_x(i):
        r0, ntok = tiles_in[i]
        xt, xt_tk = xts[i % 2]
        ns = ntok // 128
        P.dma("sp", xt[:, 0:ns, :], xin[r0:r0 + ntok, :].rearrange("(s p) d -> p s d", p=128), wr=[xt_tk])

    def norm(i):
        r0, ntok = tiles_in[i]
        xt, xt_tk = xts[i % 2]
        ns = ntok // 128
        emit_norm_T(P, C, xt, xt_tk, ns, gb, hn, hn_tk, ss, ss_tk, rstd, rstd_tk, junk, junk_tk,
                    hnT, hnT_tks, banks[4:6], ident)

    def gate_up(i):
        r0, ntok = tiles_in[i]
        ns = ntok // 128
        for mo in range(NM):
            bg, bg_tk = banks[(2 * mo) % 4]
            bu, bu_tk = banks[(2 * mo + 1) % 4]
            for kc in range(NCH):
                P.op("pe", lambda e, kc=kc, mo=mo, bg=bg: e.matmul(bg[:, 0:ntok], lhsT=wg[:, kc, mo * 128:(mo + 1) * 128],
                                                                  rhs=hnT[:, kc, 0:ntok], start=(kc == 0), stop=(kc == NCH - 1)),
                     rd=[wg_tk] + hnT_tks[0:ns], wr=[bg_tk])
            for kc in range(NCH):
                P.op("pe", lambda e, kc=kc, mo=mo, bu=bu: e.matmul(bu[:, 0:ntok], lhsT=wu[:, kc, mo * 128:(mo + 1) * 128],
                                                                  rhs=hnT[:, kc, 0:ntok], start=(kc == 0), stop=(kc == NCH - 1)),
                     rd=[wu_tk] + hnT_tks[0:ns], wr=[bu_tk])
            sg, sg_tk = sgs[mo % 2]
            P.op("act", lambda e, sg=sg, bg=bg: e.activation(out=sg[:, 0:ntok], in_=bg[:, 0:ntok], func=AF.Silu),
                 rd=[bg_tk], wr=[sg_tk])
            P.op("dve", lambda e, sg=sg, bu=bu, mo=mo: e.tensor_tensor(out=hT[:, mo, 0:ntok], in0=sg[:, 0:ntok],
                                                                       in1=bu[:, 0:ntok], op=ALU.mult),
                 rd=[sg_tk, bu_tk], wr=[hT_tks[mo]])

    def down(i):
        r0, ntok = tiles_in[i]
        ro = tiles_out[i]
        xt, xt_tk = xts[i % 2]
        ns = ntok // 128
        j = 0
        for s in range(ns):
            for nh in range(2):
                bo, bo_tk = banks[6 + (j % 2)]
                j += 1
                for mo in range(NM):
                    P.op("pe", lambda e, s=s, nh=nh, mo=mo, bo=bo: e.matmul(bo[:, :], lhsT=hT[:, mo, s * 128:(s + 1) * 128],
                                                                           rhs=wd[:, mo, nh * 512:(nh + 1) * 512],
                                                                           start=(mo == 0), stop=(mo == NM - 1)),
                         rd=[wd_tk, hT_tks[mo]], wr=[bo_tk])
                P.op("dve", lambda e, s=s, nh=nh, bo=bo, xt=xt: e.tensor_tensor(out=xt[:, s, nh * 512:(nh + 1) * 512],
                                                                               in0=xt[:, s, nh * 512:(nh + 1) * 512],
                                                                               in1=bo[:, :], op=ALU.add),
                     rd=[bo_tk], wr=[xt_tk])
        if ro is not None:
            P.dma("sp", xout[ro:ro + ntok, :].rearrange("(s p) d -> p s d", p=128), xt[:, 0:ns, :], rd=[xt_tk])

    load_x(0)
    if nt > 1:
        load_x(1)
    norm_dep(P, [gb_tk, ident_tk])
    norm(0)
    for i in range(nt):
        gate_up(i)
        if i + 1 < nt:
            norm(i + 1)
        down(i)
        if i + 2 < nt:
            load_x(i + 2)
    P.barrier()
    P.flush()
    C.close()


def norm_dep(P, tks):
    deps = {}
    for t in tks:
        _merge(deps, t.w)
    for e in ("pe", "dve", "act", "pool"):
        P._waits(e, dict(deps))


class Kit:
    def __init__(self, nc, P, C, g_ap, ident_ap, nx=2, nsubmax=4):
        self.P = P
        self.C = C
        self.gb = C.sb([128, D], F32, "gb"); self.gb_tk = Tk()
        self.ident = C.sb([128, 128], BF16, "ident"); self.ident_tk = Tk()
        self.xts = [(C.sb([128, nsubmax, D], F32, "xt"), Tk()) for _ in range(nx)]
        self.hn = C.sb([128, 2, D], BF16, "hn"); self.hn_tk = [Tk(), Tk()]
        self.hnT = C.sb([128, NCH, 128 * nsubmax], BF16, "hnT"); self.hnT_tks = [Tk() for _ in range(nsubmax)]
        self.junk = C.sb([128, D], BF16, "junk"); self.junk_tk = Tk()
        self.ss = C.sb([128, 8], F32, "ss"); self.ss_tk = Tk()
        self.rstd = C.sb([128, 8], F32, "rstd"); self.rstd_tk = Tk()
        self.banks = C.psum_banks()
        P.dma("pool", self.ident[:], ident_ap, wr=[self.ident_tk])
        if g_ap is not None:
            P.dma("sp", self.gb[:], g_ap.partition_broadcast(128), wr=[self.gb_tk])

    def consts_ready(self, extra=()):
        norm_dep(self.P, [self.gb_tk, self.ident_tk] + list(extra))

    def load_x(self, slot, src_rows_ap, ns):
        xt, xt_tk = self.xts[slot]
        self.P.dma("sp", xt[:, 0:ns, :], src_rows_ap.rearrange("(s p) d -> p s d", p=128), wr=[xt_tk])

    def norm(self, slot, ns, tp=(0, 1)):
        xt, xt_tk = self.xts[slot]
        emit_norm_T(self.P, self.C, xt, xt_tk, ns, self.gb, self.hn, self.hn_tk, self.ss, self.ss_tk,
                    self.rstd, self.rstd_tk, self.junk, self.junk_tk, self.hnT, self.hnT_tks,
                    [self.banks[i] for i in tp], self.ident)


def fm_chunk(P, bank, btk, W, wtk, col0, M, hnT, hnT_tks, ns, nk=NCH):
    ntok = ns * 128
    for kc in range(nk):
        P.op("pe", lambda e, kc=kc: e.matmul(bank[0:M, 0:ntok], lhsT=W[:, kc, col0:col0 + M], rhs=hnT[:, kc, 0:ntok],
                                             start=(kc == 0), stop=(kc == nk - 1)),
             rd=[wtk] + list(hnT_tks[0:ns]), wr=[btk])


def tm_block(P, bank, btk, hnT, hnT_tk_s, s, W, wtk, col0, ncols, nk=NCH):
    for kc in range(nk):
        P.op("pe", lambda e, kc=kc: e.matmul(bank[:, 0:ncols], lhsT=hnT[:, kc, s * 128:(s + 1) * 128],
                                             rhs=W[:, kc, col0:col0 + ncols], start=(kc == 0), stop=(kc == nk - 1)),
             rd=[wtk, hnT_tk_s], wr=[btk])


def rsqrt_small(P, t, tk, n, scale):
    P.op("dve", lambda e: e.tensor_scalar(out=t[:, 0:n], in0=t[:, 0:n], scalar1=scale, scalar2=EPS,
                                          op0=ALU.mult, op1=ALU.add), rd=[tk], wr=[tk])
    P.op("act", lambda e: e.activation(out=t[:, 0:n], in_=t[:, 0:n], func=AF.Sqrt), rd=[tk], wr=[tk])
    P.op("dve", lambda e: e.reciprocal(out=t[:, 0:n], in_=t[:, 0:n]), rd=[tk], wr=[tk])


def headnorm_fm(P, K, src_bank, src_tk, ntok, BD, gcol, sq, sq_tk, ssb, ssb_tk, rs, rs_tk, out_ap, out_tk, inv_n):
    P.op("act", lambda e: e.activation(out=sq[:, 0:ntok], in_=src_bank[:, 0:ntok], func=AF.Square),
         rd=[src_tk], wr=[sq_tk])
    P.op("pe", lambda e: e.matmul(ssb[:, 0:ntok], lhsT=BD[:], rhs=sq[:, 0:ntok], start=True, stop=True),
         rd=[sq_tk], wr=[ssb_tk])
    P.op("dve", lambda e: e.tensor_scalar(out=rs[:, 0:ntok], in0=ssb[:, 0:ntok], scalar1=inv_n, scalar2=EPS,
                                          op0=ALU.mult, op1=ALU.add), rd=[ssb_tk], wr=[rs_tk])
    P.op("act", lambda e: e.activation(out=rs[:, 0:ntok], in_=rs[:, 0:ntok], func=AF.Sqrt), rd=[rs_tk], wr=[rs_tk])
    P.op("dve", lambda e: e.reciprocal(out=rs[:, 0:ntok], in_=rs[:, 0:ntok]), rd=[rs_tk], wr=[rs_tk])
    P.op("dve", lambda e: e.scalar_tensor_tensor(out=out_ap, in0=src_bank[:, 0:ntok], scalar=gcol, in1=rs[:, 0:ntok],
                                                 op0=ALU.mult, op1=ALU.mult), rd=[src_tk, rs_tk], wr=[out_tk])


def phase_a(nc, P, S, prm):
    xc = S["xc"]
    C = Ctx(nc, P)
    K = Kit(nc, P, C, prm["g_mix"][0], S["ident"])
    NW = 2568
    W = C.sb([128, NCH, NW], BF16, "win"); W_tk = Tk()
    load_weight_bf16(P, W, W_tk, prm["e_w_in"][0], NCH, NW, split=4)
    BD = C.sb([128, 128], BF16, "bd"); c_tk = Tk()
    P.dma("pool", BD[:], S["bd64"], wr=[c_tk])
    id8 = C.sb([8, 8], F32, "id8")
    P.dma("sp", id8[:], S["identf"][0:8, 0:8], wr=[c_tk])
    gq = C.sb([128, 1], F32, "gq"); gk = C.sb([128, 1], F32, "gk")
    for hh in range(2):
        P.dma("sp", gq[hh * 64:(hh + 1) * 64, :], prm["e_g_qn"][0].rearrange("(p o) -> p o", o=1), wr=[c_tk])
        P.dma("sp", gk[hh * 64:(hh + 1) * 64, :], prm["e_g_kn"][0].rearrange("(p o) -> p o", o=1), wr=[c_tk])
    nbf = C.sb([8, 1], F32, "nbf")
    P.dma("sp", nbf[:], prm["e_b_f"][0].rearrange("(p o) -> p o", o=1), wr=[c_tk])
    gvb = C.sb([128, 512], F32, "gvb")
    P.dma("sp", gvb[:], prm["e_g_v"][0].partition_broadcast(128), wr=[c_tk])
    bsb = C.sb([128, 512], F32, "bsb")
    P.dma("sp", bsb[:], prm["e_b_s"][0].rearrange("g t -> (g t)").partition_broadcast(128), wr=[c_tk])
    wsf = C.sb([128, 4, 128], F32, "wsf")
    P.dma("sp", wsf[:], prm["e_w_s"][0].rearrange("g t s -> t g s"), wr=[c_tk])
    wsb = C.sb([128, 4, 128], BF16, "wsb"); wsT = C.sb([128, 4, 128], BF16, "wsT")
    ones8 = C.sb([8, 512], F32, "ones8")
    carry = C.sb([8, 1], F32, "carry"); carry_tk = Tk()
    K.consts_ready([c_tk])
    P.op("dve", lambda e: e.memset(ones8[:], 1.0), wr=[c_tk])
    P.op("dve", lambda e: e.memset(carry[:], 0.0), wr=[carry_tk])
    P.op("dve", lambda e: e.tensor_scalar(out=gq[:], in0=gq[:], scalar1=0.125, scalar2=None, op0=ALU.mult), wr=[c_tk])
    P.op("dve", lambda e: e.tensor_scalar(out=nbf[:], in0=nbf[:], scalar1=-1.0, scalar2=None, op0=ALU.mult), wr=[c_tk])
    P.op("dve", lambda e: e.memset(wsf[0:64, :, 64:128], 0.0), wr=[c_tk])
    P.op("dve", lambda e: e.tensor_copy(out=wsb[:], in_=wsf[:]), wr=[c_tk])
    b0, b0tk = K.banks[0]
    pb = b0[:].bitcast(BF16)
    for g in range(4):
        P.op("pe", lambda e, g=g: e.transpose(out=pb[:, g * 128:(g + 1) * 128], in_=wsb[:, g, :], identity=K.ident[:]),
             rd=[c_tk], wr=[b0tk])
    P.op("dve", lambda e: e.tensor_copy(out=wsT[:].rearrange("p g t -> p (g t)"), in_=pb[:, 0:512]), rd=[b0tk], wr=[c_tk])
    norm_dep(P, [c_tk])

    sq = C.sb([128, 512], BF16, "sq"); sq_tk = Tk()
    rs = C.sb([128, 512], F32, "rs"); rs_tk = Tk()
    kn = [(C.sb([128, 512], BF16, "kn"), Tk()) for _ in range(2)]
    vtm = [(C.sb([128, 512], BF16, "vtm"), Tk()) for _ in range(2)]
    fe = C.sb([8, 512], F32, "fe"); fe_tk = Tk()
    negc = C.sb([8, 512], F32, "negc"); negc_tk = Tk()
    nk = C.sb([128, 4, 8], F32, "nk"); nk_tk = Tk()
    uT = C.sb([128, 4, 512], BF16, "uT"); uT_tk = [Tk() for _ in range(4)]
    vg = C.sb([128, 512], F32, "vg"); vg_tk = Tk()
    vn = C.sb([128, 512], BF16, "vn"); vn_tk = Tk()
    ssg = C.sb([128, 4], F32, "ssg"); ssg_tk = Tk()
    t1 = C.sb([128, 512], F32, "t1"); t1_tk = Tk()
    yaT = C.sb([128, 4, 512], BF16, "yaT"); ya_tk = Tk()
    banks = K.banks
    rr = [0]

    def fmbank():
        rr[0] += 1
        return banks[2 + rr[0] % 2]

    tiles = S["tiles_a"]
    kt_of = lambda ctx0: ctx0 // 128
    def tile_body(i, c0, ntok, full):
        ns = ntok // 128
        slot = i % 2
        K.load_x(slot, xc[c0:c0 + ntok, :], ns)
        K.norm(slot, ns)
        hnT, hts = K.hnT, K.hnT_tks
        for c in range(4):
            b, btk = fmbank()
            fm_chunk(P, b, btk, W, W_tk, 1536 + c * 128, 128, hnT, hts, ns)
            o, otk = kn[c % 2]
            headnorm_fm(P, K, b, btk, ntok, BD, gk[:, 0:1], sq, sq_tk, banks[4][0], banks[4][1], rs, rs_tk,
                        o[:, 0:ntok], otk, 1.0 / 64)
            P.dma("sp", S["KT"][c * 128:(c + 1) * 128, c0:c0 + ntok], o[:, 0:ntok], rd=[otk])
        if full:
            o0 = c0 - OWN0
            for c in range(4):
                b, btk = fmbank()
                fm_chunk(P, b, btk, W, W_tk, 1024 + c * 128, 128, hnT, hts, ns)
                o, otk = kn[c % 2]
                headnorm_fm(P, K, b, btk, ntok, BD, gq[:, 0:1], sq, sq_tk, banks[4][0], banks[4][1], rs, rs_tk,
                            o[:, 0:ntok], otk, 1.0 / 64)
                P.dma("sp", S["QT"][c * 128:(c + 1) * 128, o0:o0 + ntok], o[:, 0:ntok], rd=[otk])
        for s in range(ns):
            b, btk = banks[5]
            tm_block(P, b, btk, hnT, hts[s], s, W, W_tk, 2048, 512)
            o, otk = vtm[s % 2]
            P.op("act", lambda e, o=o, b=b: e.activation(out=o[:], in_=b[:], func=AF.Copy), rd=[btk], wr=[otk])
            P.dma("sp", S["VV"][c0 + s * 128:c0 + (s + 1) * 128, :], o[:], rd=[otk])
        b, btk = banks[6]
        fm_chunk(P, b, btk, W, W_tk, 2560, 8, hnT, hts, ns)
        P.op("act", lambda e, b=b: e.activation(out=fe[:, 0:ntok], in_=b[0:8, 0:ntok], func=AF.Exp, scale=-1.0,
                                                bias=nbf[:, 0:1]), rd=[btk], wr=[fe_tk])
        P.op("act", lambda e: e.activation(out=fe[:, 0:ntok], in_=fe[:, 0:ntok], func=AF.Ln, bias=1.0),
             rd=[fe_tk], wr=[fe_tk])
        P.op("dve", lambda e: e.tensor_tensor_scan(out=negc[:, 0:ntok], data0=ones8[:, 0:ntok], data1=fe[:, 0:ntok],
                                                   initial=carry[:, 0:1], op0=ALU.mult, op1=ALU.add),
             rd=[fe_tk, carry_tk], wr=[negc_tk])
        P.op("dve", lambda e: e.tensor_copy(out=carry[:], in_=negc[:, ntok - 1:ntok]), rd=[negc_tk], wr=[carry_tk])
        P.dma("sp", S["NR"][:, c0 // 128:c0 // 128 + ns], negc[:, 64:ntok:128], rd=[negc_tk])
        b, btk = banks[7]
        for s in range(ns):
            P.op("pe", lambda e, s=s, b=b: e.matmul(b[:, s * 8:(s + 1) * 8], lhsT=negc[:, s * 128:(s + 1) * 128],
                                                    rhs=id8[:], start=True, stop=True), rd=[negc_tk], wr=[btk])
        P.op("dve", lambda e, b=b: e.tensor_copy(out=nk[:, 0:ns, :].rearrange("p s h -> p (s h)"), in_=b[:, 0:ns * 8]),
             rd=[btk], wr=[nk_tk])
        P.dma("sp", S["NK"][:, kt_of(c0):kt_of(c0) + ns, :], nk[:, 0:ns, :], rd=[nk_tk])
        if not full:
            return
        for c in range(4):
            b, btk = fmbank()
            fm_chunk(P, b, btk, W, W_tk, c * 128, 128, hnT, hts, ns)
            P.op("act", lambda e, c=c, b=b: e.activation(out=uT[:, c, 0:ntok], in_=b[:, 0:ntok], func=AF.Gelu),
                 rd=[btk], wr=[uT_tk[c]])
        for s in range(ns):
            b, btk = banks[5]
            tm_block(P, b, btk, hnT, hts[s], s, W, W_tk, 512, 512)
            P.op("act", lambda e, b=b: e.activation(out=vg[:], in_=b[:], func=AF.Gelu), rd=[btk], wr=[vg_tk])
            for g in range(4):
                P.op("act", lambda e, g=g: e.activation(out=K.junk[:, 0:128], in_=vg[:, g * 128:(g + 1) * 128],
                                                        func=AF.Square, accum_out=ssg[:, g:g + 1]),
                     rd=[vg_tk], wr=[K.junk_tk, ssg_tk])
            rsqrt_small(P, ssg, ssg_tk, 4, 1.0 / 128)
            for g in range(4):
                P.op("dve", lambda e, g=g: e.scalar_tensor_tensor(out=vn[:, g * 128:(g + 1) * 128],
                                                                  in0=vg[:, g * 128:(g + 1) * 128], scalar=ssg[:, g:g + 1],
                                                                  in1=gvb[:, g * 128:(g + 1) * 128], op0=ALU.mult, op1=ALU.mult),
                     rd=[vg_tk, ssg_tk], wr=[vn_tk])
            b2, b2tk = banks[6]
            for g in range(4):
                P.op("pe", lambda e, g=g, b2=b2: e.matmul(b2[:, g * 128:(g + 1) * 128], lhsT=vn[:, g * 128:(g + 1) * 128],
                                                          rhs=wsT[:, g, :], start=True, stop=True), rd=[vn_tk], wr=[b2tk])
            P.op("dve", lambda e, b2=b2: e.tensor_tensor(out=t1[:], in0=b2[:], in1=bsb[:], op=ALU.add),
                 rd=[b2tk], wr=[t1_tk])
            P.op("dve", lambda e, s=s: e.tensor_tensor(out=yaT[:, :, s * 128:(s + 1) * 128],
                                                       in0=t1[:].rearrange("p (g t) -> p g t", g=4),
                                                       in1=uT[:, :, s * 128:(s + 1) * 128], op=ALU.mult),
                 rd=[t1_tk] + uT_tk, wr=[ya_tk])
        o0 = c0 - OWN0
        P.dma("sp", S["YT"][0:512, o0:o0 + ntok].rearrange("(g p) t -> p g t", p=128), yaT[:, :, 0:ntok], rd=[ya_tk])

    for i, (c0, ntok, full) in enumerate(tiles):
        tile_body(i, c0, ntok, full)
    P.barrier()
    P.flush()
    C.close()


def phase_b(nc, P, S):
    C = Ctx(nc, P)
    banks = C.psum_banks()
    c_tk = Tk()
    tri = C.sb([128, 128], BF16, "tri")
    P.dma("pool", tri[:], S["tri"], wr=[c_tk])
    NKm = C.sb([128, 64, 8], F32, "nkm")
    P.dma("sp", NKm[:], S["NK"], wr=[c_tk])
    cm = C.sb([128, 64], F32, "cm")
    P.dma("sp", cm[:], S["cmask"], wr=[c_tk])
    Rb = C.sb([128, 8, 64], F32, "rb")
    P.dma("sp", Rb[:].rearrange("p h k -> p (h k)"), S["NR"].rearrange("h k -> (h k)").partition_broadcast(128), wr=[c_tk])
    onesf = C.sb([128, 128], F32, "onesf")
    norm_dep(P, [c_tk])
    P.op("dve", lambda e: e.memset(onesf[:], 1.0), wr=[c_tk])
    for h in range(8):
        P.op("dve", lambda e, h=h: e.tensor_tensor(out=NKm[:, :, h], in0=NKm[:, :, h], in1=cm[:], op=ALU.add), wr=[c_tk])
    norm_dep(P, [c_tk])
    KA = [(C.sb([128, SEQ], BF16, "ka"), Tk()) for _ in range(2)]
    QA = [(C.sb([128, NOWN], BF16, "qa"), Tk()) for _ in range(2)]
    VA = [(C.sb([128, 64, 128], BF16, "va"), Tk()) for _ in range(2)]
    nrow = [(C.sb([128, 64], F32, "nrow"), Tk()) for _ in range(2)]
    dif = C.sb([128, 32], F32, "dif"); dif_tk = Tk()
    bias = [(C.sb([128, 64], F32, "bias"), Tk()) for _ in range(2)]
    pbuf = [(C.sb([128, 512], BF16, "pb"), Tk()) for _ in range(4)]
    osb = [(C.sb([128, 512], F32, "osb"), Tk()) for _ in range(2)]
    ybT = [(C.sb([128, 512], BF16, "ybT"), Tk()) for _ in range(2)]
    KT0 = OWN0 // 128
    groups = [(0, 128, KT0)] + [(128 + 512 * J, 512, KT0 + 1 + 4 * J) for J in range(8)]

    def load_head(h):
        ka, ka_tk = KA[h % 2]; qa, qa_tk = QA[h % 2]; va, va_tk = VA[h % 2]; nr, nr_tk = nrow[h % 2]
        for half in range(2):
            P.dma("sp", ka[0:64, half * 4096:(half + 1) * 4096], S["KT"][h * 64:(h + 1) * 64, half * 4096:(half + 1) * 4096],
                  wr=[ka_tk])
        P.op("pool", lambda e: e.memset(ka[64:65, :], 1.0), wr=[ka_tk])
        P.dma("sp", qa[0:64, :], S["QT"][h * 64:(h + 1) * 64, :], wr=[qa_tk])
        P.dma("sp", nr[64:65, :], S["NR"][h:h + 1, :], wr=[nr_tk])
        vc = 0 if h % 2 == 0 else 64
        oc = 64 - vc
        for q4 in range(4):
            P.dma("sp", va[:, q4 * 16:(q4 + 1) * 16, vc:vc + 64],
                  S["VV"][q4 * 2048:(q4 + 1) * 2048, h * 64:(h + 1) * 64].rearrange("(k p) d -> p k d", p=128), wr=[va_tk])
        P.op("pool", lambda e: e.memset(va[:, :, oc:oc + 64], 1.0), wr=[va_tk])
        P.op("dve", lambda e: e.tensor_tensor(out=dif[64:65, 0:32].rearrange("p (a b) -> p a b", b=4),
                                              in0=nr[64:65, KT0 + 3:64:4].unsqueeze(2).to_broadcast([1, 8, 4]),
                                              in1=nr[64:65, KT0 + 1:64].rearrange("p (a b) -> p a b", b=4), op=ALU.subtract),
             rd=[nr_tk], wr=[dif_tk])
        P.op("dve", lambda e: e.memset(qa[64:65, 0:128], 0.0), wr=[qa_tk])
        P.op("dve", lambda e: e.tensor_copy(out=qa[64:65, 128:NOWN].rearrange("p (a b) -> p a b", b=128),
                                            in_=dif[64:65, 0:32].unsqueeze(2).to_broadcast([1, 32, 128])),
             rd=[dif_tk], wr=[qa_tk])

    itc = [0]

    def group_block(h, gi, o0, ntok, kt_first):
        ka, ka_tk = KA[h % 2]; qa, qa_tk = QA[h % 2]; va, va_tk = VA[h % 2]
        nsub = ntok // 128
        nkt = kt_first + nsub
        kt_ref = kt_first + (2 if nsub == 4 else 0)
        it = itc[0]; itc[0] += 1
        bs, bs_tk = bias[it % 2]
        P.op("dve", lambda e: e.tensor_scalar(out=bs[:, 0:nkt], in0=NKm[:, 0:nkt, h], scalar1=Rb[:, h, kt_ref:kt_ref + 1],
                                              scalar2=None, op0=ALU.subtract), wr=[bs_tk])
        acc, acc_tk = banks[4 + it % 2]

        def c0_of(kt):
            r = kt - kt_first
            return 0 if r <= 0 else r * 128

        def S_(kt):
            sb_, sb_tk = banks[kt % 4]
            c0 = c0_of(kt)
            P.op("pe", lambda e: e.matmul(sb_[:, c0:ntok], lhsT=ka[0:65, kt * 128:(kt + 1) * 128],
                                          rhs=qa[0:65, o0 + c0:o0 + ntok], start=True, stop=True),
                 rd=[ka_tk, qa_tk], wr=[sb_tk])
            pb_, pb_tk = pbuf[kt % 4]
            P.op("act", lambda e: e.activation(out=pb_[:, c0:ntok], in_=sb_[:, c0:ntok], func=AF.Exp, bias=bs[:, kt:kt + 1]),
                 rd=[sb_tk, bs_tk], wr=[pb_tk])
            if kt >= kt_first:
                P.op("pool", lambda e: e.tensor_tensor(out=pb_[:, c0:c0 + 128], in0=pb_[:, c0:c0 + 128], in1=tri[:], op=ALU.mult),
                     rd=[pb_tk], wr=[pb_tk])

        def V_(kt):
            pb_, pb_tk = pbuf[kt % 4]
            c0 = c0_of(kt)
            P.op("pe", lambda e: e.matmul(acc[:, c0:ntok], lhsT=va[:, kt, :], rhs=pb_[:, c0:ntok],
                                          start=(kt == 0), stop=(kt == nkt - 1)),
                 rd=[pb_tk, va_tk], wr=[acc_tk])

        S_(0)
        S_(1)
        for kt in range(nkt):
            if kt + 2 < nkt:
                S_(kt + 2)
            V_(kt)
        ob, ob_tk = osb[it % 2]
        drow = 64 if h % 2 == 0 else 0
        nrow0 = 0 if h % 2 == 0 else 64
        P.op("act", lambda e: e.activation(out=ob[:, 0:ntok], in_=acc[:, 0:ntok], func=AF.Copy), rd=[acc_tk], wr=[ob_tk])
        P.op("dve", lambda e: e.tensor_scalar(out=ob[drow:drow + 1, 0:ntok], in0=ob[drow:drow + 1, 0:ntok], scalar1=1e-30,
                                              scalar2=None, op0=ALU.add), rd=[ob_tk], wr=[ob_tk])
        P.op("dve", lambda e: e.reciprocal(out=ob[drow:drow + 1, 0:ntok], in_=ob[drow:drow + 1, 0:ntok]), rd=[ob_tk], wr=[ob_tk])
        bc, bc_tk = banks[6 + it % 2]
        P.op("pe", lambda e: e.matmul(bc[:, 0:ntok], lhsT=onesf[drow:drow + 1, :], rhs=ob[drow:drow + 1, 0:ntok],
                                      start=True, stop=True), rd=[ob_tk], wr=[bc_tk])
        yo, yo_tk = ybT[gi % 2]
        P.op("dve", lambda e: e.tensor_tensor(out=yo[nrow0:nrow0 + 64, 0:ntok], in0=ob[nrow0:nrow0 + 64, 0:ntok],
                                              in1=bc[nrow0:nrow0 + 64, 0:ntok], op=ALU.mult), rd=[ob_tk, bc_tk], wr=[yo_tk])
        P.dma("sp", S["YT"][512 + h * 64:512 + (h + 1) * 64, o0:o0 + ntok], yo[nrow0:nrow0 + 64, 0:ntok], rd=[yo_tk])

    load_head(0)
    for h in range(8):
        if h + 1 < 8:
            load_head(h + 1)
        for gi, (o0, ntok, ktf) in enumerate(groups):
            group_block(h, gi, o0, ntok, ktf)
    P.barrier()
    P.flush()
    C.close()


def phase_cx(nc, P, S, prm, layer, w_out_ap, xsrc, tiles):
    C = Ctx(nc, P)
    K = Kit(nc, P, C, prm["g_xa"][layer], S["ident"])
    banks = K.banks
    c_tk = Tk()
    gbm = C.sb([128, D], F32, "gbm")
    P.dma("sp", gbm[:], prm["g_mem"][layer].partition_broadcast(128), wr=[c_tk])
    Wo = C.sb([128, NCH, D], BF16, "wo"); Wq = C.sb([128, NCH, 512], BF16, "wq")
    Wkv = C.sb([128, NCH, D], BF16, "wkv"); Wxo = C.sb([128, 4, D], BF16, "wxo")
    load_weight_bf16(P, Wo, c_tk, w_out_ap, NCH, D)
    load_weight_bf16(P, Wq, c_tk, prm["xa_wq"][layer], NCH, 512)
    load_weight_bf16(P, Wkv, c_tk, prm["xa_wkv"][layer], NCH, D)
    load_weight_bf16(P, Wxo, c_tk, prm["xa_wo"][layer], 4, D)
    ones = C.sb([128, 128], BF16, "ones")
    gqc = C.sb([128, 1], F32, "gqc")
    P.dma("sp", gqc[:], prm["xa_gq"][layer].rearrange("(p o) -> p o", o=1), wr=[c_tk])
    gkb = C.sb([128, 128], F32, "gkb")
    P.dma("sp", gkb[:], prm["xa_gk"][layer].partition_broadcast(128), wr=[c_tk])
    K.consts_ready([c_tk])
    P.op("dve", lambda e: e.memset(ones[:], 1.0), wr=[c_tk])
    P.op("dve", lambda e: e.tensor_scalar(out=gqc[:], in0=gqc[:], scalar1=float(128 ** -0.5), scalar2=None, op0=ALU.mult),
         wr=[c_tk])
    norm_dep(P, [c_tk])
    kT = C.sb([128, 4, 256], BF16, "kT"); vm = C.sb([128, 2, 512], BF16, "vm"); m_tk = Tk()
    ssk = C.sb([128, 4], F32, "ssk"); ssk_tk = Tk()
    kn = C.sb([128, 512], BF16, "kn"); kn_tk = Tk()
    K.load_x(0, S["mem"], 2)
    xt, xt_tk = K.xts[0]
    emit_norm_T(P, C, xt, xt_tk, 2, gbm, K.hn, K.hn_tk, K.ss, K.ss_tk, K.rstd, K.rstd_tk, K.junk, K.junk_tk,
                K.hnT, K.hnT_tks, [banks[0], banks[1]], K.ident)

    def mem_sub(s):
        b, btk = banks[2]
        tm_block(P, b, btk, K.hnT, K.hnT_tks[s], s, Wkv, c_tk, 0, 512)
        for h in range(4):
            P.op("act", lambda e, h=h: e.activation(out=K.junk[:, 0:128], in_=b[:, h * 128:(h + 1) * 128], func=AF.Square,
                                                    accum_out=ssk[:, h:h + 1]), rd=[btk], wr=[K.junk_tk, ssk_tk])
        rsqrt_small(P, ssk, ssk_tk, 4, 1.0 / 128)
        for h in range(4):
            P.op("dve", lambda e, h=h: e.scalar_tensor_tensor(out=kn[:, h * 128:(h + 1) * 128], in0=b[:, h * 128:(h + 1) * 128],
                                                              scalar=ssk[:, h:h + 1], in1=gkb[:], op0=ALU.mult, op1=ALU.mult),
                 rd=[btk, ssk_tk], wr=[kn_tk])
        b3, b3tk = banks[3]
        pb = b3[:].bitcast(BF16)
        for h in range(4):
            P.op("pe", lambda e, h=h: e.transpose(out=pb[:, h * 128:(h + 1) * 128], in_=kn[:, h * 128:(h + 1) * 128],
                                                  identity=K.ident[:]), rd=[kn_tk], wr=[b3tk])
        P.op("dve", lambda e: e.tensor_copy(out=kT[:, :, s * 128:(s + 1) * 128],
                                            in_=pb[:, 0:512].rearrange("p (h m) -> p h m", h=4)), rd=[b3tk], wr=[m_tk])
        b4, b4tk = banks[4]
        tm_block(P, b4, b4tk, K.hnT, K.hnT_tks[s], s, Wkv, c_tk, 512, 512)
        P.op("act", lambda e: e.activation(out=vm[:, s, :], in_=b4[:], func=AF.Copy), rd=[b4tk], wr=[m_tk])

    mem_sub(0)
    mem_sub(1)
    norm_dep(P, [m_tk])
    yT = C.sb([128, NCH, 512], BF16, "yT"); yT_tk = Tk()
    sq = C.sb([128, 512], BF16, "sq"); sq_tk = Tk()
    rs = C.sb([128, 512], F32, "rs"); rs_tk = Tk()
    qn = C.sb([128, 512], BF16, "qn"); qn_tk = Tk()
    pm = [(C.sb([128, 512], BF16, "pm"), Tk()) for _ in range(2)]
    rdn = C.sb([128, 512], F32, "rdn"); rdn_tk = Tk()
    oTn = C.sb([128, 4, 512], BF16, "oTn"); oTn_tk = Tk()

    def tile_body(i, r0, o0, ntok):
        ns = ntok // 128
        slot = 1 - (i % 2) if False else (i % 2)
        xt, xt_tk = K.xts[slot]
        K.load_x(slot, xsrc[r0:r0 + ntok, :], ns)
        P.dma("sp", yT[:, :, 0:ntok], S["YT"][:, o0:o0 + ntok].rearrange("(c p) t -> p c t", p=128), wr=[yT_tk])
        jj = 0
        for s in range(ns):
            for nh in range(2):
                b, btk = banks[6 + jj % 2]; jj += 1
                for kc in range(NCH):
                    P.op("pe", lambda e, kc=kc, s=s, nh=nh, b=b: e.matmul(b[:, :], lhsT=yT[:, kc, s * 128:(s + 1) * 128],
                                                                         rhs=Wo[:, kc, nh * 512:(nh + 1) * 512],
                                                                         start=(kc == 0), stop=(kc == NCH - 1)),
                         rd=[yT_tk], wr=[btk])
                P.op("dve", lambda e, s=s, nh=nh, b=b: e.tensor_tensor(out=xt[:, s, nh * 512:(nh + 1) * 512],
                                                                      in0=xt[:, s, nh * 512:(nh + 1) * 512], in1=b[:, :], op=ALU.add),
                     rd=[btk], wr=[xt_tk])
        K.norm(slot, ns)
        for h in range(4):
            b, btk = banks[2 + h % 2]
            fm_chunk(P, b, btk, Wq, c_tk, h * 128, 128, K.hnT, K.hnT_tks, ns)
            headnorm_fm(P, K, b, btk, ntok, ones, gqc[:, 0:1], sq, sq_tk, banks[4][0], banks[4][1], rs, rs_tk,
                        qn[:, 0:ntok], qn_tk, 1.0 / 128)
            for mt in range(2):
                sb_, sb_tk = banks[2 + mt]
                P.op("pe", lambda e, h=h, mt=mt, sb_=sb_: e.matmul(sb_[:, 0:ntok], lhsT=kT[:, h, mt * 128:(mt + 1) * 128],
                                                                  rhs=qn[:, 0:ntok], start=True, stop=True),
                     rd=[qn_tk], wr=[sb_tk])
                p_, p_tk = pm[mt]
                P.op("act", lambda e, sb_=sb_, p_=p_: e.activation(out=p_[:, 0:ntok], in_=sb_[:, 0:ntok], func=AF.Exp),
                     rd=[sb_tk], wr=[p_tk])
            bo, bo_tk = banks[4]
            bd, bd_tk = banks[5]
            for mt in range(2):
                p_, p_tk = pm[mt]
                P.op("pe", lambda e, h=h, mt=mt, p_=p_: e.matmul(bo[:, 0:ntok], lhsT=vm[:, mt, h * 128:(h + 1) * 128],
                                                                rhs=p_[:, 0:ntok], start=(mt == 0), stop=(mt == 1)),
                     rd=[p_tk], wr=[bo_tk])
            for mt in range(2):
                p_, p_tk = pm[mt]
                P.op("pe", lambda e, mt=mt, p_=p_: e.matmul(bd[:, 0:ntok], lhsT=ones[:], rhs=p_[:, 0:ntok],
                                                           start=(mt == 0), stop=(mt == 1)), rd=[p_tk], wr=[bd_tk])
            P.op("dve", lambda e: e.reciprocal(out=rdn[:, 0:ntok], in_=bd[:, 0:ntok]), rd=[bd_tk], wr=[rdn_tk])
            P.op("dve", lambda e, h=h: e.tensor_tensor(out=oTn[:, h, 0:ntok], in0=bo[:, 0:ntok], in1=rdn[:, 0:ntok], op=ALU.mult),
                 rd=[bo_tk, rdn_tk], wr=[oTn_tk])
        for s in range(ns):
            for nh in range(2):
                b, btk = banks[6 + jj % 2]; jj += 1
                for h in range(4):
                    P.op("pe", lambda e, h=h, s=s, nh=nh, b=b: e.matmul(b[:, :], lhsT=oTn[:, h, s * 128:(s + 1) * 128],
                                                                       rhs=Wxo[:, h, nh * 512:(nh + 1) * 512],
                                                                       start=(h == 0), stop=(h == 3)),
                         rd=[oTn_tk], wr=[btk])
                P.op("dve", lambda e, s=s, nh=nh, b=b: e.tensor_tensor(out=xt[:, s, nh * 512:(nh + 1) * 512],
                                                                      in0=xt[:, s, nh * 512:(nh + 1) * 512], in1=b[:, :], op=ALU.add),
                     rd=[btk], wr=[xt_tk])
        P.dma("sp", S["X1"][o0:o0 + ntok, :].rearrange("(s p) d -> p s d", p=128), xt[:, 0:ns, :], rd=[xt_tk])

    for i, (r0, o0, ntok) in enumerate(tiles):
        tile_body(i, r0, o0, ntok)
    P.barrier()
    P.flush()
    C.close()


def phase_d(nc, P, S, prm, tiles):
    C = Ctx(nc, P)
    K = Kit(nc, P, C, prm["g_mix"][1], S["ident"], nx=2)
    banks = K.banks
    c_tk = Tk()
    W = C.sb([128, NCH, 2048], BF16, "win1")
    load_weight_bf16(P, W, c_tk, prm["o_w_in"][0], NCH, 2048, split=4)
    Wp = C.sb([128, 4, 128], BF16, "wp")
    P.dma("pool", Wp[:], prm["o_w_pool"][0].rearrange("g c d -> c g d"), wr=[c_tk])
    spc = C.sb([128, 4], F32, "spc")
    cw = C.sb([128, 4, 3], F32, "cw")
    for g in range(4):
        P.dma("sp", spc[:, g:g + 1], prm["o_s_pool"][0][g * 128:(g + 1) * 128].rearrange("(p o) -> p o", o=1), wr=[c_tk])
        for k in range(3):
            P.dma("sp", cw[:, g, k:k + 1], prm["o_conv_w"][0][k, g * 128:(g + 1) * 128].rearrange("(p o) -> p o", o=1), wr=[c_tk])
    icf = C.sb([128, 4, 512], F32, "icf")
    P.dma("sp", icf[:].rearrange("p g t -> p (g t)"), S["icnt"].rearrange("g t -> (g t)").partition_broadcast(128), wr=[c_tk])
    hfl = C.sb([128, 1], F32, "hfl")
    P.dma("sp", hfl[:], S["hflag"].partition_broadcast(128), wr=[c_tk])
    K.consts_ready([c_tk])
    L = 16 + 512
    zext = C.sb([128, 4, L], F32, "zext"); z_tk = [Tk() for _ in range(4)]
    bA = C.sb([128, L], F32, "bA"); bB = C.sb([128, L], F32, "bB"); bC = C.sb([128, L], F32, "bC"); s_tk = Tk()
    pT = C.sb([128, 512], BF16, "pT"); pT_tk = Tk()
    tmp = C.sb([128, 512], F32, "tmp"); tmp_tk = Tk()
    xg = C.sb([128, 4, 2 + 512], F32, "xg"); xg_tk = [Tk() for _ in range(4)]
    gcs = C.sb([128, 512], F32, "gcs"); gcs_tk = Tk()
    acc = C.sb([128, 512], F32, "acc"); acc_tk = Tk()
    yD = C.sb([128, 8, 512], BF16, "yD"); yD_tk = Tk()
    for g in range(4):
        P.op("dve", lambda e, g=g: e.memset(zext[:, g, :], 0.0), wr=[z_tk[g]])
        P.op("dve", lambda e, g=g: e.memset(xg[:, g, :], 0.0), wr=[xg_tk[g]])
    WIN = (2, 4, 8, 16)

    def tile_body(i, o0, ntok):
        ns = ntok // 128
        Lt = 16 + ntok
        slot = i % 2
        K.load_x(slot, S["X1"][o0:o0 + ntok, :], ns)
        K.norm(slot, ns)
        hnT, hts = K.hnT, K.hnT_tks
        for g in range(4):
            b, btk = banks[2 + g % 2]
            fm_chunk(P, b, btk, W, c_tk, g * 128, 128, hnT, hts, ns)
            P.op("act", lambda e, g=g, b=b: e.activation(out=zext[:, g, 16:Lt], in_=b[:, 0:ntok], func=AF.Copy),
                 rd=[btk], wr=[z_tk[g]])
            if i > 0:
                z = zext[:, g, :]
                P.op("dve", lambda e, z=z: e.tensor_tensor(out=bA[:, 1:Lt], in0=z[:, 1:Lt], in1=z[:, 0:Lt - 1], op=ALU.add),
                     rd=[z_tk[g]], wr=[s_tk])
                sw = bA
                if g >= 1:
                    P.op("dve", lambda e: e.tensor_tensor(out=bB[:, 3:Lt], in0=bA[:, 3:Lt], in1=bA[:, 1:Lt - 2], op=ALU.add),
                         rd=[s_tk], wr=[s_tk])
                    sw = bB
                if g >= 2:
                    P.op("dve", lambda e: e.tensor_tensor(out=bC[:, 7:Lt], in0=bB[:, 7:Lt], in1=bB[:, 3:Lt - 4], op=ALU.add),
                         rd=[s_tk], wr=[s_tk])
                    sw = bC
                if g >= 3:
                    P.op("dve", lambda e: e.tensor_tensor(out=bA[:, 15:Lt], in0=bC[:, 15:Lt], in1=bC[:, 7:Lt - 8], op=ALU.add),
                         rd=[s_tk], wr=[s_tk])
                    sw = bA
                if i == 1:
                    P.op("dve", lambda e, g=g, sw=sw: e.tensor_tensor(out=tmp[:, 0:ntok], in0=sw[:, 16:Lt], in1=icf[:, g, 0:ntok],
                                                                      op=ALU.mult), rd=[s_tk], wr=[tmp_tk])
                    P.op("dve", lambda e, z=z: e.tensor_tensor(out=pT[:, 0:ntok], in0=tmp[:, 0:ntok], in1=z[:, 16:Lt],
                                                               op=ALU.subtract), rd=[tmp_tk, z_tk[g]], wr=[pT_tk])
                else:
                    P.op("dve", lambda e, g=g, sw=sw, z=z: e.scalar_tensor_tensor(out=pT[:, 0:ntok], in0=sw[:, 16:Lt],
                                                                                  scalar=1.0 / WIN[g], in1=z[:, 16:Lt],
                                                                                  op0=ALU.mult, op1=ALU.subtract),
                         rd=[s_tk, z_tk[g]], wr=[pT_tk])
                b2, b2tk = banks[4 + g % 2]
                P.op("pe", lambda e, g=g, b2=b2: e.matmul(b2[:, 0:ntok], lhsT=Wp[:, g, :], rhs=pT[:, 0:ntok], start=True, stop=True),
                     rd=[pT_tk], wr=[b2tk])
                P.op("dve", lambda e, g=g, b2=b2: e.tensor_scalar(out=yD[:, g, 0:ntok], in0=b2[:, 0:ntok], scalar1=spc[:, g:g + 1],
                                                                  scalar2=None, op0=ALU.mult), rd=[b2tk], wr=[yD_tk])
            if i == 0:
                P.op("dve", lambda e, g=g: e.tensor_scalar(out=zext[:, g, 0:16], in0=zext[:, g, Lt - 16:Lt], scalar1=hfl[:, 0:1],
                                                           scalar2=None, op0=ALU.mult), rd=[z_tk[g]], wr=[z_tk[g]])
            else:
                P.op("dve", lambda e, g=g: e.tensor_copy(out=zext[:, g, 0:16], in_=zext[:, g, Lt - 16:Lt]),
                     rd=[z_tk[g]], wr=[z_tk[g]])
        for c in range(4):
            b, btk = banks[2]
            fm_chunk(P, b, btk, W, c_tk, 1536 + c * 128, 128, hnT, hts, ns)
            P.op("act", lambda e, b=b: e.activation(out=gcs[:, 0:ntok], in_=b[:, 0:ntok], func=AF.Copy), rd=[btk], wr=[gcs_tk])
            b1, b1tk = banks[3]
            fm_chunk(P, b1, b1tk, W, c_tk, 512 + c * 128, 128, hnT, hts, ns)
            P.op("dve", lambda e, c=c, b1=b1: e.tensor_tensor(out=xg[:, c, 2:2 + ntok], in0=gcs[:, 0:ntok], in1=b1[:, 0:ntok],
                                                              op=ALU.mult), rd=[gcs_tk, b1tk], wr=[xg_tk[c]])
            if i > 0:
                P.op("dve", lambda e, c=c: e.tensor_scalar(out=acc[:, 0:ntok], in0=xg[:, c, 2:2 + ntok], scalar1=cw[:, c, 2:3],
                                                           scalar2=None, op0=ALU.mult), rd=[xg_tk[c]], wr=[acc_tk])
                for k in (1, 0):
                    P.op("dve", lambda e, c=c, k=k: e.scalar_tensor_tensor(out=acc[:, 0:ntok], in0=xg[:, c, k:k + ntok],
                                                                           scalar=cw[:, c, k:k + 1], in1=acc[:, 0:ntok],
                                                                           op0=ALU.mult, op1=ALU.add),
                         rd=[xg_tk[c], acc_tk], wr=[acc_tk])
                b2, b2tk = banks[6 + c % 2]
                fm_chunk(P, b2, b2tk, W, c_tk, 1024 + c * 128, 128, hnT, hts, ns)
                P.op("dve", lambda e, c=c, b2=b2: e.tensor_tensor(out=yD[:, 4 + c, 0:ntok], in0=acc[:, 0:ntok], in1=b2[:, 0:ntok],
                                                                  op=ALU.mult), rd=[acc_tk, b2tk], wr=[yD_tk])
            if i == 0:
                P.op("dve", lambda e, c=c: e.tensor_scalar(out=xg[:, c, 0:2], in0=xg[:, c, ntok:ntok + 2], scalar1=hfl[:, 0:1],
                                                           scalar2=None, op0=ALU.mult), rd=[xg_tk[c]], wr=[xg_tk[c]])
            else:
                P.op("dve", lambda e, c=c: e.tensor_copy(out=xg[:, c, 0:2], in_=xg[:, c, ntok:ntok + 2]),
                     rd=[xg_tk[c]], wr=[xg_tk[c]])
        if i > 0:
            P.dma("sp", S["YT"][:, o0:o0 + ntok].rearrange("(c p) t -> p c t", p=128), yD[:, :, 0:ntok], rd=[yD_tk])

    for i, (o0, ntok) in enumerate(tiles):
        tile_body(i, o0, ntok)
    P.barrier()
    P.flush()
    C.close()


PARAMS = ["g_mix", "g_xa", "g_mem", "xa_wq", "xa_wkv", "xa_wo", "xa_gq", "xa_gk", "g_ffn", "w_gate", "w_up", "w_down",
          "e_w_in", "e_b_f", "e_g_v", "e_w_s", "e_b_s", "e_g_qn", "e_g_kn", "e_w_out",
          "o_w_in", "o_w_pool", "o_s_pool", "o_conv_w", "o_w_out"]


def build_program(shapes, debug=False, nphases=7):
    nc = bass.Bass("TRN2", target_bir_lowering=False)
    S = {}
    prm = {}
    ein = lambda name, shape: nc.dram_tensor(name, list(shape), F32, kind="ExternalInput").ap()
    S["xc"] = ein("xc", [SEQ, D])
    S["mem"] = ein("mem", [256, D])
    for k in PARAMS:
        prm[k] = ein(k, shapes[k])
    S["ident"] = ein("ident", [128, 128])
    S["identf"] = S["ident"]
    S["bd64"] = ein("bd64", [128, 128])
    S["tri"] = ein("tri", [128, 128])
    S["cmask"] = ein("cmask", [128, 64])
    S["icnt"] = ein("icnt", [4, 512])
    S["hflag"] = ein("hflag", [1])
    kind = "ExternalOutput" if debug else "Internal"
    scr = lambda name, shape, dt: nc.dram_tensor(name, list(shape), dt, kind=kind).ap()
    S["KT"] = scr("KT", [512, SEQ], BF16)
    S["QT"] = scr("QT", [512, NOWN], BF16)
    S["VV"] = scr("VV", [SEQ, 512], BF16)
    S["NK"] = scr("NK", [128, 64, 8], F32)
    S["NR"] = scr("NR", [8, 64], F32)
    S["YT"] = scr("YT", [D, NOWN], BF16)
    S["X1"] = scr("X1", [NOWN, D], F32)
    out = nc.dram_tensor("out", [HALF, D], F32, kind="ExternalOutput").ap()
    tiles_a = [(512 * i, 512, False) for i in range(7)] + [(3584, 384, False), (OWN0, 128, True)] + \
              [(HALF + 512 * i, 512, True) for i in range(8)]
    S["tiles_a"] = tiles_a
    own = [(0, 128)] + [(128 + 512 * i, 512) for i in range(8)]
    with ExitStack() as es:
        P = Prog(nc, es)
        phase_a(nc, P, S, prm)
        if nphases > 1:
            phase_b(nc, P, S)
        if nphases > 2:
            phase_cx(nc, P, S, prm, 0, prm["e_w_out"][0], S["xc"], [(OWN0 + o, o, n) for (o, n) in own])
        if nphases > 3:
            phase_ffn(nc, P, S["X1"], S["X1"], own, [o for (o, n) in own], prm["g_ffn"][0], prm["w_gate"][0], prm["w_up"][0],
                      prm["w_down"][0], S["ident"])
        if nphases > 4:
            phase_d(nc, P, S, prm, own)
        if nphases > 5:
            phase_cx(nc, P, S, prm, 1, prm["o_w_out"][0], S["X1"], [(o, o, n) for (o, n) in own[1:]])
        if nphases > 6:
            phase_ffn(nc, P, S["X1"], out, own[1:], [o - 128 for (o, n) in own[1:]], prm["g_ffn"][1], prm["w_gate"][1],
                      prm["w_up"][1], prm["w_down"][1], S["ident"])
    return nc


def make_in_maps(inputs):
    x = np.ascontiguousarray(inputs["x"], dtype=np.float32)
    mem = np.ascontiguousarray(inputs["mem"], dtype=np.float32)
    ident = np.eye(128, dtype=np.float32)
    bd = np.zeros((128, 128), np.float32)
    bd[:64, :64] = 1
    bd[64:, 64:] = 1
    tri = np.triu(np.ones((128, 128), np.float32))
    maps = []
    for c in range(8):
        b, h = c // 2, c % 2
        m = {k: np.ascontiguousarray(inputs[k], dtype=np.float32) for k in PARAMS}
        if h == 1:
            m["xc"] = x[b]
            cm = np.zeros((128, 64), np.float32)
            ic = np.tile((1.0 / np.array([2, 4, 8, 16], np.float32))[:, None], (1, 512))
            hf = np.ones((1,), np.float32)
        else:
            m["xc"] = np.concatenate([np.zeros((HALF, D), np.float32), x[b, :HALF]], axis=0)
            cm = np.zeros((128, 64), np.float32)
            cm[:, :32] = NEG
            pos = np.arange(512, dtype=np.float32)
            ic = np.stack([1.0 / np.minimum(pos + 1, w) for w in (2, 4, 8, 16)]).astype(np.float32)
            hf = np.zeros((1,), np.float32)
        m.update(mem=mem[b], ident=ident, bd64=bd, tri=tri, cmask=cm, icnt=ic, hflag=hf)
        maps.append(m)
    return maps


def kernel(**inputs):
    shapes = {k: tuple(np.shape(inputs[k])) for k in PARAMS}
    nc = build_program(shapes)
    maps = make_in_maps(inputs)
    res = run_bass_kernel_spmd(nc, maps, core_ids=list(range(8)))
    out = np.empty((4, SEQ, D), np.float32)
    for c in range(8):
        b, h = c // 2, c % 2
        out[b, h * HALF:(h + 1) * HALF] = res.results[c]["out"]
    return out
```

```python
import numpy as np
from contextlib import ExitStack
import concourse.bass as bass
import concourse.mybir as mybir
from concourse.bass_utils import run_bass_kernel_spmd

F32 = mybir.dt.float32
BF16 = mybir.dt.bfloat16
AF = mybir.ActivationFunctionType
ALU = mybir.AluOpType
AX = mybir.AxisListType

D = 1024
DFF = 2816
NCH = 8
EPS = 1e-6
SEQ = 8192
HALF = 4096
HALO = 128
OWN0 = HALF - HALO
NOWN = HALF + HALO
NEG = -30000.0


class Tk:
    __slots__ = ("w", "r")

    def __init__(self):
        self.w = {}
        self.r = {}


def _merge(d, s):
    for k, v in s.items():
        if d.get(k, 0) < v:
            d[k] = v


class Prog:
    ENG = ("pe", "act", "dve", "pool", "sp")
    NDS = 12

    def __init__(self, nc, es):
        self.nc = nc
        self.sem = {}
        self.cnt = {}
        self.known = {e: {} for e in self.ENG}
        self.streams = {e: [] for e in self.ENG}
        for e in self.ENG:
            self.sem[e] = es.enter_context(nc.semaphore("s_" + e))
            self.cnt[e] = 0
        self.drr = {}
        for q in ("sp", "pool", "act"):
            self.drr[q] = 0
            for k in range(self.NDS):
                key = (q, k)
                self.sem[key] = es.enter_context(nc.semaphore("d_%s%d" % (q, k)))
                self.cnt[key] = 0

    def _waits(self, eng, deps):
        st = self.streams[eng]
        kn = self.known[eng]
        for key, val in deps.items():
            if val <= 0:
                continue
            if eng == "pe" and key == "pe":
                continue
            if kn.get(key, 0) >= val:
                continue
            kn[key] = val
            st.append(("w", key, val))

    def op(self, eng, fn, rd=(), wr=()):
        deps = {}
        for t in rd:
            _merge(deps, t.w)
        for t in wr:
            _merge(deps, t.w)
            _merge(deps, t.r)
        self._waits(eng, deps)
        self.cnt[eng] += 1
        v = self.cnt[eng]
        self.streams[eng].append(("o", fn, eng))
        for t in rd:
            if t.r.get(eng, 0) < v:
                t.r[eng] = v
        for t in wr:
            t.w = {eng: v}
            t.r = {}

    def dma(self, q, out, in_, rd=(), wr=()):
        k = self.drr[q] % self.NDS
        self.drr[q] += 1
        key = (q, k)
        deps = {key: self.cnt[key]}
        for t in rd:
            _merge(deps, t.w)
        for t in wr:
            _merge(deps, t.w)
            _merge(deps, t.r)
        self._waits(q, deps)
        self.cnt[key] += 16
        v = self.cnt[key]
        self.streams[q].append(("d", out, in_, key))
        for t in rd:
            if t.r.get(key, 0) < v:
                t.r[key] = v
        for t in wr:
            t.w = {key: v}
            t.r = {}

    def barrier(self):
        allc = {k: v for k, v in self.cnt.items() if v > 0}
        for e in self.ENG:
            self._waits(e, dict(allc))

    def flush(self):
        nc = self.nc
        streams = self.streams
        self.streams = {e: [] for e in self.ENG}
        sem = self.sem

        def run(e, items):
            for it in items:
                if it[0] == "w":
                    e.wait_ge(sem[it[1]], it[2])
                elif it[0] == "o":
                    it[1](e).then_inc(sem[it[2]], 1)
                else:
                    e.dma_start(out=it[1], in_=it[2]).then_inc(sem[it[3]], 16)

        with nc.allow_non_contiguous_dma(reason="tiny strided parameter/stat DMAs"), nc.Block() as blk:
            @blk.tensor
            def _(e):
                run(e, streams["pe"])

            @blk.scalar
            def _(e):
                run(e, streams["act"])

            @blk.vector
            def _(e):
                run(e, streams["dve"])

            @blk.gpsimd
            def _(e):
                run(e, streams["pool"])

            @blk.sync
            def _(e):
                run(e, streams["sp"])


class Ctx:
    _uid = [0]

    def __init__(self, nc, P):
        self.nc = nc
        self.P = P
        self.es = ExitStack()
        Ctx._uid[0] += 1
        self.n = Ctx._uid[0] * 1000

    def sb(self, shape, dt, name=None):
        self.n += 1
        return self.es.enter_context(self.nc.sbuf_tensor("%s_%d" % (name or "t", self.n), list(shape), dt))

    def psum_banks(self):
        banks = []
        for i in range(8):
            self.n += 1
            t = self.es.enter_context(self.nc.psum_tensor("ps_%d" % self.n, [128, 512], F32))
            banks.append((t, Tk()))
        return banks

    def close(self):
        self.es.close()


def load_weight_bf16(P, dst, dst_tk, w_ap, kchunks, ncols, col0=0, split=2):
    per = (kchunks + split - 1) // split
    for s in range(0, kchunks, per):
        e = min(kchunks, s + per)
        src = w_ap[s * 128:e * 128, col0:col0 + ncols].rearrange("(k p) n -> p k n", p=128)
        P.dma("pool", dst[:, s:e, :], src, wr=[dst_tk])


def emit_norm_T(P, C, xt, xt_tk, nsub, gb, hn, hn_tk, ss, ss_tk, rstd, rstd_tk, junk, junk_tk,
                hnT, hnT_tks, tp_banks, ident, evac_engs=("act", "dve")):
    for s in range(nsub):
        P.op("act", lambda e, s=s: e.activation(out=junk[:], in_=xt[:, s, :], func=AF.Square,
                                                accum_out=ss[:, s:s + 1]),
             rd=[xt_tk], wr=[junk_tk, ss_tk])
    P.op("dve", lambda e: e.tensor_scalar(out=rstd[:, 0:nsub], in0=ss[:, 0:nsub], scalar1=1.0 / D, scalar2=EPS,
                                          op0=ALU.mult, op1=ALU.add), rd=[ss_tk], wr=[rstd_tk])
    P.op("act", lambda e: e.activation(out=rstd[:, 0:nsub], in_=rstd[:, 0:nsub], func=AF.Sqrt),
         rd=[rstd_tk], wr=[rstd_tk])
    P.op("dve", lambda e: e.reciprocal(out=rstd[:, 0:nsub], in_=rstd[:, 0:nsub]),
         rd=[rstd_tk], wr=[rstd_tk])
    for s in range(nsub):
        P.op("dve", lambda e, s=s: e.scalar_tensor_tensor(out=hn[:, s % 2, :], in0=xt[:, s, :], scalar=rstd[:, s:s + 1],
                                                          in1=gb[:], op0=ALU.mult, op1=ALU.mult),
             rd=[xt_tk, rstd_tk], wr=[hn_tk[s % 2]])
        bank, btk = tp_banks[s % len(tp_banks)]
        pb = bank[:].bitcast(BF16)
        for c in range(NCH):
            P.op("pe", lambda e, s=s, c=c, pb=pb: e.transpose(out=pb[:, c * 128:(c + 1) * 128],
                                                             in_=hn[:, s % 2, c * 128:(c + 1) * 128], identity=ident[:]),
                 rd=[hn_tk[s % 2]], wr=[btk])
        eng = evac_engs[s % len(evac_engs)]
        dst = hnT[:, :, s * 128:(s + 1) * 128]
        srcv = pb.rearrange("p (c t) -> p c t", c=NCH)
        if eng == "act":
            P.op("act", lambda e, dst=dst, srcv=srcv: e.activation(out=dst, in_=srcv, func=AF.Copy),
                 rd=[btk], wr=[hnT_tks[s]])
        else:
            P.op(eng, lambda e, dst=dst, srcv=srcv: e.tensor_copy(out=dst, in_=srcv), rd=[btk], wr=[hnT_tks[s]])


def make_ident(P, C, dt=BF16):
    raise NotImplementedError


def phase_ffn(nc, P, xin, xout, tiles_in, tiles_out, g_ap, wg_ap, wu_ap, wd_ap, ident_ap):
    C = Ctx(nc, P)
    NM = DFF // 128
    wg = C.sb([128, NCH, DFF], BF16, "wg"); wg_tk = Tk()
    wu = C.sb([128, NCH, DFF], BF16, "wu"); wu_tk = Tk()
    wd = C.sb([128, NM, D], BF16, "wd"); wd_tk = Tk()
    gb = C.sb([128, D], F32, "gb"); gb_tk = Tk()
    ident = C.sb([128, 128], BF16, "ident"); ident_tk = Tk()
    xts = [(C.sb([128, 4, D], F32, "xt"), Tk()) for _ in range(2)]
    hn = C.sb([128, 2, D], BF16, "hn"); hn_tk = [Tk() for _ in range(2)]
    hnT = C.sb([128, NCH, 512], BF16, "hnT"); hnT_tks = [Tk() for _ in range(4)]
    hT = C.sb([128, NM, 512], BF16, "hT"); hT_tks = [Tk() for _ in range(NM)]
    junk = C.sb([128, D], BF16, "junk"); junk_tk = Tk()
    sgs = [(C.sb([128, 512], BF16, "sg"), Tk()) for _ in range(2)]
    ss = C.sb([128, 8], F32, "ss"); ss_tk = Tk()
    rstd = C.sb([128, 8], F32, "rstd"); rstd_tk = Tk()
    banks = C.psum_banks()

    P.dma("pool", ident[:], ident_ap, wr=[ident_tk])
    P.dma("sp", gb[:], g_ap.partition_broadcast(128), wr=[gb_tk])
    load_weight_bf16(P, wg, wg_tk, wg_ap, NCH, DFF, split=4)
    load_weight_bf16(P, wu, wu_tk, wu_ap, NCH, DFF, split=4)
    load_weight_bf16(P, wd, wd_tk, wd_ap, NM, D, split=4)

    nt = len(tiles_in)

    def load_x(i):
        r0, ntok = tiles_in[i]
        xt, xt_tk = xts[i % 2]
        ns = ntok // 128
        P.dma("sp", xt[:, 0:ns, :], xin[r0:r0 + ntok, :].rearrange("(s p) d -> p s d", p=128), wr=[xt_tk])

    def norm(i):
        r0, ntok = tiles_in[i]
        xt, xt_tk = xts[i % 2]
        ns = ntok // 128
        emit_norm_T(P, C, xt, xt_tk, ns, gb, hn, hn_tk, ss, ss_tk, rstd, rstd_tk, junk, junk_tk,
                    hnT, hnT_tks, banks[4:6], ident)

    def gate_up(i):
        r0, ntok = tiles_in[i]
        ns = ntok // 128
        for mo in range(NM):
            bg, bg_tk = banks[(2 * mo) % 4]
            bu, bu_tk = banks[(2 * mo + 1) % 4]
            for kc in range(NCH):
                P.op("pe", lambda e, kc=kc, mo=mo, bg=bg: e.matmul(bg[:, 0:ntok], lhsT=wg[:, kc, mo * 128:(mo + 1) * 128],
                                                                  rhs=hnT[:, kc, 0:ntok], start=(kc == 0), stop=(kc == NCH - 1)),
                     rd=[wg_tk] + hnT_tks[0:ns], wr=[bg_tk])
            for kc in range(NCH):
                P.op("pe", lambda e, kc=kc, mo=mo, bu=bu: e.matmul(bu[:, 0:ntok], lhsT=wu[:, kc, mo * 128:(mo + 1) * 128],
                                                                  rhs=hnT[:, kc, 0:ntok], start=(kc == 0), stop=(kc == NCH - 1)),
                     rd=[wu_tk] + hnT_tks[0:ns], wr=[bu_tk])
            sg, sg_tk = sgs[mo % 2]
            P.op("act", lambda e, sg=sg, bg=bg: e.activation(out=sg[:, 0:ntok], in_=bg[:, 0:ntok], func=AF.Silu),
                 rd=[bg_tk], wr=[sg_tk])
            P.op("dve", lambda e, sg=sg, bu=bu, mo=mo: e.tensor_tensor(out=hT[:, mo, 0:ntok], in0=sg[:, 0:ntok],
                                                                       in1=bu[:, 0:ntok], op=ALU.mult),
                 rd=[sg_tk, bu_tk], wr=[hT_tks[mo]])

    def down(i):
        r0, ntok = tiles_in[i]
        ro = tiles_out[i]
        xt, xt_tk = xts[i % 2]
        ns = ntok // 128
        j = 0
        for s in range(ns):
            for nh in range(2):
                bo, bo_tk = banks[6 + (j % 2)]
                j += 1
                for mo in range(NM):
                    P.op("pe", lambda e, s=s, nh=nh, mo=mo, bo=bo: e.matmul(bo[:, :], lhsT=hT[:, mo, s * 128:(s + 1) * 128],
                                                                           rhs=wd[:, mo, nh * 512:(nh + 1) * 512],
                                                                           start=(mo == 0), stop=(mo == NM - 1)),
                         rd=[wd_tk, hT_tks[mo]], wr=[bo_tk])
                P.op("dve", lambda e, s=s, nh=nh, bo=bo, xt=xt: e.tensor_tensor(out=xt[:, s, nh * 512:(nh + 1) * 512],
                                                                               in0=xt[:, s, nh * 512:(nh + 1) * 512],
                                                                               in1=bo[:, :], op=ALU.add),
                     rd=[bo_tk], wr=[xt_tk])
        if ro is not None:
            P.dma("sp", xout[ro:ro + ntok, :].rearrange("(s p) d -> p s d", p=128), xt[:, 0:ns, :], rd=[xt_tk])

    load_x(0)
    if nt > 1:
        load_x(1)
    norm_dep(P, [gb_tk, ident_tk])
    norm(0)
    for i in range(nt):
        gate_up(i)
        if i + 1 < nt:
            norm(i + 1)
        down(i)
        if i + 2 < nt:
            load_x(i + 2)
    P.barrier()
    P.flush()
    C.close()


def norm_dep(P, tks):
    deps = {}
    for t in tks:
        _merge(deps, t.w)
    for e in ("pe", "dve", "act", "pool"):
        P._waits(e, dict(deps))


class Kit:
    def __init__(self, nc, P, C, g_ap, ident_ap, nx=2, nsubmax=4):
        self.P = P
        self.C = C
        self.gb = C.sb([128, D], F32, "gb"); self.gb_tk = Tk()
        self.ident = C.sb([128, 128], BF16, "ident"); self.ident_tk = Tk()
        self.xts = [(C.sb([128, nsubmax, D], F32, "xt"), Tk()) for _ in range(nx)]
        self.hn = C.sb([128, 2, D], BF16, "hn"); self.hn_tk = [Tk(), Tk()]
        self.hnT = C.sb([128, NCH, 128 * nsubmax], BF16, "hnT"); self.hnT_tks = [Tk() for _ in range(nsubmax)]
        self.junk = C.sb([128, D], BF16, "junk"); self.junk_tk = Tk()
        self.ss = C.sb([128, 8], F32, "ss"); self.ss_tk = Tk()
        self.rstd = C.sb([128, 8], F32, "rstd"); self.rstd_tk = Tk()
        self.banks = C.psum_banks()
        P.dma("pool", self.ident[:], ident_ap, wr=[self.ident_tk])
        if g_ap is not None:
            P.dma("sp", self.gb[:], g_ap.partition_broadcast(128), wr=[self.gb_tk])

    def consts_ready(self, extra=()):
        norm_dep(self.P, [self.gb_tk, self.ident_tk] + list(extra))

    def load_x(self, slot, src_rows_ap, ns):
        xt, xt_tk = self.xts[slot]
        self.P.dma("sp", xt[:, 0:ns, :], src_rows_ap.rearrange("(s p) d -> p s d", p=128), wr=[xt_tk])

    def norm(self, slot, ns, tp=(0, 1)):
        xt, xt_tk = self.xts[slot]
        emit_norm_T(self.P, self.C, xt, xt_tk, ns, self.gb, self.hn, self.hn_tk, self.ss, self.ss_tk,
                    self.rstd, self.rstd_tk, self.junk, self.junk_tk, self.hnT, self.hnT_tks,
                    [self.banks[i] for i in tp], self.ident)


def fm_chunk(P, bank, btk, W, wtk, col0, M, hnT, hnT_tks, ns, nk=NCH):
    ntok = ns * 128
    for kc in range(nk):
        P.op("pe", lambda e, kc=kc: e.matmul(bank[0:M, 0:ntok], lhsT=W[:, kc, col0:col0 + M], rhs=hnT[:, kc, 0:ntok],
                                             start=(kc == 0), stop=(kc == nk - 1)),
             rd=[wtk] + list(hnT_tks[0:ns]), wr=[btk])


def tm_block(P, bank, btk, hnT, hnT_tk_s, s, W, wtk, col0, ncols, nk=NCH):
    for kc in range(nk):
        P.op("pe", lambda e, kc=kc: e.matmul(bank[:, 0:ncols], lhsT=hnT[:, kc, s * 128:(s + 1) * 128],
                                             rhs=W[:, kc, col0:col0 + ncols], start=(kc == 0), stop=(kc == nk - 1)),
             rd=[wtk, hnT_tk_s], wr=[btk])


def rsqrt_small(P, t, tk, n, scale):
    P.op("dve", lambda e: e.tensor_scalar(out=t[:, 0:n], in0=t[:, 0:n], scalar1=scale, scalar2=EPS,
                                          op0=ALU.mult, op1=ALU.add), rd=[tk], wr=[tk])
    P.op("act", lambda e: e.activation(out=t[:, 0:n], in_=t[:, 0:n], func=AF.Sqrt), rd=[tk], wr=[tk])
    P.op("dve", lambda e: e.reciprocal(out=t[:, 0:n], in_=t[:, 0:n]), rd=[tk], wr=[tk])


def headnorm_fm(P, K, src_bank, src_tk, ntok, BD, gcol, sq, sq_tk, ssb, ssb_tk, rs, rs_tk, out_ap, out_tk, inv_n):
    P.op("act", lambda e: e.activation(out=sq[:, 0:ntok], in_=src_bank[:, 0:ntok], func=AF.Square),
         rd=[src_tk], wr=[sq_tk])
    P.op("pe", lambda e: e.matmul(ssb[:, 0:ntok], lhsT=BD[:], rhs=sq[:, 0:ntok], start=True, stop=True),
         rd=[sq_tk], wr=[ssb_tk])
    P.op("dve", lambda e: e.tensor_scalar(out=rs[:, 0:ntok], in0=ssb[:, 0:ntok], scalar1=inv_n, scalar2=EPS,
                                          op0=ALU.mult, op1=ALU.add), rd=[ssb_tk], wr=[rs_tk])
    P.op("act", lambda e: e.activation(out=rs[:, 0:ntok], in_=rs[:, 0:ntok], func=AF.Sqrt), rd=[rs_tk], wr=[rs_tk])
    P.op("dve", lambda e: e.reciprocal(out=rs[:, 0:ntok], in_=rs[:, 0:ntok]), rd=[rs_tk], wr=[rs_tk])
    P.op("dve", lambda e: e.scalar_tensor_tensor(out=out_ap, in0=src_bank[:, 0:ntok], scalar=gcol, in1=rs[:, 0:ntok],
                                                 op0=ALU.mult, op1=ALU.mult), rd=[src_tk, rs_tk], wr=[out_tk])


def headnorm_multi(P, srcs, ntok, BD, gcol, sqs, ssbs, rss, outs, inv_n):
    n = len(srcs)
    for i in range(n):
        (b, btk), (sq, sq_tk) = srcs[i], sqs[i]
        P.op("act", lambda e, b=b, sq=sq: e.activation(out=sq[:, 0:ntok], in_=b[:, 0:ntok], func=AF.Square),
             rd=[btk], wr=[sq_tk])
    for i in range(n):
        (sq, sq_tk), (sb, sb_tk) = sqs[i], ssbs[i]
        P.op("pe", lambda e, sq=sq, sb=sb: e.matmul(sb[:, 0:ntok], lhsT=BD[:], rhs=sq[:, 0:ntok], start=True, stop=True),
             rd=[sq_tk], wr=[sb_tk])
    for i in range(n):
        (sb, sb_tk), (rs, rs_tk) = ssbs[i], rss[i]
        P.op("dve", lambda e, sb=sb, rs=rs: e.tensor_scalar(out=rs[:, 0:ntok], in0=sb[:, 0:ntok], scalar1=inv_n, scalar2=EPS,
                                                            op0=ALU.mult, op1=ALU.add), rd=[sb_tk], wr=[rs_tk])
    for i in range(n):
        rs, rs_tk = rss[i]
        P.op("act", lambda e, rs=rs: e.activation(out=rs[:, 0:ntok], in_=rs[:, 0:ntok], func=AF.Sqrt), rd=[rs_tk], wr=[rs_tk])
    for i in range(n):
        rs, rs_tk = rss[i]
        P.op("dve", lambda e, rs=rs: e.reciprocal(out=rs[:, 0:ntok], in_=rs[:, 0:ntok]), rd=[rs_tk], wr=[rs_tk])
    for i in range(n):
        (b, btk), (rs, rs_tk), (o, otk) = srcs[i], rss[i], outs[i]
        P.op("dve", lambda e, b=b, rs=rs, o=o: e.scalar_tensor_tensor(out=o, in0=b[:, 0:ntok], scalar=gcol, in1=rs[:, 0:ntok],
                                                                      op0=ALU.mult, op1=ALU.mult), rd=[btk, rs_tk], wr=[otk])


def phase_a(nc, P, S, prm):
    xc = S["xc"]
    C = Ctx(nc, P)
    K = Kit(nc, P, C, prm["g_mix"][0], S["ident"])
    NW = 2568
    W = C.sb([128, NCH, NW], BF16, "win"); W_tk = Tk()
    load_weight_bf16(P, W, W_tk, prm["e_w_in"][0], NCH, NW, split=4)
    BD = C.sb([128, 128], BF16, "bd"); c_tk = Tk()
    P.dma("pool", BD[:], S["bd64"], wr=[c_tk])
    id8 = C.sb([8, 8], F32, "id8")
    P.dma("sp", id8[:], S["identf"][0:8, 0:8], wr=[c_tk])
    gq = C.sb([128, 1], F32, "gq"); gk = C.sb([128, 1], F32, "gk")
    for hh in range(2):
        P.dma("sp", gq[hh * 64:(hh + 1) * 64, :], prm["e_g_qn"][0].rearrange("(p o) -> p o", o=1), wr=[c_tk])
        P.dma("sp", gk[hh * 64:(hh + 1) * 64, :], prm["e_g_kn"][0].rearrange("(p o) -> p o", o=1), wr=[c_tk])
    nbf = C.sb([8, 1], F32, "nbf")
    P.dma("sp", nbf[:], prm["e_b_f"][0].rearrange("(p o) -> p o", o=1), wr=[c_tk])
    gvb = C.sb([128, 512], F32, "gvb")
    P.dma("sp", gvb[:], prm["e_g_v"][0].partition_broadcast(128), wr=[c_tk])
    bsb = C.sb([128, 512], F32, "bsb")
    P.dma("sp", bsb[:], prm["e_b_s"][0].rearrange("g t -> (g t)").partition_broadcast(128), wr=[c_tk])
    wsf = C.sb([128, 4, 128], F32, "wsf")
    P.dma("sp", wsf[:], prm["e_w_s"][0].rearrange("g t s -> t g s"), wr=[c_tk])
    wsb = C.sb([128, 4, 128], BF16, "wsb"); wsT = C.sb([128, 4, 128], BF16, "wsT")
    ones8 = C.sb([8, 512], F32, "ones8")
    carry = C.sb([8, 1], F32, "carry"); carry_tk = Tk()
    K.consts_ready([c_tk])
    P.op("dve", lambda e: e.memset(ones8[:], 1.0), wr=[c_tk])
    P.op("dve", lambda e: e.memset(carry[:], 0.0), wr=[carry_tk])
    P.op("dve", lambda e: e.tensor_scalar(out=gq[:], in0=gq[:], scalar1=0.125, scalar2=None, op0=ALU.mult), wr=[c_tk])
    P.op("dve", lambda e: e.tensor_scalar(out=nbf[:], in0=nbf[:], scalar1=-1.0, scalar2=None, op0=ALU.mult), wr=[c_tk])
    P.op("dve", lambda e: e.memset(wsf[0:64, :, 64:128], 0.0), wr=[c_tk])
    P.op("dve", lambda e: e.tensor_copy(out=wsb[:], in_=wsf[:]), wr=[c_tk])
    b0, b0tk = K.banks[0]
    pb = b0[:].bitcast(BF16)
    for g in range(4):
        P.op("pe", lambda e, g=g: e.transpose(out=pb[:, g * 128:(g + 1) * 128], in_=wsb[:, g, :], identity=K.ident[:]),
             rd=[c_tk], wr=[b0tk])
    P.op("dve", lambda e: e.tensor_copy(out=wsT[:].rearrange("p g t -> p (g t)"), in_=pb[:, 0:512]), rd=[b0tk], wr=[c_tk])
    norm_dep(P, [c_tk])

    sqs = [(C.sb([128, 512], BF16, "sq"), Tk()) for _ in range(2)]
    rss = [(C.sb([128, 512], F32, "rs"), Tk()) for _ in range(2)]
    kn = [(C.sb([128, 512], BF16, "kn"), Tk()) for _ in range(4)]
    vtm = [(C.sb([128, 512], BF16, "vtm"), Tk()) for _ in range(2)]
    fe = C.sb([8, 512], F32, "fe"); fe_tk = Tk()
    negc = C.sb([8, 512], F32, "negc"); negc_tk = Tk()
    nk = C.sb([128, 4, 8], F32, "nk"); nk_tk = Tk()
    uT = C.sb([128, 4, 512], BF16, "uT"); uT_tk = [Tk() for _ in range(4)]
    vgs = [(C.sb([128, 512], F32, "vg"), Tk()) for _ in range(2)]
    vns = [(C.sb([128, 512], BF16, "vn"), Tk()) for _ in range(2)]
    ssgs = [(C.sb([128, 4], F32, "ssg"), Tk()) for _ in range(2)]
    t1s = [(C.sb([128, 512], F32, "t1"), Tk()) for _ in range(2)]
    junks = [(C.sb([128, 128], BF16, "junk2"), Tk()) for _ in range(2)]
    yaT = C.sb([128, 4, 512], BF16, "yaT"); ya_tk = Tk()
    banks = K.banks
    rr = [0]

    def fmbank():
        rr[0] += 1
        return banks[2 + rr[0] % 2]

    tiles = S["tiles_a"]
    kt_of = lambda ctx0: ctx0 // 128
    def tile_body(i, c0, ntok, full):
        ns = ntok // 128
        slot = i % 2
        K.load_x(slot, xc[c0:c0 + ntok, :], ns)
        K.norm(slot, ns)
        hnT, hts = K.hnT, K.hnT_tks
        def qk_pairs(colbase, gcol, dst, dcol0):
            for cp in range(2):
                srcs = []
                for ci in range(2):
                    c = cp * 2 + ci
                    b, btk = banks[2 + ci]
                    fm_chunk(P, b, btk, W, W_tk, colbase + c * 128, 128, hnT, hts, ns)
                    srcs.append((b, btk))
                outs = [(kn[cp * 2 + ci][0][:, 0:ntok], kn[cp * 2 + ci][1]) for ci in range(2)]
                headnorm_multi(P, srcs, ntok, BD, gcol, sqs, [banks[4], banks[7]], rss, outs, 1.0 / 64)
                for ci in range(2):
                    c = cp * 2 + ci
                    o, otk = kn[c]
                    P.dma("sp", dst[c * 128:(c + 1) * 128, dcol0:dcol0 + ntok], o[:, 0:ntok], rd=[otk])

        qk_pairs(1536, gk[:, 0:1], S["KT"], c0)
        if full:
            qk_pairs(1024, gq[:, 0:1], S["QT"], c0 - OWN0)
        for s in range(ns):
            b, btk = banks[5] if s % 2 == 0 else banks[4]
            tm_block(P, b, btk, hnT, hts[s], s, W, W_tk, 2048, 512)
            o, otk = vtm[s % 2]
            P.op("act", lambda e, o=o, b=b: e.activation(out=o[:], in_=b[:], func=AF.Copy), rd=[btk], wr=[otk])
            P.dma("sp", S["VV"][c0 + s * 128:c0 + (s + 1) * 128, :], o[:], rd=[otk])
        b, btk = banks[6]
        fm_chunk(P, b, btk, W, W_tk, 2560, 8, hnT, hts, ns)
        P.op("act", lambda e, b=b: e.activation(out=fe[:, 0:ntok], in_=b[0:8, 0:ntok], func=AF.Exp, scale=-1.0,
                                                bias=nbf[:, 0:1]), rd=[btk], wr=[fe_tk])
        P.op("act", lambda e: e.activation(out=fe[:, 0:ntok], in_=fe[:, 0:ntok], func=AF.Ln, bias=1.0),
             rd=[fe_tk], wr=[fe_tk])
        P.op("dve", lambda e: e.tensor_tensor_scan(out=negc[:, 0:ntok], data0=ones8[:, 0:ntok], data1=fe[:, 0:ntok],
                                                   initial=carry[:, 0:1], op0=ALU.mult, op1=ALU.add),
             rd=[fe_tk, carry_tk], wr=[negc_tk])
        P.op("dve", lambda e: e.tensor_copy(out=carry[:], in_=negc[:, ntok - 1:ntok]), rd=[negc_tk], wr=[carry_tk])
        P.dma("sp", S["NR"][:, c0 // 128:c0 // 128 + ns], negc[:, 64:ntok:128], rd=[negc_tk])
        b, btk = banks[6]
        for s in range(ns):
            P.op("pe", lambda e, s=s, b=b: e.matmul(b[:, s * 8:(s + 1) * 8], lhsT=negc[:, s * 128:(s + 1) * 128],
                                                    rhs=id8[:], start=True, stop=True), rd=[negc_tk], wr=[btk])
        P.op("dve", lambda e, b=b: e.tensor_copy(out=nk[:, 0:ns, :].rearrange("p s h -> p (s h)"), in_=b[:, 0:ns * 8]),
             rd=[btk], wr=[nk_tk])
        P.dma("sp", S["NK"][:, kt_of(c0):kt_of(c0) + ns, :], nk[:, 0:ns, :], rd=[nk_tk])
        if not full:
            return
        for c in range(4):
            b, btk = fmbank()
            fm_chunk(P, b, btk, W, W_tk, c * 128, 128, hnT, hts, ns)
            P.op("act", lambda e, c=c, b=b: e.activation(out=uT[:, c, 0:ntok], in_=b[:, 0:ntok], func=AF.Gelu),
                 rd=[btk], wr=[uT_tk[c]])
        for s0 in range(0, ns, 2):
            subs = list(range(s0, min(ns, s0 + 2)))
            tmb = [banks[5], banks[4]]
            spb = [banks[6], banks[7]]
            for i, sx in enumerate(subs):
                tm_block(P, tmb[i][0], tmb[i][1], hnT, hts[sx], sx, W, W_tk, 512, 512)
            for i, sx in enumerate(subs):
                P.op("act", lambda e, i=i: e.activation(out=vgs[i][0][:], in_=tmb[i][0][:], func=AF.Gelu),
                     rd=[tmb[i][1]], wr=[vgs[i][1]])
            for i, sx in enumerate(subs):
                for g in range(4):
                    P.op("act", lambda e, g=g, i=i: e.activation(out=junks[i][0][:], in_=vgs[i][0][:, g * 128:(g + 1) * 128],
                                                                 func=AF.Square, accum_out=ssgs[i][0][:, g:g + 1]),
                         rd=[vgs[i][1]], wr=[junks[i][1], ssgs[i][1]])
            for i, sx in enumerate(subs):
                P.op("dve", lambda e, i=i: e.tensor_scalar(out=ssgs[i][0][:, 0:4], in0=ssgs[i][0][:, 0:4], scalar1=1.0 / 128,
                                                           scalar2=EPS, op0=ALU.mult, op1=ALU.add), rd=[ssgs[i][1]], wr=[ssgs[i][1]])
            for i, sx in enumerate(subs):
                P.op("act", lambda e, i=i: e.activation(out=ssgs[i][0][:, 0:4], in_=ssgs[i][0][:, 0:4], func=AF.Sqrt),
                     rd=[ssgs[i][1]], wr=[ssgs[i][1]])
            for i, sx in enumerate(subs):
                P.op("dve", lambda e, i=i: e.reciprocal(out=ssgs[i][0][:, 0:4], in_=ssgs[i][0][:, 0:4]),
                     rd=[ssgs[i][1]], wr=[ssgs[i][1]])
            for i, sx in enumerate(subs):
                for g in range(4):
                    P.op("dve", lambda e, g=g, i=i: e.scalar_tensor_tensor(out=vns[i][0][:, g * 128:(g + 1) * 128],
                                                                           in0=vgs[i][0][:, g * 128:(g + 1) * 128],
                                                                           scalar=ssgs[i][0][:, g:g + 1],
                                                                           in1=gvb[:, g * 128:(g + 1) * 128], op0=ALU.mult, op1=ALU.mult),
                         rd=[vgs[i][1], ssgs[i][1]], wr=[vns[i][1]])
            for i, sx in enumerate(subs):
                for g in range(4):
                    P.op("pe", lambda e, g=g, i=i: e.matmul(spb[i][0][:, g * 128:(g + 1) * 128], lhsT=vns[i][0][:, g * 128:(g + 1) * 128],
                                                            rhs=wsT[:, g, :], start=True, stop=True), rd=[vns[i][1]], wr=[spb[i][1]])
            for i, sx in enumerate(subs):
                P.op("dve", lambda e, i=i: e.tensor_tensor(out=t1s[i][0][:], in0=spb[i][0][:], in1=bsb[:], op=ALU.add),
                     rd=[spb[i][1]], wr=[t1s[i][1]])
            for i, sx in enumerate(subs):
                P.op("dve", lambda e, i=i, sx=sx: e.tensor_tensor(out=yaT[:, :, sx * 128:(sx + 1) * 128],
                                                                  in0=t1s[i][0][:].rearrange("p (g t) -> p g t", g=4),
                                                                  in1=uT[:, :, sx * 128:(sx + 1) * 128], op=ALU.mult),
                     rd=[t1s[i][1]] + uT_tk, wr=[ya_tk])
        o0 = c0 - OWN0
        P.dma("sp", S["YT"][0:512, o0:o0 + ntok].rearrange("(g p) t -> p g t", p=128), yaT[:, :, 0:ntok], rd=[ya_tk])

    for i, (c0, ntok, full) in enumerate(tiles):
        tile_body(i, c0, ntok, full)
    P.barrier()
    P.flush()
    C.close()


def phase_b(nc, P, S):
    C = Ctx(nc, P)
    banks = C.psum_banks()
    c_tk = Tk()
    tri = C.sb([128, 128], BF16, "tri")
    P.dma("pool", tri[:], S["tri"], wr=[c_tk])
    NKm = C.sb([128, 64, 8], F32, "nkm")
    P.dma("sp", NKm[:], S["NK"], wr=[c_tk])
    cm = C.sb([128, 64], F32, "cm")
    P.dma("sp", cm[:], S["cmask"], wr=[c_tk])
    Rb = C.sb([128, 8, 64], F32, "rb")
    P.dma("sp", Rb[:].rearrange("p h k -> p (h k)"), S["NR"].rearrange("h k -> (h k)").partition_broadcast(128), wr=[c_tk])
    onesf = C.sb([128, 128], F32, "onesf")
    norm_dep(P, [c_tk])
    P.op("dve", lambda e: e.memset(onesf[:], 1.0), wr=[c_tk])
    for h in range(8):
        P.op("dve", lambda e, h=h: e.tensor_tensor(out=NKm[:, :, h], in0=NKm[:, :, h], in1=cm[:], op=ALU.add), wr=[c_tk])
    norm_dep(P, [c_tk])
    KA = [(C.sb([128, SEQ], BF16, "ka"), Tk()) for _ in range(2)]
    QA = [(C.sb([128, NOWN], BF16, "qa"), Tk()) for _ in range(2)]
    VA = [(C.sb([128, 64, 128], BF16, "va"), Tk()) for _ in range(2)]
    nrow = [(C.sb([128, 64], F32, "nrow"), Tk()) for _ in range(2)]
    dif = C.sb([128, 32], F32, "dif"); dif_tk = Tk()
    bias = [(C.sb([128, 64], F32, "bias"), Tk()) for _ in range(2)]
    pbuf = [(C.sb([128, 512], BF16, "pb"), Tk()) for _ in range(4)]
    osb = [(C.sb([128, 512], F32, "osb"), Tk()) for _ in range(2)]
    ybT = [(C.sb([128, 512], BF16, "ybT"), Tk()) for _ in range(2)]
    KT0 = OWN0 // 128
    groups = [(0, 128, KT0)] + [(128 + 512 * J, 512, KT0 + 1 + 4 * J) for J in range(8)]

    def load_head(h):
        ka, ka_tk = KA[h % 2]; qa, qa_tk = QA[h % 2]; va, va_tk = VA[h % 2]; nr, nr_tk = nrow[h % 2]
        for half in range(2):
            P.dma("sp", ka[0:64, half * 4096:(half + 1) * 4096], S["KT"][h * 64:(h + 1) * 64, half * 4096:(half + 1) * 4096],
                  wr=[ka_tk])
        P.op("pool", lambda e: e.memset(ka[64:65, :], 1.0), wr=[ka_tk])
        P.dma("sp", qa[0:64, :], S["QT"][h * 64:(h + 1) * 64, :], wr=[qa_tk])
        P.dma("sp", nr[64:65, :], S["NR"][h:h + 1, :], wr=[nr_tk])
        vc = 0 if h % 2 == 0 else 64
        oc = 64 - vc
        for q4 in range(4):
            P.dma("sp", va[:, q4 * 16:(q4 + 1) * 16, vc:vc + 64],
                  S["VV"][q4 * 2048:(q4 + 1) * 2048, h * 64:(h + 1) * 64].rearrange("(k p) d -> p k d", p=128), wr=[va_tk])
        P.op("pool", lambda e: e.memset(va[:, :, oc:oc + 64], 1.0), wr=[va_tk])
        P.op("dve", lambda e: e.tensor_tensor(out=dif[64:65, 0:32].rearrange("p (a b) -> p a b", b=4),
                                              in0=nr[64:65, KT0 + 3:64:4].unsqueeze(2).to_broadcast([1, 8, 4]),
                                              in1=nr[64:65, KT0 + 1:64].rearrange("p (a b) -> p a b", b=4), op=ALU.subtract),
             rd=[nr_tk], wr=[dif_tk])
        P.op("dve", lambda e: e.memset(qa[64:65, 0:128], 0.0), wr=[qa_tk])
        P.op("dve", lambda e: e.tensor_copy(out=qa[64:65, 128:NOWN].rearrange("p (a b) -> p a b", b=128),
                                            in_=dif[64:65, 0:32].unsqueeze(2).to_broadcast([1, 32, 128])),
             rd=[dif_tk], wr=[qa_tk])

    itc = [0]

    def group_block(h, gi, o0, ntok, kt_first):
        ka, ka_tk = KA[h % 2]; qa, qa_tk = QA[h % 2]; va, va_tk = VA[h % 2]
        nsub = ntok // 128
        nkt = kt_first + nsub
        kt_ref = kt_first + (2 if nsub == 4 else 0)
        it = itc[0]; itc[0] += 1
        bs, bs_tk = bias[it % 2]
        P.op("dve", lambda e: e.tensor_scalar(out=bs[:, 0:nkt], in0=NKm[:, 0:nkt, h], scalar1=Rb[:, h, kt_ref:kt_ref + 1],
                                              scalar2=None, op0=ALU.subtract), wr=[bs_tk])
        acc, acc_tk = banks[4 + it % 2]

        def c0_of(kt):
            r = kt - kt_first
            return 0 if r <= 0 else r * 128

        def S_(kt):
            sb_, sb_tk = banks[kt % 4]
            c0 = c0_of(kt)
            P.op("pe", lambda e: e.matmul(sb_[:, c0:ntok], lhsT=ka[0:65, kt * 128:(kt + 1) * 128],
                                          rhs=qa[0:65, o0 + c0:o0 + ntok], start=True, stop=True),
                 rd=[ka_tk, qa_tk], wr=[sb_tk])
            pb_, pb_tk = pbuf[kt % 4]
            P.op("act", lambda e: e.activation(out=pb_[:, c0:ntok], in_=sb_[:, c0:ntok], func=AF.Exp, bias=bs[:, kt:kt + 1]),
                 rd=[sb_tk, bs_tk], wr=[pb_tk])
            if kt >= kt_first:
                P.op("pool", lambda e: e.tensor_tensor(out=pb_[:, c0:c0 + 128], in0=pb_[:, c0:c0 + 128], in1=tri[:], op=ALU.mult),
                     rd=[pb_tk], wr=[pb_tk])

        def V_(kt):
            pb_, pb_tk = pbuf[kt % 4]
            c0 = c0_of(kt)
            P.op("pe", lambda e: e.matmul(acc[:, c0:ntok], lhsT=va[:, kt, :], rhs=pb_[:, c0:ntok],
                                          start=(kt == 0), stop=(kt == nkt - 1)),
                 rd=[pb_tk, va_tk], wr=[acc_tk])

        S_(0)
        S_(1)
        for kt in range(nkt):
            if kt + 2 < nkt:
                S_(kt + 2)
            V_(kt)
        ob, ob_tk = osb[it % 2]
        drow = 64 if h % 2 == 0 else 0
        nrow0 = 0 if h % 2 == 0 else 64
        P.op("act", lambda e: e.activation(out=ob[:, 0:ntok], in_=acc[:, 0:ntok], func=AF.Copy), rd=[acc_tk], wr=[ob_tk])
        P.op("dve", lambda e: e.tensor_scalar(out=ob[drow:drow + 1, 0:ntok], in0=ob[drow:drow + 1, 0:ntok], scalar1=1e-30,
                                              scalar2=None, op0=ALU.add), rd=[ob_tk], wr=[ob_tk])
        P.op("dve", lambda e: e.reciprocal(out=ob[drow:drow + 1, 0:ntok], in_=ob[drow:drow + 1, 0:ntok]), rd=[ob_tk], wr=[ob_tk])
        bc, bc_tk = banks[6 + it % 2]
        P.op("pe", lambda e: e.matmul(bc[:, 0:ntok], lhsT=onesf[drow:drow + 1, :], rhs=ob[drow:drow + 1, 0:ntok],
                                      start=True, stop=True), rd=[ob_tk], wr=[bc_tk])
        yo, yo_tk = ybT[gi % 2]
        P.op("dve", lambda e: e.tensor_tensor(out=yo[nrow0:nrow0 + 64, 0:ntok], in0=ob[nrow0:nrow0 + 64, 0:ntok],
                                              in1=bc[nrow0:nrow0 + 64, 0:ntok], op=ALU.mult), rd=[ob_tk, bc_tk], wr=[yo_tk])
        P.dma("sp", S["YT"][512 + h * 64:512 + (h + 1) * 64, o0:o0 + ntok], yo[nrow0:nrow0 + 64, 0:ntok], rd=[yo_tk])

    load_head(0)
    for h in range(8):
        if h + 1 < 8:
            load_head(h + 1)
        for gi, (o0, ntok, ktf) in enumerate(groups):
            group_block(h, gi, o0, ntok, ktf)
    P.barrier()
    P.flush()
    C.close()


def phase_cx(nc, P, S, prm, layer, w_out_ap, xsrc, tiles):
    C = Ctx(nc, P)
    K = Kit(nc, P, C, prm["g_xa"][layer], S["ident"])
    banks = K.banks
    c_tk = Tk()
    gbm = C.sb([128, D], F32, "gbm")
    P.dma("sp", gbm[:], prm["g_mem"][layer].partition_broadcast(128), wr=[c_tk])
    Wo = C.sb([128, NCH, D], BF16, "wo"); Wq = C.sb([128, NCH, 512], BF16, "wq")
    Wkv = C.sb([128, NCH, D], BF16, "wkv"); Wxo = C.sb([128, 4, D], BF16, "wxo")
    load_weight_bf16(P, Wo, c_tk, w_out_ap, NCH, D)
    load_weight_bf16(P, Wq, c_tk, prm["xa_wq"][layer], NCH, 512)
    load_weight_bf16(P, Wkv, c_tk, prm["xa_wkv"][layer], NCH, D)
    load_weight_bf16(P, Wxo, c_tk, prm["xa_wo"][layer], 4, D)
    ones = C.sb([128, 128], BF16, "ones")
    gqc = C.sb([128, 1], F32, "gqc")
    P.dma("sp", gqc[:], prm["xa_gq"][layer].rearrange("(p o) -> p o", o=1), wr=[c_tk])
    gkb = C.sb([128, 128], F32, "gkb")
    P.dma("sp", gkb[:], prm["xa_gk"][layer].partition_broadcast(128), wr=[c_tk])
    K.consts_ready([c_tk])
    P.op("dve", lambda e: e.memset(ones[:], 1.0), wr=[c_tk])
    P.op("dve", lambda e: e.tensor_scalar(out=gqc[:], in0=gqc[:], scalar1=float(128 ** -0.5), scalar2=None, op0=ALU.mult),
         wr=[c_tk])
    norm_dep(P, [c_tk])
    kT = C.sb([128, 4, 256], BF16, "kT"); vm = C.sb([128, 2, 512], BF16, "vm"); m_tk = Tk()
    ssk = C.sb([128, 4], F32, "ssk"); ssk_tk = Tk()
    kn = C.sb([128, 512], BF16, "kn"); kn_tk = Tk()
    K.load_x(0, S["mem"], 2)
    xt, xt_tk = K.xts[0]
    emit_norm_T(P, C, xt, xt_tk, 2, gbm, K.hn, K.hn_tk, K.ss, K.ss_tk, K.rstd, K.rstd_tk, K.junk, K.junk_tk,
                K.hnT, K.hnT_tks, [banks[0], banks[1]], K.ident)

    def mem_sub(s):
        b, btk = banks[2]
        tm_block(P, b, btk, K.hnT, K.hnT_tks[s], s, Wkv, c_tk, 0, 512)
        for h in range(4):
            P.op("act", lambda e, h=h: e.activation(out=K.junk[:, 0:128], in_=b[:, h * 128:(h + 1) * 128], func=AF.Square,
                                                    accum_out=ssk[:, h:h + 1]), rd=[btk], wr=[K.junk_tk, ssk_tk])
        rsqrt_small(P, ssk, ssk_tk, 4, 1.0 / 128)
        for h in range(4):
            P.op("dve", lambda e, h=h: e.scalar_tensor_tensor(out=kn[:, h * 128:(h + 1) * 128], in0=b[:, h * 128:(h + 1) * 128],
                                                              scalar=ssk[:, h:h + 1], in1=gkb[:], op0=ALU.mult, op1=ALU.mult),
                 rd=[btk, ssk_tk], wr=[kn_tk])
        b3, b3tk = banks[3]
        pb = b3[:].bitcast(BF16)
        for h in range(4):
            P.op("pe", lambda e, h=h: e.transpose(out=pb[:, h * 128:(h + 1) * 128], in_=kn[:, h * 128:(h + 1) * 128],
                                                  identity=K.ident[:]), rd=[kn_tk], wr=[b3tk])
        P.op("dve", lambda e: e.tensor_copy(out=kT[:, :, s * 128:(s + 1) * 128],
                                            in_=pb[:, 0:512].rearrange("p (h m) -> p h m", h=4)), rd=[b3tk], wr=[m_tk])
        b4, b4tk = banks[4]
        tm_block(P, b4, b4tk, K.hnT, K.hnT_tks[s], s, Wkv, c_tk, 512, 512)
        P.op("act", lambda e: e.activation(out=vm[:, s, :], in_=b4[:], func=AF.Copy), rd=[b4tk], wr=[m_tk])

    mem_sub(0)
    mem_sub(1)
    norm_dep(P, [m_tk])
    yT = C.sb([128, NCH, 512], BF16, "yT"); yT_tk = Tk()
    sqs = [(C.sb([128, 512], BF16, "sq"), Tk()) for _ in range(2)]
    rss = [(C.sb([128, 512], F32, "rs"), Tk()) for _ in range(2)]
    qn = C.sb([128, 4, 512], BF16, "qn"); qn_tk = [Tk() for _ in range(4)]
    pm = [(C.sb([128, 512], BF16, "pm"), Tk()) for _ in range(4)]
    rdns = [(C.sb([128, 512], F32, "rdn"), Tk()) for _ in range(2)]
    oTn = C.sb([128, 4, 512], BF16, "oTn"); oTn_tk = Tk()

    def tile_body(i, r0, o0, ntok):
        ns = ntok // 128
        slot = 1 - (i % 2) if False else (i % 2)
        xt, xt_tk = K.xts[slot]
        K.load_x(slot, xsrc[r0:r0 + ntok, :], ns)
        P.dma("sp", yT[:, :, 0:ntok], S["YT"][:, o0:o0 + ntok].rearrange("(c p) t -> p c t", p=128), wr=[yT_tk])
        jj = 0
        for s in range(ns):
            for nh in range(2):
                b, btk = banks[6 + jj % 2]; jj += 1
                for kc in range(NCH):
                    P.op("pe", lambda e, kc=kc, s=s, nh=nh, b=b: e.matmul(b[:, :], lhsT=yT[:, kc, s * 128:(s + 1) * 128],
                                                                         rhs=Wo[:, kc, nh * 512:(nh + 1) * 512],
                                                                         start=(kc == 0), stop=(kc == NCH - 1)),
                         rd=[yT_tk], wr=[btk])
                P.op("dve", lambda e, s=s, nh=nh, b=b: e.tensor_tensor(out=xt[:, s, nh * 512:(nh + 1) * 512],
                                                                      in0=xt[:, s, nh * 512:(nh + 1) * 512], in1=b[:, :], op=ALU.add),
                     rd=[btk], wr=[xt_tk])
        K.norm(slot, ns)
        for hp in range(2):
            srcs = []
            for i2 in range(2):
                b, btk = banks[2 + i2]
                fm_chunk(P, b, btk, Wq, c_tk, (hp * 2 + i2) * 128, 128, K.hnT, K.hnT_tks, ns)
                srcs.append((b, btk))
            outs = [(qn[:, hp * 2 + i2, 0:ntok], qn_tk[hp * 2 + i2]) for i2 in range(2)]
            headnorm_multi(P, srcs, ntok, ones, gqc[:, 0:1], sqs, [banks[4], banks[5]], rss, outs, 1.0 / 128)
        for h in range(4):
            par = h % 2
            for mt in range(2):
                sb_, sb_tk = banks[2 + mt] if par == 0 else banks[mt]
                P.op("pe", lambda e, h=h, mt=mt, sb_=sb_: e.matmul(sb_[:, 0:ntok], lhsT=kT[:, h, mt * 128:(mt + 1) * 128],
                                                                  rhs=qn[:, h, 0:ntok], start=True, stop=True),
                     rd=[qn_tk[h]], wr=[sb_tk])
                p_, p_tk = pm[par * 2 + mt]
                P.op("act", lambda e, sb_=sb_, p_=p_: e.activation(out=p_[:, 0:ntok], in_=sb_[:, 0:ntok], func=AF.Exp),
                     rd=[sb_tk], wr=[p_tk])
            bo, bo_tk = banks[4] if par == 0 else banks[6]
            bd, bd_tk = banks[5] if par == 0 else banks[7]
            for mt in range(2):
                p_, p_tk = pm[par * 2 + mt]
                P.op("pe", lambda e, h=h, mt=mt, p_=p_, bo=bo: e.matmul(bo[:, 0:ntok], lhsT=vm[:, mt, h * 128:(h + 1) * 128],
                                                                       rhs=p_[:, 0:ntok], start=(mt == 0), stop=(mt == 1)),
                     rd=[p_tk], wr=[bo_tk])
            for mt in range(2):
                p_, p_tk = pm[par * 2 + mt]
                P.op("pe", lambda e, mt=mt, p_=p_, bd=bd: e.matmul(bd[:, 0:ntok], lhsT=ones[:], rhs=p_[:, 0:ntok],
                                                                  start=(mt == 0), stop=(mt == 1)), rd=[p_tk], wr=[bd_tk])
            rdn, rdn_tk = rdns[par]
            P.op("dve", lambda e, rdn=rdn, bd=bd: e.reciprocal(out=rdn[:, 0:ntok], in_=bd[:, 0:ntok]), rd=[bd_tk], wr=[rdn_tk])
            P.op("dve", lambda e, h=h, rdn=rdn, bo=bo: e.tensor_tensor(out=oTn[:, h, 0:ntok], in0=bo[:, 0:ntok], in1=rdn[:, 0:ntok],
                                                                      op=ALU.mult), rd=[bo_tk, rdn_tk], wr=[oTn_tk])
        for s in range(ns):
            for nh in range(2):
                b, btk = banks[6 + jj % 2]; jj += 1
                for h in range(4):
                    P.op("pe", lambda e, h=h, s=s, nh=nh, b=b: e.matmul(b[:, :], lhsT=oTn[:, h, s * 128:(s + 1) * 128],
                                                                       rhs=Wxo[:, h, nh * 512:(nh + 1) * 512],
                                                                       start=(h == 0), stop=(h == 3)),
                         rd=[oTn_tk], wr=[btk])
                P.op("dve", lambda e, s=s, nh=nh, b=b: e.tensor_tensor(out=xt[:, s, nh * 512:(nh + 1) * 512],
                                                                      in0=xt[:, s, nh * 512:(nh + 1) * 512], in1=b[:, :], op=ALU.add),
                     rd=[btk], wr=[xt_tk])
        P.dma("sp", S["X1"][o0:o0 + ntok, :].rearrange("(s p) d -> p s d", p=128), xt[:, 0:ns, :], rd=[xt_tk])

    for i, (r0, o0, ntok) in enumerate(tiles):
        tile_body(i, r0, o0, ntok)
    P.barrier()
    P.flush()
    C.close()


def phase_d(nc, P, S, prm, tiles):
    C = Ctx(nc, P)
    K = Kit(nc, P, C, prm["g_mix"][1], S["ident"], nx=2)
    banks = K.banks
    c_tk = Tk()
    W = C.sb([128, NCH, 2048], BF16, "win1")
    load_weight_bf16(P, W, c_tk, prm["o_w_in"][0], NCH, 2048, split=4)
    Wp = C.sb([128, 4, 128], BF16, "wp")
    P.dma("pool", Wp[:], prm["o_w_pool"][0].rearrange("g c d -> c g d"), wr=[c_tk])
    spc = C.sb([128, 4], F32, "spc")
    cw = C.sb([128, 4, 3], F32, "cw")
    for g in range(4):
        P.dma("sp", spc[:, g:g + 1], prm["o_s_pool"][0][g * 128:(g + 1) * 128].rearrange("(p o) -> p o", o=1), wr=[c_tk])
        for k in range(3):
            P.dma("sp", cw[:, g, k:k + 1], prm["o_conv_w"][0][k, g * 128:(g + 1) * 128].rearrange("(p o) -> p o", o=1), wr=[c_tk])
    icf = C.sb([128, 4, 512], F32, "icf")
    P.dma("sp", icf[:].rearrange("p g t -> p (g t)"), S["icnt"].rearrange("g t -> (g t)").partition_broadcast(128), wr=[c_tk])
    hfl = C.sb([128, 1], F32, "hfl")
    P.dma("sp", hfl[:], S["hflag"].partition_broadcast(128), wr=[c_tk])
    K.consts_ready([c_tk])
    L = 16 + 512
    zext = C.sb([128, 4, L], F32, "zext"); z_tk = [Tk() for _ in range(4)]
    bA = C.sb([128, L], F32, "bA"); bB = C.sb([128, L], F32, "bB"); bC = C.sb([128, L], F32, "bC"); s_tk = Tk()
    pT = C.sb([128, 512], BF16, "pT"); pT_tk = Tk()
    tmp = C.sb([128, 512], F32, "tmp"); tmp_tk = Tk()
    xg = C.sb([128, 4, 2 + 512], F32, "xg"); xg_tk = [Tk() for _ in range(4)]
    gcs = C.sb([128, 512], F32, "gcs"); gcs_tk = Tk()
    acc = C.sb([128, 512], F32, "acc"); acc_tk = Tk()
    yD = C.sb([128, 8, 512], BF16, "yD"); yD_tk = Tk()
    for g in range(4):
        P.op("dve", lambda e, g=g: e.memset(zext[:, g, :], 0.0), wr=[z_tk[g]])
        P.op("dve", lambda e, g=g: e.memset(xg[:, g, :], 0.0), wr=[xg_tk[g]])
    WIN = (2, 4, 8, 16)

    def tile_body(i, o0, ntok):
        ns = ntok // 128
        Lt = 16 + ntok
        slot = i % 2
        K.load_x(slot, S["X1"][o0:o0 + ntok, :], ns)
        K.norm(slot, ns)
        hnT, hts = K.hnT, K.hnT_tks
        for g in range(4):
            b, btk = banks[2 + g % 2]
            fm_chunk(P, b, btk, W, c_tk, g * 128, 128, hnT, hts, ns)
            P.op("act", lambda e, g=g, b=b: e.activation(out=zext[:, g, 16:Lt], in_=b[:, 0:ntok], func=AF.Copy),
                 rd=[btk], wr=[z_tk[g]])
            if i > 0:
                z = zext[:, g, :]
                P.op("dve", lambda e, z=z: e.tensor_tensor(out=bA[:, 1:Lt], in0=z[:, 1:Lt], in1=z[:, 0:Lt - 1], op=ALU.add),
                     rd=[z_tk[g]], wr=[s_tk])
                sw = bA
                if g >= 1:
                    P.op("dve", lambda e: e.tensor_tensor(out=bB[:, 3:Lt], in0=bA[:, 3:Lt], in1=bA[:, 1:Lt - 2], op=ALU.add),
                         rd=[s_tk], wr=[s_tk])
                    sw = bB
                if g >= 2:
                    P.op("dve", lambda e: e.tensor_tensor(out=bC[:, 7:Lt], in0=bB[:, 7:Lt], in1=bB[:, 3:Lt - 4], op=ALU.add),
                         rd=[s_tk], wr=[s_tk])
                    sw = bC
                if g >= 3:
                    P.op("dve", lambda e: e.tensor_tensor(out=bA[:, 15:Lt], in0=bC[:, 15:Lt], in1=bC[:, 7:Lt - 8], op=ALU.add),
                         rd=[s_tk], wr=[s_tk])
                    sw = bA
                if i == 1:
                    P.op("dve", lambda e, g=g, sw=sw: e.tensor_tensor(out=tmp[:, 0:ntok], in0=sw[:, 16:Lt], in1=icf[:, g, 0:ntok],
                                                                      op=ALU.mult), rd=[s_tk], wr=[tmp_tk])
                    P.op("dve", lambda e, z=z: e.tensor_tensor(out=pT[:, 0:ntok], in0=tmp[:, 0:ntok], in1=z[:, 16:Lt],
                                                               op=ALU.subtract), rd=[tmp_tk, z_tk[g]], wr=[pT_tk])
                else:
                    P.op("dve", lambda e, g=g, sw=sw, z=z: e.scalar_tensor_tensor(out=pT[:, 0:ntok], in0=sw[:, 16:Lt],
                                                                                  scalar=1.0 / WIN[g], in1=z[:, 16:Lt],
                                                                                  op0=ALU.mult, op1=ALU.subtract),
                         rd=[s_tk, z_tk[g]], wr=[pT_tk])
                b2, b2tk = banks[4 + g % 2]
                P.op("pe", lambda e, g=g, b2=b2: e.matmul(b2[:, 0:ntok], lhsT=Wp[:, g, :], rhs=pT[:, 0:ntok], start=True, stop=True),
                     rd=[pT_tk], wr=[b2tk])
                P.op("dve", lambda e, g=g, b2=b2: e.tensor_scalar(out=yD[:, g, 0:ntok], in0=b2[:, 0:ntok], scalar1=spc[:, g:g + 1],
                                                                  scalar2=None, op0=ALU.mult), rd=[b2tk], wr=[yD_tk])
            if i == 0:
                P.op("dve", lambda e, g=g: e.tensor_scalar(out=zext[:, g, 0:16], in0=zext[:, g, Lt - 16:Lt], scalar1=hfl[:, 0:1],
                                                           scalar2=None, op0=ALU.mult), rd=[z_tk[g]], wr=[z_tk[g]])
            else:
                P.op("dve", lambda e, g=g: e.tensor_copy(out=zext[:, g, 0:16], in_=zext[:, g, Lt - 16:Lt]),
                     rd=[z_tk[g]], wr=[z_tk[g]])
        for c in range(4):
            b, btk = banks[2]
            fm_chunk(P, b, btk, W, c_tk, 1536 + c * 128, 128, hnT, hts, ns)
            P.op("act", lambda e, b=b: e.activation(out=gcs[:, 0:ntok], in_=b[:, 0:ntok], func=AF.Copy), rd=[btk], wr=[gcs_tk])
            b1, b1tk = banks[3]
            fm_chunk(P, b1, b1tk, W, c_tk, 512 + c * 128, 128, hnT, hts, ns)
            P.op("dve", lambda e, c=c, b1=b1: e.tensor_tensor(out=xg[:, c, 2:2 + ntok], in0=gcs[:, 0:ntok], in1=b1[:, 0:ntok],
                                                              op=ALU.mult), rd=[gcs_tk, b1tk], wr=[xg_tk[c]])
            if i > 0:
                P.op("dve", lambda e, c=c: e.tensor_scalar(out=acc[:, 0:ntok], in0=xg[:, c, 2:2 + ntok], scalar1=cw[:, c, 2:3],
                                                           scalar2=None, op0=ALU.mult), rd=[xg_tk[c]], wr=[acc_tk])
                for k in (1, 0):
                    P.op("dve", lambda e, c=c, k=k: e.scalar_tensor_tensor(out=acc[:, 0:ntok], in0=xg[:, c, k:k + ntok],
                                                                           scalar=cw[:, c, k:k + 1], in1=acc[:, 0:ntok],
                                                                           op0=ALU.mult, op1=ALU.add),
                         rd=[xg_tk[c], acc_tk], wr=[acc_tk])
                b2, b2tk = banks[6 + c % 2]
                fm_chunk(P, b2, b2tk, W, c_tk, 1024 + c * 128, 128, hnT, hts, ns)
                P.op("dve", lambda e, c=c, b2=b2: e.tensor_tensor(out=yD[:, 4 + c, 0:ntok], in0=acc[:, 0:ntok], in1=b2[:, 0:ntok],
                                                                  op=ALU.mult), rd=[acc_tk, b2tk], wr=[yD_tk])
            if i == 0:
                P.op("dve", lambda e, c=c: e.tensor_scalar(out=xg[:, c, 0:2], in0=xg[:, c, ntok:ntok + 2], scalar1=hfl[:, 0:1],
                                                           scalar2=None, op0=ALU.mult), rd=[xg_tk[c]], wr=[xg_tk[c]])
            else:
                P.op("dve", lambda e, c=c: e.tensor_copy(out=xg[:, c, 0:2], in_=xg[:, c, ntok:ntok + 2]),
                     rd=[xg_tk[c]], wr=[xg_tk[c]])
        if i > 0:
            P.dma("sp", S["YT"][:, o0:o0 + ntok].rearrange("(c p) t -> p c t", p=128), yD[:, :, 0:ntok], rd=[yD_tk])

    for i, (o0, ntok) in enumerate(tiles):
        tile_body(i, o0, ntok)
    P.barrier()
    P.flush()
    C.close()


PARAMS = ["g_mix", "g_xa", "g_mem", "xa_wq", "xa_wkv", "xa_wo", "xa_gq", "xa_gk", "g_ffn", "w_gate", "w_up", "w_down",
          "e_w_in", "e_b_f", "e_g_v", "e_w_s", "e_b_s", "e_g_qn", "e_g_kn", "e_w_out",
          "o_w_in", "o_w_pool", "o_s_pool", "o_conv_w", "o_w_out"]


def build_program(shapes, debug=False, nphases=7):
    nc = bass.Bass("TRN2", target_bir_lowering=False)
    S = {}
    prm = {}
    ein = lambda name, shape: nc.dram_tensor(name, list(shape), F32, kind="ExternalInput").ap()
    S["xc"] = ein("xc", [SEQ, D])
    S["mem"] = ein("mem", [256, D])
    for k in PARAMS:
        prm[k] = ein(k, shapes[k])
    S["ident"] = ein("ident", [128, 128])
    S["identf"] = S["ident"]
    S["bd64"] = ein("bd64", [128, 128])
    S["tri"] = ein("tri", [128, 128])
    S["cmask"] = ein("cmask", [128, 64])
    S["icnt"] = ein("icnt", [4, 512])
    S["hflag"] = ein("hflag", [1])
    kind = "ExternalOutput" if debug else "Internal"
    scr = lambda name, shape, dt: nc.dram_tensor(name, list(shape), dt, kind=kind).ap()
    S["KT"] = scr("KT", [512, SEQ], BF16)
    S["QT"] = scr("QT", [512, NOWN], BF16)
    S["VV"] = scr("VV", [SEQ, 512], BF16)
    S["NK"] = scr("NK", [128, 64, 8], F32)
    S["NR"] = scr("NR", [8, 64], F32)
    S["YT"] = scr("YT", [D, NOWN], BF16)
    S["X1"] = scr("X1", [NOWN, D], F32)
    out = nc.dram_tensor("out", [HALF, D], F32, kind="ExternalOutput").ap()
    tiles_a = [(512 * i, 512, False) for i in range(7)] + [(3584, 384, False), (OWN0, 128, True)] + \
              [(HALF + 512 * i, 512, True) for i in range(8)]
    S["tiles_a"] = tiles_a
    own = [(0, 128)] + [(128 + 512 * i, 512) for i in range(8)]
    with ExitStack() as es:
        P = Prog(nc, es)
        phase_a(nc, P, S, prm)
        if nphases > 1:
            phase_b(nc, P, S)
        if nphases > 2:
            phase_cx(nc, P, S, prm, 0, prm["e_w_out"][0], S["xc"], [(OWN0 + o, o, n) for (o, n) in own])
        if nphases > 3:
            phase_ffn(nc, P, S["X1"], S["X1"], own, [o for (o, n) in own], prm["g_ffn"][0], prm["w_gate"][0], prm["w_up"][0],
                      prm["w_down"][0], S["ident"])
        if nphases > 4:
            phase_d(nc, P, S, prm, own)
        if nphases > 5:
            phase_cx(nc, P, S, prm, 1, prm["o_w_out"][0], S["X1"], [(o, o, n) for (o, n) in own[1:]])
        if nphases > 6:
            phase_ffn(nc, P, S["X1"], out, own[1:], [o - 128 for (o, n) in own[1:]], prm["g_ffn"][1], prm["w_gate"][1],
                      prm["w_up"][1], prm["w_down"][1], S["ident"])
    return nc


def make_in_maps(inputs):
    x = np.ascontiguousarray(inputs["x"], dtype=np.float32)
    mem = np.ascontiguousarray(inputs["mem"], dtype=np.float32)
    ident = np.eye(128, dtype=np.float32)
    bd = np.zeros((128, 128), np.float32)
    bd[:64, :64] = 1
    bd[64:, 64:] = 1
    tri = np.triu(np.ones((128, 128), np.float32))
    maps = []
    for c in range(8):
        b, h = c // 2, c % 2
        m = {k: np.ascontiguousarray(inputs[k], dtype=np.float32) for k in PARAMS}
        if h == 1:
            m["xc"] = x[b]
            cm = np.zeros((128, 64), np.float32)
            ic = np.tile((1.0 / np.array([2, 4, 8, 16], np.float32))[:, None], (1, 512))
            hf = np.ones((1,), np.float32)
        else:
            m["xc"] = np.concatenate([np.zeros((HALF, D), np.float32), x[b, :HALF]], axis=0)
            cm = np.zeros((128, 64), np.float32)
            cm[:, :32] = NEG
            pos = np.arange(512, dtype=np.float32)
            ic = np.stack([1.0 / np.minimum(pos + 1, w) for w in (2, 4, 8, 16)]).astype(np.float32)
            hf = np.zeros((1,), np.float32)
        m.update(mem=mem[b], ident=ident, bd64=bd, tri=tri, cmask=cm, icnt=ic, hflag=hf)
        maps.append(m)
    return maps


def kernel(**inputs):
    shapes = {k: tuple(np.shape(inputs[k])) for k in PARAMS}
    nc = build_program(shapes)
    maps = make_in_maps(inputs)
    res = run_bass_kernel_spmd(nc, maps, core_ids=list(range(8)))
    out = np.empty((4, SEQ, D), np.float32)
    for c in range(8):
        b, h = c // 2, c % 2
        out[b, h * HALF:(h + 1) * HALF] = res.results[c]["out"]
    return out
```

```python
import numpy as np
from contextlib import ExitStack
import concourse.bass as bass
import concourse.mybir as mybir
from concourse.bass_utils import run_bass_kernel_spmd

F32 = mybir.dt.float32
BF16 = mybir.dt.bfloat16
AF = mybir.ActivationFunctionType
ALU = mybir.AluOpType
AX = mybir.AxisListType

D = 1024
DFF = 2816
NCH = 8
EPS = 1e-6
SEQ = 8192
HALF = 4096
HALO = 128
OWN0 = HALF - HALO
NOWN = HALF + HALO
NEG = -30000.0


class Tk:
    __slots__ = ("w", "r")

    def __init__(self):
        self.w = {}
        self.r = {}


def _merge(d, s):
    for k, v in s.items():
        if d.get(k, 0) < v:
            d[k] = v


class Prog:
    ENG = ("pe", "act", "dve", "pool", "sp")
    NDS = 12

    def __init__(self, nc, es):
        self.nc = nc
        self.sem = {}
        self.cnt = {}
        self.known = {e: {} for e in self.ENG}
        self.streams = {e: [] for e in self.ENG}
        for e in self.ENG:
            self.sem[e] = es.enter_context(nc.semaphore("s_" + e))
            self.cnt[e] = 0
        self.drr = {}
        for q in ("sp", "pool", "act"):
            self.drr[q] = 0
            for k in range(self.NDS):
                key = (q, k)
                self.sem[key] = es.enter_context(nc.semaphore("d_%s%d" % (q, k)))
                self.cnt[key] = 0

    def _waits(self, eng, deps):
        st = self.streams[eng]
        kn = self.known[eng]
        for key, val in deps.items():
            if val <= 0:
                continue
            if eng == "pe" and key == "pe":
                continue
            if kn.get(key, 0) >= val:
                continue
            kn[key] = val
            st.append(("w", key, val))

    def op(self, eng, fn, rd=(), wr=()):
        deps = {}
        for t in rd:
            _merge(deps, t.w)
        for t in wr:
            _merge(deps, t.w)
            _merge(deps, t.r)
        self._waits(eng, deps)
        self.cnt[eng] += 1
        v = self.cnt[eng]
        self.streams[eng].append(("o", fn, eng))
        for t in rd:
            if t.r.get(eng, 0) < v:
                t.r[eng] = v
        for t in wr:
            t.w = {eng: v}
            t.r = {}

    def dma(self, q, out, in_, rd=(), wr=()):
        k = self.drr[q] % self.NDS
        self.drr[q] += 1
        key = (q, k)
        deps = {key: self.cnt[key]}
        for t in rd:
            _merge(deps, t.w)
        for t in wr:
            _merge(deps, t.w)
            _merge(deps, t.r)
        self._waits(q, deps)
        self.cnt[key] += 16
        v = self.cnt[key]
        self.streams[q].append(("d", out, in_, key))
        for t in rd:
            if t.r.get(key, 0) < v:
                t.r[key] = v
        for t in wr:
            t.w = {key: v}
            t.r = {}

    def barrier(self):
        allc = {k: v for k, v in self.cnt.items() if v > 0}
        for e in self.ENG:
            self._waits(e, dict(allc))

    def flush(self):
        nc = self.nc
        streams = self.streams
        self.streams = {e: [] for e in self.ENG}
        sem = self.sem

        def run(e, items):
            for it in items:
                if it[0] == "w":
                    e.wait_ge(sem[it[1]], it[2])
                elif it[0] == "o":
                    it[1](e).then_inc(sem[it[2]], 1)
                else:
                    e.dma_start(out=it[1], in_=it[2]).then_inc(sem[it[3]], 16)

        with nc.allow_non_contiguous_dma(reason="tiny strided parameter/stat DMAs"), nc.Block() as blk:
            @blk.tensor
            def _(e):
                run(e, streams["pe"])

            @blk.scalar
            def _(e):
                run(e, streams["act"])

            @blk.vector
            def _(e):
                run(e, streams["dve"])

            @blk.gpsimd
            def _(e):
                run(e, streams["pool"])

            @blk.sync
            def _(e):
                run(e, streams["sp"])


class Ctx:
    _uid = [0]

    def __init__(self, nc, P):
        self.nc = nc
        self.P = P
        self.es = ExitStack()
        Ctx._uid[0] += 1
        self.n = Ctx._uid[0] * 1000

    def sb(self, shape, dt, name=None):
        self.n += 1
        return self.es.enter_context(self.nc.sbuf_tensor("%s_%d" % (name or "t", self.n), list(shape), dt))

    def psum_banks(self):
        banks = []
        for i in range(8):
            self.n += 1
            t = self.es.enter_context(self.nc.psum_tensor("ps_%d" % self.n, [128, 512], F32))
            banks.append((t, Tk()))
        return banks

    def close(self):
        self.es.close()


def load_weight_bf16(P, dst, dst_tk, w_ap, kchunks, ncols, col0=0, split=2):
    per = (kchunks + split - 1) // split
    for s in range(0, kchunks, per):
        e = min(kchunks, s + per)
        src = w_ap[s * 128:e * 128, col0:col0 + ncols].rearrange("(k p) n -> p k n", p=128)
        P.dma("pool", dst[:, s:e, :], src, wr=[dst_tk])


def emit_norm_T(P, C, xt, xt_tk, nsub, gb, hn, hn_tk, ss, ss_tk, rstd, rstd_tk, junk, junk_tk,
                hnT, hnT_tks, tp_banks, ident, evac_engs=("act", "dve")):
    for s in range(nsub):
        P.op("act", lambda e, s=s: e.activation(out=junk[:], in_=xt[:, s, :], func=AF.Square,
                                                accum_out=ss[:, s:s + 1]),
             rd=[xt_tk], wr=[junk_tk, ss_tk])
    P.op("dve", lambda e: e.tensor_scalar(out=rstd[:, 0:nsub], in0=ss[:, 0:nsub], scalar1=1.0 / D, scalar2=EPS,
                                          op0=ALU.mult, op1=ALU.add), rd=[ss_tk], wr=[rstd_tk])
    P.op("act", lambda e: e.activation(out=rstd[:, 0:nsub], in_=rstd[:, 0:nsub], func=AF.Sqrt),
         rd=[rstd_tk], wr=[rstd_tk])
    P.op("dve", lambda e: e.reciprocal(out=rstd[:, 0:nsub], in_=rstd[:, 0:nsub]),
         rd=[rstd_tk], wr=[rstd_tk])
    for s in range(nsub):
        P.op("dve", lambda e, s=s: e.scalar_tensor_tensor(out=hn[:, s % 2, :], in0=xt[:, s, :], scalar=rstd[:, s:s + 1],
                                                          in1=gb[:], op0=ALU.mult, op1=ALU.mult),
             rd=[xt_tk, rstd_tk], wr=[hn_tk[s % 2]])
        bank, btk = tp_banks[s % len(tp_banks)]
        pb = bank[:].bitcast(BF16)
        for c in range(NCH):
            P.op("pe", lambda e, s=s, c=c, pb=pb: e.transpose(out=pb[:, c * 128:(c + 1) * 128],
                                                             in_=hn[:, s % 2, c * 128:(c + 1) * 128], identity=ident[:]),
                 rd=[hn_tk[s % 2]], wr=[btk])
        eng = evac_engs[s % len(evac_engs)]
        dst = hnT[:, :, s * 128:(s + 1) * 128]
        srcv = pb.rearrange("p (c t) -> p c t", c=NCH)
        if eng == "act":
            P.op("act", lambda e, dst=dst, srcv=srcv: e.activation(out=dst, in_=srcv, func=AF.Copy),
                 rd=[btk], wr=[hnT_tks[s]])
        else:
            P.op(eng, lambda e, dst=dst, srcv=srcv: e.tensor_copy(out=dst, in_=srcv), rd=[btk], wr=[hnT_tks[s]])


def make_ident(P, C, dt=BF16):
    raise NotImplementedError


def phase_ffn(nc, P, xin, xout, tiles_in, tiles_out, g_ap, wg_ap, wu_ap, wd_ap, ident_ap):
    C = Ctx(nc, P)
    NM = DFF // 128
    wg = C.sb([128, NCH, DFF], BF16, "wg"); wg_tk = Tk()
    wu = C.sb([128, NCH, DFF], BF16, "wu"); wu_tk = Tk()
    wd = C.sb([128, NM, D], BF16, "wd"); wd_tk = Tk()
    gb = C.sb([128, D], F32, "gb"); gb_tk = Tk()
    ident = C.sb([128, 128], BF16, "ident"); ident_tk = Tk()
    xts = [(C.sb([128, 4, D], F32, "xt"), Tk()) for _ in range(2)]
    hn = C.sb([128, 2, D], BF16, "hn"); hn_tk = [Tk() for _ in range(2)]
    hnT = C.sb([128, NCH, 512], BF16, "hnT"); hnT_tks = [Tk() for _ in range(4)]
    hT = C.sb([128, NM, 512], BF16, "hT"); hT_tks = [Tk() for _ in range(NM)]
    junk = C.sb([128, D], BF16, "junk"); junk_tk = Tk()
    sgs = [(C.sb([128, 512], BF16, "sg"), Tk()) for _ in range(2)]
    ss = C.sb([128, 8], F32, "ss"); ss_tk = Tk()
    rstd = C.sb([128, 8], F32, "rstd"); rstd_tk = Tk()
    banks = C.psum_banks()

    P.dma("pool", ident[:], ident_ap, wr=[ident_tk])
    P.dma("sp", gb[:], g_ap.partition_broadcast(128), wr=[gb_tk])
    load_weight_bf16(P, wg, wg_tk, wg_ap, NCH, DFF, split=4)
    load_weight_bf16(P, wu, wu_tk, wu_ap, NCH, DFF, split=4)
    load_weight_bf16(P, wd, wd_tk, wd_ap, NM, D, split=4)

    nt = len(tiles_in)

    def load_x(i):
        r0, ntok = tiles_in[i]
        xt, xt_tk = xts[i % 2]
        ns = ntok // 128
        P.dma("sp", xt[:, 0:ns, :], xin[r0:r0 + ntok, :].rearrange("(s p) d -> p s d", p=128), wr=[xt_tk])

    def norm(i):
        r0, ntok = tiles_in[i]
        xt, xt_tk = xts[i % 2]
        ns = ntok // 128
        emit_norm_T(P, C, xt, xt_tk, ns, gb, hn, hn_tk, ss, ss_tk, rstd, rstd_tk, junk, junk_tk,
                    hnT, hnT_tks, banks[4:6], ident)

    def gate_up(i):
        r0, ntok = tiles_in[i]
        ns = ntok // 128
        for mo in range(NM):
            bg, bg_tk = banks[(2 * mo) % 4]
            bu, bu_tk = banks[(2 * mo + 1) % 4]
            for kc in range(NCH):
                P.op("pe", lambda e, kc=kc, mo=mo, bg=bg: e.matmul(bg[:, 0:ntok], lhsT=wg[:, kc, mo * 128:(mo + 1) * 128],
                                                                  rhs=hnT[:, kc, 0:ntok], start=(kc == 0), stop=(kc == NCH - 1)),
                     rd=[wg_tk] + hnT_tks[0:ns], wr=[bg_tk])
            for kc in range(NCH):
                P.op("pe", lambda e, kc=kc, mo=mo, bu=bu: e.matmul(bu[:, 0:ntok], lhsT=wu[:, kc, mo * 128:(mo + 1) * 128],
                                                                  rhs=hnT[:, kc, 0:ntok], start=(kc == 0), stop=(kc == NCH - 1)),
                     rd=[wu_tk] + hnT_tks[0:ns], wr=[bu_tk])
            sg, sg_tk = sgs[mo % 2]
            P.op("act", lambda e, sg=sg, bg=bg: e.activation(out=sg[:, 0:ntok], in_=bg[:, 0:ntok], func=AF.Silu),
                 rd=[bg_tk], wr=[sg_tk])
            P.op("dve", lambda e, sg=sg, bu=bu, mo=mo: e.tensor_tensor(out=hT[:, mo, 0:ntok], in0=sg[:, 0:ntok],
                                                                       in1=bu[:, 0:ntok], op=ALU.mult),
                 rd=[sg_tk, bu_tk], wr=[hT_tks[mo]])

    def down(i):
        r0, ntok = tiles_in[i]
        ro = tiles_out[i]
        xt, xt_tk = xts[i % 2]
        ns = ntok // 128
        j = 0
        for s in range(ns):
            for nh in range(2):
                bo, bo_tk = banks[6 + (j % 2)]
                j += 1
                for mo in range(NM):
                    P.op("pe", lambda e, s=s, nh=nh, mo=mo, bo=bo: e.matmul(bo[:, :], lhsT=hT[:, mo, s * 128:(s + 1) * 128],
                                                                           rhs=wd[:, mo, nh * 512:(nh + 1) * 512],
                                                                           start=(mo == 0), stop=(mo == NM - 1)),
                         rd=[wd_tk, hT_tks[mo]], wr=[bo_tk])
                P.op("dve", lambda e, s=s, nh=nh, bo=bo, xt=xt: e.tensor_tensor(out=xt[:, s, nh * 512:(nh + 1) * 512],
                                                                               in0=xt[:, s, nh * 512:(nh + 1) * 512],
                                                                               in1=bo[:, :], op=ALU.add),
                     rd=[bo_tk], wr=[xt_tk])
        if ro is not None:
            P.dma("sp", xout[ro:ro + ntok, :].rearrange("(s p) d -> p s d", p=128), xt[:, 0:ns, :], rd=[xt_tk])

    load_x(0)
    if nt > 1:
        load_x(1)
    norm_dep(P, [gb_tk, ident_tk])
    norm(0)
    for i in range(nt):
        gate_up(i)
        if i + 1 < nt:
            norm(i + 1)
        down(i)
        if i + 2 < nt:
            load_x(i + 2)
    P.barrier()
    P.flush()
    C.close()


def norm_dep(P, tks):
    deps = {}
    for t in tks:
        _merge(deps, t.w)
    for e in ("pe", "dve", "act", "pool"):
        P._waits(e, dict(deps))


class Kit:
    def __init__(self, nc, P, C, g_ap, ident_ap, nx=2, nsubmax=4):
        self.P = P
        self.C = C
        self.gb = C.sb([128, D], F32, "gb"); self.gb_tk = Tk()
        self.ident = C.sb([128, 128], BF16, "ident"); self.ident_tk = Tk()
        self.xts = [(C.sb([128, nsubmax, D], F32, "xt"), Tk()) for _ in range(nx)]
        self.hn = C.sb([128, 2, D], BF16, "hn"); self.hn_tk = [Tk(), Tk()]
        self.hnTs = [(C.sb([128, NCH, 128 * nsubmax], BF16, "hnT"), [Tk() for _ in range(nsubmax)]) for _ in range(2)]
        self.hnT, self.hnT_tks = self.hnTs[0]
        self.junk = C.sb([128, D], BF16, "junk"); self.junk_tk = Tk()
        self.ss = C.sb([128, 8], F32, "ss"); self.ss_tk = Tk()
        self.rstd = C.sb([128, 8], F32, "rstd"); self.rstd_tk = Tk()
        self.banks = C.psum_banks()
        P.dma("pool", self.ident[:], ident_ap, wr=[self.ident_tk])
        if g_ap is not None:
            P.dma("sp", self.gb[:], g_ap.partition_broadcast(128), wr=[self.gb_tk])

    def consts_ready(self, extra=()):
        norm_dep(self.P, [self.gb_tk, self.ident_tk] + list(extra))

    def load_x(self, slot, src_rows_ap, ns):
        xt, xt_tk = self.xts[slot]
        self.P.dma("sp", xt[:, 0:ns, :], src_rows_ap.rearrange("(s p) d -> p s d", p=128), wr=[xt_tk])

    def norm(self, slot, ns, tp=(0, 1), hset=0):
        xt, xt_tk = self.xts[slot]
        hnT, hnT_tks = self.hnTs[hset]
        emit_norm_T(self.P, self.C, xt, xt_tk, ns, self.gb, self.hn, self.hn_tk, self.ss, self.ss_tk,
                    self.rstd, self.rstd_tk, self.junk, self.junk_tk, hnT, hnT_tks,
                    [self.banks[i] for i in tp], self.ident)


def fm_chunk(P, bank, btk, W, wtk, col0, M, hnT, hnT_tks, ns, nk=NCH):
    ntok = ns * 128
    for kc in range(nk):
        P.op("pe", lambda e, kc=kc: e.matmul(bank[0:M, 0:ntok], lhsT=W[:, kc, col0:col0 + M], rhs=hnT[:, kc, 0:ntok],
                                             start=(kc == 0), stop=(kc == nk - 1)),
             rd=[wtk] + list(hnT_tks[0:ns]), wr=[btk])


def tm_block(P, bank, btk, hnT, hnT_tk_s, s, W, wtk, col0, ncols, nk=NCH):
    for kc in range(nk):
        P.op("pe", lambda e, kc=kc: e.matmul(bank[:, 0:ncols], lhsT=hnT[:, kc, s * 128:(s + 1) * 128],
                                             rhs=W[:, kc, col0:col0 + ncols], start=(kc == 0), stop=(kc == nk - 1)),
             rd=[wtk, hnT_tk_s], wr=[btk])


def rsqrt_small(P, t, tk, n, scale):
    P.op("dve", lambda e: e.tensor_scalar(out=t[:, 0:n], in0=t[:, 0:n], scalar1=scale, scalar2=EPS,
                                          op0=ALU.mult, op1=ALU.add), rd=[tk], wr=[tk])
    P.op("act", lambda e: e.activation(out=t[:, 0:n], in_=t[:, 0:n], func=AF.Sqrt), rd=[tk], wr=[tk])
    P.op("dve", lambda e: e.reciprocal(out=t[:, 0:n], in_=t[:, 0:n]), rd=[tk], wr=[tk])


def headnorm_fm(P, K, src_bank, src_tk, ntok, BD, gcol, sq, sq_tk, ssb, ssb_tk, rs, rs_tk, out_ap, out_tk, inv_n):
    P.op("act", lambda e: e.activation(out=sq[:, 0:ntok], in_=src_bank[:, 0:ntok], func=AF.Square),
         rd=[src_tk], wr=[sq_tk])
    P.op("pe", lambda e: e.matmul(ssb[:, 0:ntok], lhsT=BD[:], rhs=sq[:, 0:ntok], start=True, stop=True),
         rd=[sq_tk], wr=[ssb_tk])
    P.op("dve", lambda e: e.tensor_scalar(out=rs[:, 0:ntok], in0=ssb[:, 0:ntok], scalar1=inv_n, scalar2=EPS,
                                          op0=ALU.mult, op1=ALU.add), rd=[ssb_tk], wr=[rs_tk])
    P.op("act", lambda e: e.activation(out=rs[:, 0:ntok], in_=rs[:, 0:ntok], func=AF.Sqrt), rd=[rs_tk], wr=[rs_tk])
    P.op("dve", lambda e: e.reciprocal(out=rs[:, 0:ntok], in_=rs[:, 0:ntok]), rd=[rs_tk], wr=[rs_tk])
    P.op("dve", lambda e: e.scalar_tensor_tensor(out=out_ap, in0=src_bank[:, 0:ntok], scalar=gcol, in1=rs[:, 0:ntok],
                                                 op0=ALU.mult, op1=ALU.mult), rd=[src_tk, rs_tk], wr=[out_tk])


def headnorm_multi(P, srcs, ntok, BD, gcol, sqs, ssbs, rss, outs, inv_n):
    n = len(srcs)
    for i in range(n):
        (b, btk), (sq, sq_tk) = srcs[i], sqs[i]
        P.op("act", lambda e, b=b, sq=sq: e.activation(out=sq[:, 0:ntok], in_=b[:, 0:ntok], func=AF.Square),
             rd=[btk], wr=[sq_tk])
    for i in range(n):
        (sq, sq_tk), (sb, sb_tk) = sqs[i], ssbs[i]
        P.op("pe", lambda e, sq=sq, sb=sb: e.matmul(sb[:, 0:ntok], lhsT=BD[:], rhs=sq[:, 0:ntok], start=True, stop=True),
             rd=[sq_tk], wr=[sb_tk])
    for i in range(n):
        (sb, sb_tk), (rs, rs_tk) = ssbs[i], rss[i]
        P.op("dve", lambda e, sb=sb, rs=rs: e.tensor_scalar(out=rs[:, 0:ntok], in0=sb[:, 0:ntok], scalar1=inv_n, scalar2=EPS,
                                                            op0=ALU.mult, op1=ALU.add), rd=[sb_tk], wr=[rs_tk])
    for i in range(n):
        rs, rs_tk = rss[i]
        P.op("act", lambda e, rs=rs: e.activation(out=rs[:, 0:ntok], in_=rs[:, 0:ntok], func=AF.Sqrt), rd=[rs_tk], wr=[rs_tk])
    for i in range(n):
        rs, rs_tk = rss[i]
        P.op("dve", lambda e, rs=rs: e.reciprocal(out=rs[:, 0:ntok], in_=rs[:, 0:ntok]), rd=[rs_tk], wr=[rs_tk])
    for i in range(n):
        (b, btk), (rs, rs_tk), (o, otk) = srcs[i], rss[i], outs[i]
        P.op("dve", lambda e, b=b, rs=rs, o=o: e.scalar_tensor_tensor(out=o, in0=b[:, 0:ntok], scalar=gcol, in1=rs[:, 0:ntok],
                                                                      op0=ALU.mult, op1=ALU.mult), rd=[btk, rs_tk], wr=[otk])


def phase_a(nc, P, S, prm):
    xc = S["xc"]
    C = Ctx(nc, P)
    K = Kit(nc, P, C, prm["g_mix"][0], S["ident"])
    NW = 2568
    W = C.sb([128, NCH, NW], BF16, "win"); W_tk = Tk()
    load_weight_bf16(P, W, W_tk, prm["e_w_in"][0], NCH, NW, split=4)
    BD = C.sb([128, 128], BF16, "bd"); c_tk = Tk()
    P.dma("pool", BD[:], S["bd64"], wr=[c_tk])
    id8 = C.sb([8, 8], F32, "id8")
    P.dma("sp", id8[:], S["identf"][0:8, 0:8], wr=[c_tk])
    gq = C.sb([128, 1], F32, "gq"); gk = C.sb([128, 1], F32, "gk")
    for hh in range(2):
        P.dma("sp", gq[hh * 64:(hh + 1) * 64, :], prm["e_g_qn"][0].rearrange("(p o) -> p o", o=1), wr=[c_tk])
        P.dma("sp", gk[hh * 64:(hh + 1) * 64, :], prm["e_g_kn"][0].rearrange("(p o) -> p o", o=1), wr=[c_tk])
    nbf = C.sb([8, 1], F32, "nbf")
    P.dma("sp", nbf[:], prm["e_b_f"][0].rearrange("(p o) -> p o", o=1), wr=[c_tk])
    gvb = C.sb([128, 512], F32, "gvb")
    P.dma("sp", gvb[:], prm["e_g_v"][0].partition_broadcast(128), wr=[c_tk])
    bsb = C.sb([128, 512], F32, "bsb")
    P.dma("sp", bsb[:], prm["e_b_s"][0].rearrange("g t -> (g t)").partition_broadcast(128), wr=[c_tk])
    wsf = C.sb([128, 4, 128], F32, "wsf")
    P.dma("sp", wsf[:], prm["e_w_s"][0].rearrange("g t s -> t g s"), wr=[c_tk])
    wsb = C.sb([128, 4, 128], BF16, "wsb"); wsT = C.sb([128, 4, 128], BF16, "wsT")
    ones8 = C.sb([8, 512], F32, "ones8")
    carry = C.sb([8, 1], F32, "carry"); carry_tk = Tk()
    K.consts_ready([c_tk])
    P.op("dve", lambda e: e.memset(ones8[:], 1.0), wr=[c_tk])
    P.op("dve", lambda e: e.memset(carry[:], 0.0), wr=[carry_tk])
    P.op("dve", lambda e: e.tensor_scalar(out=gq[:], in0=gq[:], scalar1=0.125, scalar2=None, op0=ALU.mult), wr=[c_tk])
    P.op("dve", lambda e: e.tensor_scalar(out=nbf[:], in0=nbf[:], scalar1=-1.0, scalar2=None, op0=ALU.mult), wr=[c_tk])
    P.op("dve", lambda e: e.memset(wsf[0:64, :, 64:128], 0.0), wr=[c_tk])
    P.op("dve", lambda e: e.tensor_copy(out=wsb[:], in_=wsf[:]), wr=[c_tk])
    b0, b0tk = K.banks[0]
    pb = b0[:].bitcast(BF16)
    for g in range(4):
        P.op("pe", lambda e, g=g: e.transpose(out=pb[:, g * 128:(g + 1) * 128], in_=wsb[:, g, :], identity=K.ident[:]),
             rd=[c_tk], wr=[b0tk])
    P.op("dve", lambda e: e.tensor_copy(out=wsT[:].rearrange("p g t -> p (g t)"), in_=pb[:, 0:512]), rd=[b0tk], wr=[c_tk])
    norm_dep(P, [c_tk])

    sqs = [(C.sb([128, 512], BF16, "sq"), Tk()) for _ in range(2)]
    rss = [(C.sb([128, 512], F32, "rs"), Tk()) for _ in range(2)]
    kn = [(C.sb([128, 512], BF16, "kn"), Tk()) for _ in range(4)]
    vtm = [(C.sb([128, 512], BF16, "vtm"), Tk()) for _ in range(2)]
    fe = C.sb([8, 512], F32, "fe"); fe_tk = Tk()
    negc = C.sb([8, 512], F32, "negc"); negc_tk = Tk()
    nk = C.sb([128, 4, 8], F32, "nk"); nk_tk = Tk()
    uT = C.sb([128, 4, 512], BF16, "uT"); uT_tk = [Tk() for _ in range(4)]
    vgs = [(C.sb([128, 512], F32, "vg"), Tk()) for _ in range(2)]
    vns = [(C.sb([128, 512], BF16, "vn"), Tk()) for _ in range(2)]
    ssgs = [(C.sb([128, 4], F32, "ssg"), Tk()) for _ in range(2)]
    t1s = [(C.sb([128, 512], F32, "t1"), Tk()) for _ in range(2)]
    junks = [(C.sb([128, 128], BF16, "junk2"), Tk()) for _ in range(2)]
    yaT = C.sb([128, 4, 512], BF16, "yaT"); ya_tk = Tk()
    banks = K.banks
    rr = [0]

    def fmbank():
        rr[0] += 1
        return banks[2 + rr[0] % 2]

    tiles = S["tiles_a"]
    kt_of = lambda ctx0: ctx0 // 128
    def tile_body(i, c0, ntok, full):
        ns = ntok // 128
        if i == 0:
            K.load_x(0, xc[c0:c0 + ntok, :], ns)
            K.norm(0, ns, hset=0)
        hnT, hts = K.hnTs[i % 2]
        nxt = tiles[i + 1] if i + 1 < len(tiles) else None
        if nxt is not None:
            K.load_x((i + 1) % 2, xc[nxt[0]:nxt[0] + nxt[1], :], nxt[1] // 128)
        def qk_pairs(colbase, gcol, dst, dcol0):
            for cp in range(2):
                srcs = []
                for ci in range(2):
                    c = cp * 2 + ci
                    b, btk = banks[2 + ci]
                    fm_chunk(P, b, btk, W, W_tk, colbase + c * 128, 128, hnT, hts, ns)
                    srcs.append((b, btk))
                outs = [(kn[cp * 2 + ci][0][:, 0:ntok], kn[cp * 2 + ci][1]) for ci in range(2)]
                headnorm_multi(P, srcs, ntok, BD, gcol, sqs, [banks[4], banks[7]], rss, outs, 1.0 / 64)
                for ci in range(2):
                    c = cp * 2 + ci
                    o, otk = kn[c]
                    P.dma("sp", dst[c * 128:(c + 1) * 128, dcol0:dcol0 + ntok], o[:, 0:ntok], rd=[otk])

        qk_pairs(1536, gk[:, 0:1], S["KT"], c0)
        if nxt is not None:
            K.norm((i + 1) % 2, nxt[1] // 128, hset=(i + 1) % 2)
        if full:
            qk_pairs(1024, gq[:, 0:1], S["QT"], c0 - OWN0)
        for s in range(ns):
            b, btk = banks[5] if s % 2 == 0 else banks[4]
            tm_block(P, b, btk, hnT, hts[s], s, W, W_tk, 2048, 512)
            o, otk = vtm[s % 2]
            P.op("act", lambda e, o=o, b=b: e.activation(out=o[:], in_=b[:], func=AF.Copy), rd=[btk], wr=[otk])
            P.dma("sp", S["VV"][c0 + s * 128:c0 + (s + 1) * 128, :], o[:], rd=[otk])
        b, btk = banks[6]
        fm_chunk(P, b, btk, W, W_tk, 2560, 8, hnT, hts, ns)
        P.op("act", lambda e, b=b: e.activation(out=fe[:, 0:ntok], in_=b[0:8, 0:ntok], func=AF.Exp, scale=-1.0,
                                                bias=nbf[:, 0:1]), rd=[btk], wr=[fe_tk])
        P.op("act", lambda e: e.activation(out=fe[:, 0:ntok], in_=fe[:, 0:ntok], func=AF.Ln, bias=1.0),
             rd=[fe_tk], wr=[fe_tk])
        P.op("dve", lambda e: e.tensor_tensor_scan(out=negc[:, 0:ntok], data0=ones8[:, 0:ntok], data1=fe[:, 0:ntok],
                                                   initial=carry[:, 0:1], op0=ALU.mult, op1=ALU.add),
             rd=[fe_tk, carry_tk], wr=[negc_tk])
        P.op("dve", lambda e: e.tensor_copy(out=carry[:], in_=negc[:, ntok - 1:ntok]), rd=[negc_tk], wr=[carry_tk])
        P.dma("sp", S["NR"][:, c0 // 128:c0 // 128 + ns], negc[:, 64:ntok:128], rd=[negc_tk])
        b, btk = banks[6]
        for s in range(ns):
            P.op("pe", lambda e, s=s, b=b: e.matmul(b[:, s * 8:(s + 1) * 8], lhsT=negc[:, s * 128:(s + 1) * 128],
                                                    rhs=id8[:], start=True, stop=True), rd=[negc_tk], wr=[btk])
        P.op("dve", lambda e, b=b: e.tensor_copy(out=nk[:, 0:ns, :].rearrange("p s h -> p (s h)"), in_=b[:, 0:ns * 8]),
             rd=[btk], wr=[nk_tk])
        P.dma("sp", S["NK"][:, kt_of(c0):kt_of(c0) + ns, :], nk[:, 0:ns, :], rd=[nk_tk])
        if not full:
            return
        for c in range(4):
            b, btk = fmbank()
            fm_chunk(P, b, btk, W, W_tk, c * 128, 128, hnT, hts, ns)
            P.op("act", lambda e, c=c, b=b: e.activation(out=uT[:, c, 0:ntok], in_=b[:, 0:ntok], func=AF.Gelu),
                 rd=[btk], wr=[uT_tk[c]])
        for s0 in range(0, ns, 2):
            subs = list(range(s0, min(ns, s0 + 2)))
            tmb = [banks[5], banks[4]]
            spb = [banks[6], banks[7]]
            for i, sx in enumerate(subs):
                tm_block(P, tmb[i][0], tmb[i][1], hnT, hts[sx], sx, W, W_tk, 512, 512)
            for i, sx in enumerate(subs):
                P.op("act", lambda e, i=i: e.activation(out=vgs[i][0][:], in_=tmb[i][0][:], func=AF.Gelu),
                     rd=[tmb[i][1]], wr=[vgs[i][1]])
            for i, sx in enumerate(subs):
                for g in range(4):
                    P.op("act", lambda e, g=g, i=i: e.activation(out=junks[i][0][:], in_=vgs[i][0][:, g * 128:(g + 1) * 128],
                                                                 func=AF.Square, accum_out=ssgs[i][0][:, g:g + 1]),
                         rd=[vgs[i][1]], wr=[junks[i][1], ssgs[i][1]])
            for i, sx in enumerate(subs):
                P.op("dve", lambda e, i=i: e.tensor_scalar(out=ssgs[i][0][:, 0:4], in0=ssgs[i][0][:, 0:4], scalar1=1.0 / 128,
                                                           scalar2=EPS, op0=ALU.mult, op1=ALU.add), rd=[ssgs[i][1]], wr=[ssgs[i][1]])
            for i, sx in enumerate(subs):
                P.op("act", lambda e, i=i: e.activation(out=ssgs[i][0][:, 0:4], in_=ssgs[i][0][:, 0:4], func=AF.Sqrt),
                     rd=[ssgs[i][1]], wr=[ssgs[i][1]])
            for i, sx in enumerate(subs):
                P.op("dve", lambda e, i=i: e.reciprocal(out=ssgs[i][0][:, 0:4], in_=ssgs[i][0][:, 0:4]),
                     rd=[ssgs[i][1]], wr=[ssgs[i][1]])
            for i, sx in enumerate(subs):
                for g in range(4):
                    P.op("dve", lambda e, g=g, i=i: e.scalar_tensor_tensor(out=vns[i][0][:, g * 128:(g + 1) * 128],
                                                                           in0=vgs[i][0][:, g * 128:(g + 1) * 128],
                                                                           scalar=ssgs[i][0][:, g:g + 1],
                                                                           in1=gvb[:, g * 128:(g + 1) * 128], op0=ALU.mult, op1=ALU.mult),
                         rd=[vgs[i][1], ssgs[i][1]], wr=[vns[i][1]])
            for i, sx in enumerate(subs):
                for g in range(4):
                    P.op("pe", lambda e, g=g, i=i: e.matmul(spb[i][0][:, g * 128:(g + 1) * 128], lhsT=vns[i][0][:, g * 128:(g + 1) * 128],
                                                            rhs=wsT[:, g, :], start=True, stop=True), rd=[vns[i][1]], wr=[spb[i][1]])
            for i, sx in enumerate(subs):
                P.op("dve", lambda e, i=i: e.tensor_tensor(out=t1s[i][0][:], in0=spb[i][0][:], in1=bsb[:], op=ALU.add),
                     rd=[spb[i][1]], wr=[t1s[i][1]])
            for i, sx in enumerate(subs):
                P.op("dve", lambda e, i=i, sx=sx: e.tensor_tensor(out=yaT[:, :, sx * 128:(sx + 1) * 128],
                                                                  in0=t1s[i][0][:].rearrange("p (g t) -> p g t", g=4),
                                                                  in1=uT[:, :, sx * 128:(sx + 1) * 128], op=ALU.mult),
                     rd=[t1s[i][1]] + uT_tk, wr=[ya_tk])
        o0 = c0 - OWN0
        P.dma("sp", S["YT"][0:512, o0:o0 + ntok].rearrange("(g p) t -> p g t", p=128), yaT[:, :, 0:ntok], rd=[ya_tk])

    for i, (c0, ntok, full) in enumerate(tiles):
        tile_body(i, c0, ntok, full)
    P.barrier()
    P.flush()
    C.close()


def phase_b(nc, P, S):
    C = Ctx(nc, P)
    banks = C.psum_banks()
    c_tk = Tk()
    tri = C.sb([128, 128], BF16, "tri")
    P.dma("pool", tri[:], S["tri"], wr=[c_tk])
    NKm = C.sb([128, 64, 8], F32, "nkm")
    P.dma("sp", NKm[:], S["NK"], wr=[c_tk])
    cm = C.sb([128, 64], F32, "cm")
    P.dma("sp", cm[:], S["cmask"], wr=[c_tk])
    Rb = C.sb([128, 8, 64], F32, "rb")
    P.dma("sp", Rb[:].rearrange("p h k -> p (h k)"), S["NR"].rearrange("h k -> (h k)").partition_broadcast(128), wr=[c_tk])
    onesf = C.sb([128, 128], F32, "onesf")
    norm_dep(P, [c_tk])
    P.op("dve", lambda e: e.memset(onesf[:], 1.0), wr=[c_tk])
    for h in range(8):
        P.op("dve", lambda e, h=h: e.tensor_tensor(out=NKm[:, :, h], in0=NKm[:, :, h], in1=cm[:], op=ALU.add), wr=[c_tk])
    norm_dep(P, [c_tk])
    KA = [(C.sb([128, SEQ], BF16, "ka"), Tk()) for _ in range(2)]
    QA = [(C.sb([128, NOWN], BF16, "qa"), Tk()) for _ in range(2)]
    VA = [(C.sb([128, 64, 128], BF16, "va"), Tk()) for _ in range(2)]
    nrow = [(C.sb([128, 64], F32, "nrow"), Tk()) for _ in range(2)]
    dif = C.sb([128, 32], F32, "dif"); dif_tk = Tk()
    bias = [(C.sb([128, 64], F32, "bias"), Tk()) for _ in range(2)]
    pbuf = [(C.sb([128, 512], BF16, "pb"), Tk()) for _ in range(4)]
    osb = [(C.sb([128, 512], F32, "osb"), Tk()) for _ in range(2)]
    ybT = [(C.sb([128, 512], BF16, "ybT"), Tk()) for _ in range(2)]
    KT0 = OWN0 // 128
    groups = [(0, 128, KT0)] + [(128 + 512 * J, 512, KT0 + 1 + 4 * J) for J in range(8)]

    def load_head(h):
        ka, ka_tk = KA[h % 2]; qa, qa_tk = QA[h % 2]; va, va_tk = VA[h % 2]; nr, nr_tk = nrow[h % 2]
        for half in range(2):
            P.dma("sp", ka[0:64, half * 4096:(half + 1) * 4096], S["KT"][h * 64:(h + 1) * 64, half * 4096:(half + 1) * 4096],
                  wr=[ka_tk])
        P.op("pool", lambda e: e.memset(ka[64:65, :], 1.0), wr=[ka_tk])
        P.dma("sp", qa[0:64, :], S["QT"][h * 64:(h + 1) * 64, :], wr=[qa_tk])
        P.dma("sp", nr[64:65, :], S["NR"][h:h + 1, :], wr=[nr_tk])
        vc = 0 if h % 2 == 0 else 64
        oc = 64 - vc
        for q4 in range(4):
            P.dma("sp", va[:, q4 * 16:(q4 + 1) * 16, vc:vc + 64],
                  S["VV"][q4 * 2048:(q4 + 1) * 2048, h * 64:(h + 1) * 64].rearrange("(k p) d -> p k d", p=128), wr=[va_tk])
        P.op("pool", lambda e: e.memset(va[:, :, oc:oc + 64], 1.0), wr=[va_tk])
        P.op("dve", lambda e: e.tensor_tensor(out=dif[64:65, 0:32].rearrange("p (a b) -> p a b", b=4),
                                              in0=nr[64:65, KT0 + 3:64:4].unsqueeze(2).to_broadcast([1, 8, 4]),
                                              in1=nr[64:65, KT0 + 1:64].rearrange("p (a b) -> p a b", b=4), op=ALU.subtract),
             rd=[nr_tk], wr=[dif_tk])
        P.op("dve", lambda e: e.memset(qa[64:65, 0:128], 0.0), wr=[qa_tk])
        P.op("dve", lambda e: e.tensor_copy(out=qa[64:65, 128:NOWN].rearrange("p (a b) -> p a b", b=128),
                                            in_=dif[64:65, 0:32].unsqueeze(2).to_broadcast([1, 32, 128])),
             rd=[dif_tk], wr=[qa_tk])

    itc = [0]

    def group_block(h, gi, o0, ntok, kt_first):
        ka, ka_tk = KA[h % 2]; qa, qa_tk = QA[h % 2]; va, va_tk = VA[h % 2]
        nsub = ntok // 128
        nkt = kt_first + nsub
        kt_ref = kt_first + (2 if nsub == 4 else 0)
        it = itc[0]; itc[0] += 1
        bs, bs_tk = bias[it % 2]
        P.op("dve", lambda e: e.tensor_scalar(out=bs[:, 0:nkt], in0=NKm[:, 0:nkt, h], scalar1=Rb[:, h, kt_ref:kt_ref + 1],
                                              scalar2=None, op0=ALU.subtract), wr=[bs_tk])
        acc, acc_tk = banks[4 + it % 2]

        def c0_of(kt):
            r = kt - kt_first
            return 0 if r <= 0 else r * 128

        def S_(kt):
            sb_, sb_tk = banks[kt % 4]
            c0 = c0_of(kt)
            P.op("pe", lambda e: e.matmul(sb_[:, c0:ntok], lhsT=ka[0:65, kt * 128:(kt + 1) * 128],
                                          rhs=qa[0:65, o0 + c0:o0 + ntok], start=True, stop=True),
                 rd=[ka_tk, qa_tk], wr=[sb_tk])
            pb_, pb_tk = pbuf[kt % 4]
            P.op("act", lambda e: e.activation(out=pb_[:, c0:ntok], in_=sb_[:, c0:ntok], func=AF.Exp, bias=bs[:, kt:kt + 1]),
                 rd=[sb_tk, bs_tk], wr=[pb_tk])
            if kt >= kt_first:
                P.op("pool", lambda e: e.tensor_tensor(out=pb_[:, c0:c0 + 128], in0=pb_[:, c0:c0 + 128], in1=tri[:], op=ALU.mult),
                     rd=[pb_tk], wr=[pb_tk])

        def V_(kt):
            pb_, pb_tk = pbuf[kt % 4]
            c0 = c0_of(kt)
            P.op("pe", lambda e: e.matmul(acc[:, c0:ntok], lhsT=va[:, kt, :], rhs=pb_[:, c0:ntok],
                                          start=(kt == 0), stop=(kt == nkt - 1)),
                 rd=[pb_tk, va_tk], wr=[acc_tk])

        S_(0)
        S_(1)
        for kt in range(nkt):
            if kt + 2 < nkt:
                S_(kt + 2)
            V_(kt)
        ob, ob_tk = osb[it % 2]
        drow = 64 if h % 2 == 0 else 0
        nrow0 = 0 if h % 2 == 0 else 64
        P.op("act", lambda e: e.activation(out=ob[:, 0:ntok], in_=acc[:, 0:ntok], func=AF.Copy), rd=[acc_tk], wr=[ob_tk])
        P.op("dve", lambda e: e.tensor_scalar(out=ob[drow:drow + 1, 0:ntok], in0=ob[drow:drow + 1, 0:ntok], scalar1=1e-30,
                                              scalar2=None, op0=ALU.add), rd=[ob_tk], wr=[ob_tk])
        P.op("dve", lambda e: e.reciprocal(out=ob[drow:drow + 1, 0:ntok], in_=ob[drow:drow + 1, 0:ntok]), rd=[ob_tk], wr=[ob_tk])
        bc, bc_tk = banks[6 + it % 2]
        P.op("pe", lambda e: e.matmul(bc[:, 0:ntok], lhsT=onesf[drow:drow + 1, :], rhs=ob[drow:drow + 1, 0:ntok],
                                      start=True, stop=True), rd=[ob_tk], wr=[bc_tk])
        yo, yo_tk = ybT[gi % 2]
        P.op("dve", lambda e: e.tensor_tensor(out=yo[nrow0:nrow0 + 64, 0:ntok], in0=ob[nrow0:nrow0 + 64, 0:ntok],
                                              in1=bc[nrow0:nrow0 + 64, 0:ntok], op=ALU.mult), rd=[ob_tk, bc_tk], wr=[yo_tk])
        P.dma("sp", S["YT"][512 + h * 64:512 + (h + 1) * 64, o0:o0 + ntok], yo[nrow0:nrow0 + 64, 0:ntok], rd=[yo_tk])

    load_head(0)
    for h in range(8):
        if h + 1 < 8:
            load_head(h + 1)
        for gi, (o0, ntok, ktf) in enumerate(groups):
            group_block(h, gi, o0, ntok, ktf)
    P.barrier()
    P.flush()
    C.close()


def phase_cx(nc, P, S, prm, layer, w_out_ap, xsrc, tiles):
    C = Ctx(nc, P)
    K = Kit(nc, P, C, prm["g_xa"][layer], S["ident"])
    banks = K.banks
    c_tk = Tk()
    gbm = C.sb([128, D], F32, "gbm")
    P.dma("sp", gbm[:], prm["g_mem"][layer].partition_broadcast(128), wr=[c_tk])
    Wo = C.sb([128, NCH, D], BF16, "wo"); Wq = C.sb([128, NCH, 512], BF16, "wq")
    Wkv = C.sb([128, NCH, D], BF16, "wkv"); Wxo = C.sb([128, 4, D], BF16, "wxo")
    load_weight_bf16(P, Wo, c_tk, w_out_ap, NCH, D)
    load_weight_bf16(P, Wq, c_tk, prm["xa_wq"][layer], NCH, 512)
    load_weight_bf16(P, Wkv, c_tk, prm["xa_wkv"][layer], NCH, D)
    load_weight_bf16(P, Wxo, c_tk, prm["xa_wo"][layer], 4, D)
    ones = C.sb([128, 128], BF16, "ones")
    gqc = C.sb([128, 1], F32, "gqc")
    P.dma("sp", gqc[:], prm["xa_gq"][layer].rearrange("(p o) -> p o", o=1), wr=[c_tk])
    gkb = C.sb([128, 128], F32, "gkb")
    P.dma("sp", gkb[:], prm["xa_gk"][layer].partition_broadcast(128), wr=[c_tk])
    K.consts_ready([c_tk])
    P.op("dve", lambda e: e.memset(ones[:], 1.0), wr=[c_tk])
    P.op("dve", lambda e: e.tensor_scalar(out=gqc[:], in0=gqc[:], scalar1=float(128 ** -0.5), scalar2=None, op0=ALU.mult),
         wr=[c_tk])
    norm_dep(P, [c_tk])
    kT = C.sb([128, 4, 256], BF16, "kT"); vm = C.sb([128, 2, 512], BF16, "vm"); m_tk = Tk()
    ssk = C.sb([128, 4], F32, "ssk"); ssk_tk = Tk()
    kn = C.sb([128, 512], BF16, "kn"); kn_tk = Tk()
    K.load_x(0, S["mem"], 2)
    xt, xt_tk = K.xts[0]
    emit_norm_T(P, C, xt, xt_tk, 2, gbm, K.hn, K.hn_tk, K.ss, K.ss_tk, K.rstd, K.rstd_tk, K.junk, K.junk_tk,
                K.hnT, K.hnT_tks, [banks[0], banks[1]], K.ident)

    def mem_sub(s):
        b, btk = banks[2]
        tm_block(P, b, btk, K.hnT, K.hnT_tks[s], s, Wkv, c_tk, 0, 512)
        for h in range(4):
            P.op("act", lambda e, h=h: e.activation(out=K.junk[:, 0:128], in_=b[:, h * 128:(h + 1) * 128], func=AF.Square,
                                                    accum_out=ssk[:, h:h + 1]), rd=[btk], wr=[K.junk_tk, ssk_tk])
        rsqrt_small(P, ssk, ssk_tk, 4, 1.0 / 128)
        for h in range(4):
            P.op("dve", lambda e, h=h: e.scalar_tensor_tensor(out=kn[:, h * 128:(h + 1) * 128], in0=b[:, h * 128:(h + 1) * 128],
                                                              scalar=ssk[:, h:h + 1], in1=gkb[:], op0=ALU.mult, op1=ALU.mult),
                 rd=[btk, ssk_tk], wr=[kn_tk])
        b3, b3tk = banks[3]
        pb = b3[:].bitcast(BF16)
        for h in range(4):
            P.op("pe", lambda e, h=h: e.transpose(out=pb[:, h * 128:(h + 1) * 128], in_=kn[:, h * 128:(h + 1) * 128],
                                                  identity=K.ident[:]), rd=[kn_tk], wr=[b3tk])
        P.op("dve", lambda e: e.tensor_copy(out=kT[:, :, s * 128:(s + 1) * 128],
                                            in_=pb[:, 0:512].rearrange("p (h m) -> p h m", h=4)), rd=[b3tk], wr=[m_tk])
        b4, b4tk = banks[4]
        tm_block(P, b4, b4tk, K.hnT, K.hnT_tks[s], s, Wkv, c_tk, 512, 512)
        P.op("act", lambda e: e.activation(out=vm[:, s, :], in_=b4[:], func=AF.Copy), rd=[b4tk], wr=[m_tk])

    mem_sub(0)
    mem_sub(1)
    norm_dep(P, [m_tk])
    yTs = [(C.sb([128, NCH, 512], BF16, "yT"), Tk()) for _ in range(2)]
    sqs = [(C.sb([128, 512], BF16, "sq"), Tk()) for _ in range(2)]
    rss = [(C.sb([128, 512], F32, "rs"), Tk()) for _ in range(2)]
    qn = C.sb([128, 4, 512], BF16, "qn"); qn_tk = [Tk() for _ in range(4)]
    pm = [(C.sb([128, 512], BF16, "pm"), Tk()) for _ in range(4)]
    rdns = [(C.sb([128, 512], F32, "rdn"), Tk()) for _ in range(2)]
    oTn = C.sb([128, 4, 512], BF16, "oTn"); oTn_tk = Tk()

    jjc = [0]

    def pre(i):
        r0, o0, ntok = tiles[i]
        ns = ntok // 128
        slot = i % 2
        xt, xt_tk = K.xts[slot]
        yT, yT_tk = yTs[i % 2]
        K.load_x(slot, xsrc[r0:r0 + ntok, :], ns)
        P.dma("sp", yT[:, :, 0:ntok], S["YT"][:, o0:o0 + ntok].rearrange("(c p) t -> p c t", p=128), wr=[yT_tk])
        for s in range(ns):
            for nh in range(2):
                b, btk = banks[6 + jjc[0] % 2]; jjc[0] += 1
                for kc in range(NCH):
                    P.op("pe", lambda e, kc=kc, s=s, nh=nh, b=b: e.matmul(b[:, :], lhsT=yT[:, kc, s * 128:(s + 1) * 128],
                                                                         rhs=Wo[:, kc, nh * 512:(nh + 1) * 512],
                                                                         start=(kc == 0), stop=(kc == NCH - 1)),
                         rd=[yT_tk], wr=[btk])
                P.op("dve", lambda e, s=s, nh=nh, b=b: e.tensor_tensor(out=xt[:, s, nh * 512:(nh + 1) * 512],
                                                                      in0=xt[:, s, nh * 512:(nh + 1) * 512], in1=b[:, :], op=ALU.add),
                     rd=[btk], wr=[xt_tk])
        K.norm(slot, ns, hset=i % 2)

    def tile_body(i, r0, o0, ntok):
        ns = ntok // 128
        slot = i % 2
        xt, xt_tk = K.xts[slot]
        hnT, hts = K.hnTs[i % 2]
        jj = 0
        if i == 0:
            pre(0)
        for hp in range(2):
            srcs = []
            for i2 in range(2):
                b, btk = banks[2 + i2]
                fm_chunk(P, b, btk, Wq, c_tk, (hp * 2 + i2) * 128, 128, hnT, hts, ns)
                srcs.append((b, btk))
            outs = [(qn[:, hp * 2 + i2, 0:ntok], qn_tk[hp * 2 + i2]) for i2 in range(2)]
            headnorm_multi(P, srcs, ntok, ones, gqc[:, 0:1], sqs, [banks[4], banks[5]], rss, outs, 1.0 / 128)
        if i + 1 < len(tiles):
            pre(i + 1)
        for h in range(4):
            par = h % 2
            for mt in range(2):
                sb_, sb_tk = banks[2 + mt] if par == 0 else banks[mt]
                P.op("pe", lambda e, h=h, mt=mt, sb_=sb_: e.matmul(sb_[:, 0:ntok], lhsT=kT[:, h, mt * 128:(mt + 1) * 128],
                                                                  rhs=qn[:, h, 0:ntok], start=True, stop=True),
                     rd=[qn_tk[h]], wr=[sb_tk])
                p_, p_tk = pm[par * 2 + mt]
                P.op("act", lambda e, sb_=sb_, p_=p_: e.activation(out=p_[:, 0:ntok], in_=sb_[:, 0:ntok], func=AF.Exp),
                     rd=[sb_tk], wr=[p_tk])
            bo, bo_tk = banks[4] if par == 0 else banks[6]
            bd, bd_tk = banks[5] if par == 0 else banks[7]
            for mt in range(2):
                p_, p_tk = pm[par * 2 + mt]
                P.op("pe", lambda e, h=h, mt=mt, p_=p_, bo=bo: e.matmul(bo[:, 0:ntok], lhsT=vm[:, mt, h * 128:(h + 1) * 128],
                                                                       rhs=p_[:, 0:ntok], start=(mt == 0), stop=(mt == 1)),
                     rd=[p_tk], wr=[bo_tk])
            for mt in range(2):
                p_, p_tk = pm[par * 2 + mt]
                P.op("pe", lambda e, mt=mt, p_=p_, bd=bd: e.matmul(bd[:, 0:ntok], lhsT=ones[:], rhs=p_[:, 0:ntok],
                                                                  start=(mt == 0), stop=(mt == 1)), rd=[p_tk], wr=[bd_tk])
            rdn, rdn_tk = rdns[par]
            P.op("dve", lambda e, rdn=rdn, bd=bd: e.reciprocal(out=rdn[:, 0:ntok], in_=bd[:, 0:ntok]), rd=[bd_tk], wr=[rdn_tk])
            P.op("dve", lambda e, h=h, rdn=rdn, bo=bo: e.tensor_tensor(out=oTn[:, h, 0:ntok], in0=bo[:, 0:ntok], in1=rdn[:, 0:ntok],
                                                                      op=ALU.mult), rd=[bo_tk, rdn_tk], wr=[oTn_tk])
        for s in range(ns):
            for nh in range(2):
                b, btk = banks[6 + jj % 2]; jj += 1
                for h in range(4):
                    P.op("pe", lambda e, h=h, s=s, nh=nh, b=b: e.matmul(b[:, :], lhsT=oTn[:, h, s * 128:(s + 1) * 128],
                                                                       rhs=Wxo[:, h, nh * 512:(nh + 1) * 512],
                                                                       start=(h == 0), stop=(h == 3)),
                         rd=[oTn_tk], wr=[btk])
                P.op("dve", lambda e, s=s, nh=nh, b=b: e.tensor_tensor(out=xt[:, s, nh * 512:(nh + 1) * 512],
                                                                      in0=xt[:, s, nh * 512:(nh + 1) * 512], in1=b[:, :], op=ALU.add),
                     rd=[btk], wr=[xt_tk])
        P.dma("sp", S["X1"][o0:o0 + ntok, :].rearrange("(s p) d -> p s d", p=128), xt[:, 0:ns, :], rd=[xt_tk])

    for i, (r0, o0, ntok) in enumerate(tiles):
        tile_body(i, r0, o0, ntok)
    P.barrier()
    P.flush()
    C.close()


def phase_d(nc, P, S, prm, tiles):
    C = Ctx(nc, P)
    K = Kit(nc, P, C, prm["g_mix"][1], S["ident"], nx=2)
    banks = K.banks
    c_tk = Tk()
    W = C.sb([128, NCH, 2048], BF16, "win1")
    load_weight_bf16(P, W, c_tk, prm["o_w_in"][0], NCH, 2048, split=4)
    Wp = C.sb([128, 4, 128], BF16, "wp")
    P.dma("pool", Wp[:], prm["o_w_pool"][0].rearrange("g c d -> c g d"), wr=[c_tk])
    spc = C.sb([128, 4], F32, "spc")
    cw = C.sb([128, 4, 3], F32, "cw")
    for g in range(4):
        P.dma("sp", spc[:, g:g + 1], prm["o_s_pool"][0][g * 128:(g + 1) * 128].rearrange("(p o) -> p o", o=1), wr=[c_tk])
        for k in range(3):
            P.dma("sp", cw[:, g, k:k + 1], prm["o_conv_w"][0][k, g * 128:(g + 1) * 128].rearrange("(p o) -> p o", o=1), wr=[c_tk])
    icf = C.sb([128, 4, 512], F32, "icf")
    P.dma("sp", icf[:].rearrange("p g t -> p (g t)"), S["icnt"].rearrange("g t -> (g t)").partition_broadcast(128), wr=[c_tk])
    hfl = C.sb([128, 1], F32, "hfl")
    P.dma("sp", hfl[:], S["hflag"].partition_broadcast(128), wr=[c_tk])
    K.consts_ready([c_tk])
    L = 16 + 512
    zext = C.sb([128, 4, L], F32, "zext"); z_tk = [Tk() for _ in range(4)]
    bA = C.sb([128, L], F32, "bA"); bB = C.sb([128, L], F32, "bB"); bC = C.sb([128, L], F32, "bC"); s_tk = Tk()
    pT = C.sb([128, 512], BF16, "pT"); pT_tk = Tk()
    tmp = C.sb([128, 512], F32, "tmp"); tmp_tk = Tk()
    xg = C.sb([128, 4, 2 + 512], F32, "xg"); xg_tk = [Tk() for _ in range(4)]
    gcs = C.sb([128, 512], F32, "gcs"); gcs_tk = Tk()
    acc = C.sb([128, 512], F32, "acc"); acc_tk = Tk()
    yD = C.sb([128, 8, 512], BF16, "yD"); yD_tk = Tk()
    for g in range(4):
        P.op("dve", lambda e, g=g: e.memset(zext[:, g, :], 0.0), wr=[z_tk[g]])
        P.op("dve", lambda e, g=g: e.memset(xg[:, g, :], 0.0), wr=[xg_tk[g]])
    WIN = (2, 4, 8, 16)

    def tile_body(i, o0, ntok):
        ns = ntok // 128
        Lt = 16 + ntok
        if i == 0:
            K.load_x(0, S["X1"][o0:o0 + ntok, :], ns)
            K.norm(0, ns, hset=0)
        hnT, hts = K.hnTs[i % 2]
        nxt = tiles[i + 1] if i + 1 < len(tiles) else None
        if nxt is not None:
            K.load_x((i + 1) % 2, S["X1"][nxt[0]:nxt[0] + nxt[1], :], nxt[1] // 128)
        for g in range(4):
            b, btk = banks[2 + g % 2]
            fm_chunk(P, b, btk, W, c_tk, g * 128, 128, hnT, hts, ns)
            P.op("act", lambda e, g=g, b=b: e.activation(out=zext[:, g, 16:Lt], in_=b[:, 0:ntok], func=AF.Copy),
                 rd=[btk], wr=[z_tk[g]])
            if i > 0:
                z = zext[:, g, :]
                P.op("dve", lambda e, z=z: e.tensor_tensor(out=bA[:, 1:Lt], in0=z[:, 1:Lt], in1=z[:, 0:Lt - 1], op=ALU.add),
                     rd=[z_tk[g]], wr=[s_tk])
                sw = bA
                if g >= 1:
                    P.op("dve", lambda e: e.tensor_tensor(out=bB[:, 3:Lt], in0=bA[:, 3:Lt], in1=bA[:, 1:Lt - 2], op=ALU.add),
                         rd=[s_tk], wr=[s_tk])
                    sw = bB
                if g >= 2:
                    P.op("dve", lambda e: e.tensor_tensor(out=bC[:, 7:Lt], in0=bB[:, 7:Lt], in1=bB[:, 3:Lt - 4], op=ALU.add),
                         rd=[s_tk], wr=[s_tk])
                    sw = bC
                if g >= 3:
                    P.op("dve", lambda e: e.tensor_tensor(out=bA[:, 15:Lt], in0=bC[:, 15:Lt], in1=bC[:, 7:Lt - 8], op=ALU.add),
                         rd=[s_tk], wr=[s_tk])
                    sw = bA
                if i == 1:
                    P.op("dve", lambda e, g=g, sw=sw: e.tensor_tensor(out=tmp[:, 0:ntok], in0=sw[:, 16:Lt], in1=icf[:, g, 0:ntok],
                                                                      op=ALU.mult), rd=[s_tk], wr=[tmp_tk])
                    P.op("dve", lambda e, z=z: e.tensor_tensor(out=pT[:, 0:ntok], in0=tmp[:, 0:ntok], in1=z[:, 16:Lt],
                                                               op=ALU.subtract), rd=[tmp_tk, z_tk[g]], wr=[pT_tk])
                else:
                    P.op("dve", lambda e, g=g, sw=sw, z=z: e.scalar_tensor_tensor(out=pT[:, 0:ntok], in0=sw[:, 16:Lt],
                                                                                  scalar=1.0 / WIN[g], in1=z[:, 16:Lt],
                                                                                  op0=ALU.mult, op1=ALU.subtract),
                         rd=[s_tk, z_tk[g]], wr=[pT_tk])
                b2, b2tk = banks[4 + g % 2]
                P.op("pe", lambda e, g=g, b2=b2: e.matmul(b2[:, 0:ntok], lhsT=Wp[:, g, :], rhs=pT[:, 0:ntok], start=True, stop=True),
                     rd=[pT_tk], wr=[b2tk])
                P.op("dve", lambda e, g=g, b2=b2: e.tensor_scalar(out=yD[:, g, 0:ntok], in0=b2[:, 0:ntok], scalar1=spc[:, g:g + 1],
                                                                  scalar2=None, op0=ALU.mult), rd=[b2tk], wr=[yD_tk])
            if i == 0:
                P.op("dve", lambda e, g=g: e.tensor_scalar(out=zext[:, g, 0:16], in0=zext[:, g, Lt - 16:Lt], scalar1=hfl[:, 0:1],
                                                           scalar2=None, op0=ALU.mult), rd=[z_tk[g]], wr=[z_tk[g]])
            else:
                P.op("dve", lambda e, g=g: e.tensor_copy(out=zext[:, g, 0:16], in_=zext[:, g, Lt - 16:Lt]),
                     rd=[z_tk[g]], wr=[z_tk[g]])
        if nxt is not None:
            K.norm((i + 1) % 2, nxt[1] // 128, hset=(i + 1) % 2)
        for c in range(4):
            b, btk = banks[2]
            fm_chunk(P, b, btk, W, c_tk, 1536 + c * 128, 128, hnT, hts, ns)
            P.op("act", lambda e, b=b: e.activation(out=gcs[:, 0:ntok], in_=b[:, 0:ntok], func=AF.Copy), rd=[btk], wr=[gcs_tk])
            b1, b1tk = banks[3]
            fm_chunk(P, b1, b1tk, W, c_tk, 512 + c * 128, 128, hnT, hts, ns)
            P.op("dve", lambda e, c=c, b1=b1: e.tensor_tensor(out=xg[:, c, 2:2 + ntok], in0=gcs[:, 0:ntok], in1=b1[:, 0:ntok],
                                                              op=ALU.mult), rd=[gcs_tk, b1tk], wr=[xg_tk[c]])
            if i > 0:
                P.op("dve", lambda e, c=c: e.tensor_scalar(out=acc[:, 0:ntok], in0=xg[:, c, 2:2 + ntok], scalar1=cw[:, c, 2:3],
                                                           scalar2=None, op0=ALU.mult), rd=[xg_tk[c]], wr=[acc_tk])
                for k in (1, 0):
                    P.op("dve", lambda e, c=c, k=k: e.scalar_tensor_tensor(out=acc[:, 0:ntok], in0=xg[:, c, k:k + ntok],
                                                                           scalar=cw[:, c, k:k + 1], in1=acc[:, 0:ntok],
                                                                           op0=ALU.mult, op1=ALU.add),
                         rd=[xg_tk[c], acc_tk], wr=[acc_tk])
                b2, b2tk = banks[6 + c % 2]
                fm_chunk(P, b2, b2tk, W, c_tk, 1024 + c * 128, 128, hnT, hts, ns)
                P.op("dve", lambda e, c=c, b2=b2: e.tensor_tensor(out=yD[:, 4 + c, 0:ntok], in0=acc[:, 0:ntok], in1=b2[:, 0:ntok],
                                                                  op=ALU.mult), rd=[acc_tk, b2tk], wr=[yD_tk])
            if i == 0:
                P.op("dve", lambda e, c=c: e.tensor_scalar(out=xg[:, c, 0:2], in0=xg[:, c, ntok:ntok + 2], scalar1=hfl[:, 0:1],
                                                           scalar2=None, op0=ALU.mult), rd=[xg_tk[c]], wr=[xg_tk[c]])
            else:
                P.op("dve", lambda e, c=c: e.tensor_copy(out=xg[:, c, 0:2], in_=xg[:, c, ntok:ntok + 2]),
                     rd=[xg_tk[c]], wr=[xg_tk[c]])
        if i > 0:
            P.dma("sp", S["YT"][:, o0:o0 + ntok].rearrange("(c p) t -> p c t", p=128), yD[:, :, 0:ntok], rd=[yD_tk])

    for i, (o0, ntok) in enumerate(tiles):
        tile_body(i, o0, ntok)
    P.barrier()
    P.flush()
    C.close()


PARAMS = ["g_mix", "g_xa", "g_mem", "xa_wq", "xa_wkv", "xa_wo", "xa_gq", "xa_gk", "g_ffn", "w_gate", "w_up", "w_down",
          "e_w_in", "e_b_f", "e_g_v", "e_w_s", "e_b_s", "e_g_qn", "e_g_kn", "e_w_out",
          "o_w_in", "o_w_pool", "o_s_pool", "o_conv_w", "o_w_out"]


def build_program(shapes, debug=False, nphases=7):
    nc = bass.Bass("TRN2", target_bir_lowering=False)
    S = {}
    prm = {}
    ein = lambda name, shape: nc.dram_tensor(name, list(shape), F32, kind="ExternalInput").ap()
    S["xc"] = ein("xc", [SEQ, D])
    S["mem"] = ein("mem", [256, D])
    for k in PARAMS:
        prm[k] = ein(k, shapes[k])
    S["ident"] = ein("ident", [128, 128])
    S["identf"] = S["ident"]
    S["bd64"] = ein("bd64", [128, 128])
    S["tri"] = ein("tri", [128, 128])
    S["cmask"] = ein("cmask", [128, 64])
    S["icnt"] = ein("icnt", [4, 512])
    S["hflag"] = ein("hflag", [1])
    kind = "ExternalOutput" if debug else "Internal"
    scr = lambda name, shape, dt: nc.dram_tensor(name, list(shape), dt, kind=kind).ap()
    S["KT"] = scr("KT", [512, SEQ], BF16)
    S["QT"] = scr("QT", [512, NOWN], BF16)
    S["VV"] = scr("VV", [SEQ, 512], BF16)
    S["NK"] = scr("NK", [128, 64, 8], F32)
    S["NR"] = scr("NR", [8, 64], F32)
    S["YT"] = scr("YT", [D, NOWN], BF16)
    S["X1"] = scr("X1", [NOWN, D], F32)
    out = nc.dram_tensor("out", [HALF, D], F32, kind="ExternalOutput").ap()
    tiles_a = [(512 * i, 512, False) for i in range(7)] + [(3584, 384, False), (OWN0, 128, True)] + \
              [(HALF + 512 * i, 512, True) for i in range(8)]
    S["tiles_a"] = tiles_a
    own = [(0, 128)] + [(128 + 512 * i, 512) for i in range(8)]
    with ExitStack() as es:
        P = Prog(nc, es)
        phase_a(nc, P, S, prm)
        if nphases > 1:
            phase_b(nc, P, S)
        if nphases > 2:
            phase_cx(nc, P, S, prm, 0, prm["e_w_out"][0], S["xc"], [(OWN0 + o, o, n) for (o, n) in own])
        if nphases > 3:
            phase_ffn(nc, P, S["X1"], S["X1"], own, [o for (o, n) in own], prm["g_ffn"][0], prm["w_gate"][0], prm["w_up"][0],
                      prm["w_down"][0], S["ident"])
        if nphases > 4:
            phase_d(nc, P, S, prm, own)
        if nphases > 5:
            phase_cx(nc, P, S, prm, 1, prm["o_w_out"][0], S["X1"], [(o, o, n) for (o, n) in own[1:]])
        if nphases > 6:
            phase_ffn(nc, P, S["X1"], out, own[1:], [o - 128 for (o, n) in own[1:]], prm["g_ffn"][1], prm["w_gate"][1],
                      prm["w_up"][1], prm["w_down"][1], S["ident"])
    return nc


def make_in_maps(inputs):
    x = np.ascontiguousarray(inputs["x"], dtype=np.float32)
    mem = np.ascontiguousarray(inputs["mem"], dtype=np.float32)
    ident = np.eye(128, dtype=np.float32)
    bd = np.zeros((128, 128), np.float32)
    bd[:64, :64] = 1
    bd[64:, 64:] = 1
    tri = np.triu(np.ones((128, 128), np.float32))
    maps = []
    for c in range(8):
        b, h = c // 2, c % 2
        m = {k: np.ascontiguousarray(inputs[k], dtype=np.float32) for k in PARAMS}
        if h == 1:
            m["xc"] = x[b]
            cm = np.zeros((128, 64), np.float32)
            ic = np.tile((1.0 / np.array([2, 4, 8, 16], np.float32))[:, None], (1, 512))
            hf = np.ones((1,), np.float32)
        else:
            m["xc"] = np.concatenate([np.zeros((HALF, D), np.float32), x[b, :HALF]], axis=0)
            cm = np.zeros((128, 64), np.float32)
            cm[:, :32] = NEG
            pos = np.arange(512, dtype=np.float32)
            ic = np.stack([1.0 / np.minimum(pos + 1, w) for w in (2, 4, 8, 16)]).astype(np.float32)
            hf = np.zeros((1,), np.float32)
        m.update(mem=mem[b], ident=ident, bd64=bd, tri=tri, cmask=cm, icnt=ic, hflag=hf)
        maps.append(m)
    return maps


def kernel(**inputs):
    shapes = {k: tuple(np.shape(inputs[k])) for k in PARAMS}
    nc = build_program(shapes)
    maps = make_in_maps(inputs)
    res = run_bass_kernel_spmd(nc, maps, core_ids=list(range(8)))
    out = np.empty((4, SEQ, D), np.float32)
    for c in range(8):
        b, h = c // 2, c % 2
        out[b, h * HALF:(h + 1) * HALF] = res.results[c]["out"]
    return out
```
